# Optimizing a Trainium2 kernel written in Bass

```python
import functools
import jax, jax.numpy as jnp
from jax import lax
import numpy as np

D_MODEL = 1024
BATCH = 32
SEQ = 256
DEPTH = 4
DEC_BATCH = 2
DEC_SEQ = 2048
PAST_LEN = 256

GRID_W = 64
WIN_H = 8
WIN_W = 16
HEAD_DIM = 64
ATTN_WIDTH = D_MODEL // 2
N_HEADS = ATTN_WIDTH // HEAD_DIM
SGU_WIDTH = D_MODEL // 4
SGU_GROUPS = 4
SGU_GROUP_DIM = SGU_WIDTH // SGU_GROUPS
CHUNK = 128
FNET_WIDTH = D_MODEL // 4
FNET_GROUPS = 4
FNET_GROUP_DIM = FNET_WIDTH // FNET_GROUPS
N_BRANCH = 3
D_FF = 2816
N_MOD = 9
Q_BLOCK = 128
IN_SPLITS = [ATTN_WIDTH, 2 * ATTN_WIDTH, 3 * ATTN_WIDTH, 3 * ATTN_WIDTH + SGU_WIDTH,
             3 * ATTN_WIDTH + 2 * SGU_WIDTH, 3 * ATTN_WIDTH + 2 * SGU_WIDTH + FNET_WIDTH]
IN_COLS = IN_SPLITS[-1] + N_BRANCH * D_MODEL
MOD_SCALE = 0.3
RMS_EPS = 1e-6
NEG_INF = -1e30

kernel_name = 'hybrid_natten_sgu_fnet_macaron_step'


def _rmsnorm(x, g):
    xf = x.astype(jnp.float32)
    y = xf * lax.rsqrt(jnp.mean(xf * xf, axis=-1, keepdims=True) + RMS_EPS)
    return (y * g.astype(jnp.float32)).astype(x.dtype)


def _modulate(x, shift, scale):
    return x * (1 + scale) + shift


def _swiglu(x, w_gate, w_up, w_down):
    return (jax.nn.silu(x @ w_gate) * (x @ w_up)) @ w_down


def _context_attention(q, k, v, rpb_l):
    B, L, H, dh = q.shape
    qb = jnp.swapaxes(q.reshape(B, L // Q_BLOCK, Q_BLOCK, H, dh), 0, 1)

    def block(qi):
        s = jnp.einsum('bqhd,blhd->bhql', qi, k).astype(jnp.float32) * (dh ** -0.5)
        p = jax.nn.softmax(s, axis=-1).astype(v.dtype)
        return jnp.einsum('bhql,blhd->bqhd', p, v)

    o = lax.map(block, qb)
    return jnp.swapaxes(o, 0, 1).reshape(B, L, H, dh)


def _natten_latent(q, k, v, rpb_l, kc, vc):
    B, T, H, dh = q.shape
    rows = T // GRID_W
    kh = min(WIN_H, rows)
    qr = q.reshape(B, rows, GRID_W, H, dh)
    kr = k.reshape(B, rows, GRID_W, H, dh)
    vr = v.reshape(B, rows, GRID_W, H, dh)
    cols = jnp.arange(GRID_W)
    col_start = jnp.clip(cols - WIN_W // 2, 0, GRID_W - WIN_W)
    col_ok = (cols[None, :] >= col_start[:, None]) & (cols[None, :] < col_start[:, None] + WIN_W)
    dcol = jnp.clip(cols[None, :] - cols[:, None], -(WIN_W - 1), WIN_W - 1) + (WIN_W - 1)
    scale = dh ** -0.5

    def row_block(r):
        r0 = jnp.clip(r - kh // 2, 0, rows - kh)
        q_blk = lax.dynamic_index_in_dim(qr, r, axis=1, keepdims=False)
        k_blk = lax.dynamic_slice_in_dim(kr, r0, kh, axis=1)
        v_blk = lax.dynamic_slice_in_dim(vr, r0, kh, axis=1)
        drow = r0 + jnp.arange(kh) - r + (WIN_H - 1)
        bias = rpb_l[:, drow[None, :, None], dcol[:, None, :]]
        s_loc = (jnp.einsum('bqhd,bkwhd->bhqkw', q_blk, k_blk).astype(jnp.float32) * scale
                 + bias[None].astype(jnp.float32))
        s_loc = jnp.where(col_ok[:, None, :], s_loc, NEG_INF).reshape(B, H, GRID_W, kh * GRID_W)
        s_ctx = jnp.einsum('bqhd,blhd->bhql', q_blk, kc).astype(jnp.float32) * scale
        p = jax.nn.softmax(jnp.concatenate([s_loc, s_ctx], axis=-1), axis=-1).astype(v.dtype)
        p_loc = p[..., :kh * GRID_W].reshape(B, H, GRID_W, kh, GRID_W)
        p_ctx = p[..., kh * GRID_W:]
        return (jnp.einsum('bhqkw,bkwhd->bqhd', p_loc, v_blk)
                + jnp.einsum('bhql,blhd->bqhd', p_ctx, vc))

    o = lax.map(row_block, jnp.arange(rows))
    return jnp.moveaxis(o, 0, 1).reshape(B, T, H, dh)


def _chunk_sgu(u, v, g, w_s, b_s):
    B, T, _ = v.shape
    vn = _rmsnorm(v, g).reshape(B, T // CHUNK, CHUNK, SGU_GROUPS, SGU_GROUP_DIM)
    mixed = jnp.einsum('gpq,bnqgc->bnpgc', w_s, vn) + jnp.swapaxes(b_s, 0, 1)[:, :, None]
    return u * mixed.reshape(B, T, SGU_WIDTH)


def _fourier_mix(f):
    B, T, _ = f.shape
    fg = f.astype(jnp.float32).reshape(B, T, FNET_GROUPS, FNET_GROUP_DIM)
    out = jnp.fft.fftn(fg, axes=(1, 3), norm='ortho').real
    return out.reshape(B, T, FNET_WIDTH).astype(f.dtype)


def _layer(h, cond, l, W, attend):
    B, T, _ = h.shape
    m = (jax.nn.silu(cond) @ W['w_mod'][l] + W['b_mod'][l]).reshape(cond.shape[0], 1, N_MOD, D_MODEL)
    z = _modulate(_rmsnorm(h, W['g_norm'][l, 0]), m[:, :, 0], m[:, :, 1])
    h = h + 0.5 * m[:, :, 2] * _swiglu(z, W['ffn_w_gate'][l, 0], W['ffn_w_up'][l, 0], W['ffn_w_down'][l, 0])
    z = _modulate(_rmsnorm(h, W['g_norm'][l, 1]), m[:, :, 3], m[:, :, 4])
    proj = z @ W['w_in'][l]
    q, k, v, u_s, v_s, f, gates = jnp.split(proj, IN_SPLITS, axis=-1)
    q = q.reshape(B, T, N_HEADS, HEAD_DIM)
    k = k.reshape(B, T, N_HEADS, HEAD_DIM)
    v = v.reshape(B, T, N_HEADS, HEAD_DIM)
    o_a = attend(q, k, v, W['rpb'][l]).reshape(B, T, ATTN_WIDTH)
    y_a = o_a @ W['p_attn'][l]
    o_b = _chunk_sgu(jax.nn.gelu(u_s), jax.nn.gelu(v_s), W['sgu_norm'][l], W['sgu_w'][l], W['sgu_b'][l])
    y_b = o_b @ W['p_sgu'][l]
    y_c = _fourier_mix(f) @ W['p_fnet'][l]
    g = jax.nn.sigmoid(gates.reshape(B, T, N_BRANCH, D_MODEL) + W['b_gate'][l])
    mixed = (g[:, :, 0] * y_a + g[:, :, 1] * y_b + g[:, :, 2] * y_c) @ W['w_out'][l]
    h = h + m[:, :, 5] * mixed
    z = _modulate(_rmsnorm(h, W['g_norm'][l, 2]), m[:, :, 6], m[:, :, 7])
    h = h + 0.5 * m[:, :, 8] * _swiglu(z, W['ffn_w_gate'][l, 1], W['ffn_w_up'][l, 1], W['ffn_w_down'][l, 1])
    return h, k, v


def setup_inputs(seed: int = 0) -> dict:
    key = jax.random.key(seed)
    ks = jax.random.split(key, 24)

    def nrm(k, shape, s):
        return jax.random.normal(k, shape, jnp.float32) * s

    return {
        'x_prompt': nrm(ks[0], (BATCH, SEQ, D_MODEL), 1.0),
        'x_sample': nrm(ks[1], (DEC_BATCH, DEC_SEQ, D_MODEL), 1.0),
        'cache_k': nrm(ks[2], (DEC_BATCH, DEPTH, PAST_LEN, N_HEADS, HEAD_DIM), 1.0),
        'cache_v': nrm(ks[3], (DEC_BATCH, DEPTH, PAST_LEN, N_HEADS, HEAD_DIM), 1.0),
        'c': nrm(ks[4], (DEC_BATCH, D_MODEL), 1.0),
        'c_ctx': nrm(ks[5], (D_MODEL,), 1.0),
        'w_mod': nrm(ks[6], (DEPTH, D_MODEL, N_MOD * D_MODEL), MOD_SCALE * D_MODEL ** -0.5),
        'b_mod': nrm(ks[7], (DEPTH, N_MOD * D_MODEL), 0.02),
        'g_norm': 1.0 + nrm(ks[8], (DEPTH, 3, D_MODEL), 0.02),
        'ffn_w_gate': nrm(ks[9], (DEPTH, 2, D_MODEL, D_FF), D_MODEL ** -0.5),
        'ffn_w_up': nrm(ks[10], (DEPTH, 2, D_MODEL, D_FF), D_MODEL ** -0.5),
        'ffn_w_down': nrm(ks[11], (DEPTH, 2, D_FF, D_MODEL), D_FF ** -0.5),
        'w_in': nrm(ks[12], (DEPTH, D_MODEL, IN_COLS), D_MODEL ** -0.5),
        'b_gate': nrm(ks[13], (DEPTH, N_BRANCH, D_MODEL), 0.02),
        'rpb': nrm(ks[14], (DEPTH, N_HEADS, 2 * WIN_H - 1, 2 * WIN_W - 1), 0.1),
        'sgu_norm': 1.0 + nrm(ks[15], (DEPTH, SGU_WIDTH), 0.02),
        'sgu_w': nrm(ks[16], (DEPTH, SGU_GROUPS, CHUNK, CHUNK), CHUNK ** -0.5),
        'sgu_b': nrm(ks[17], (DEPTH, SGU_GROUPS, CHUNK), 0.02),
        'p_attn': nrm(ks[18], (DEPTH, ATTN_WIDTH, D_MODEL), ATTN_WIDTH ** -0.5),
        'p_sgu': nrm(ks[19], (DEPTH, SGU_WIDTH, D_MODEL), SGU_WIDTH ** -0.5),
        'p_fnet': nrm(ks[20], (DEPTH, FNET_WIDTH, D_MODEL), FNET_WIDTH ** -0.5),
        'w_out': nrm(ks[21], (DEPTH, D_MODEL, D_MODEL), D_MODEL ** -0.5),
        'g_final': 1.0 + nrm(ks[22], (D_MODEL,), 0.02),
    }


def reference(x_prompt, x_sample, cache_k, cache_v, c, c_ctx, w_mod, b_mod, g_norm,
              ffn_w_gate, ffn_w_up, ffn_w_down, w_in, b_gate, rpb, sgu_norm, sgu_w, sgu_b,
              p_attn, p_sgu, p_fnet, w_out, g_final):
    W = {'w_mod': w_mod, 'b_mod': b_mod, 'g_norm': g_norm, 'ffn_w_gate': ffn_w_gate,
         'ffn_w_up': ffn_w_up, 'ffn_w_down': ffn_w_down, 'w_in': w_in, 'b_gate': b_gate,
         'rpb': rpb, 'sgu_norm': sgu_norm, 'sgu_w': sgu_w, 'sgu_b': sgu_b, 'p_attn': p_attn,
         'p_sgu': p_sgu, 'p_fnet': p_fnet, 'w_out': w_out}

    h = x_prompt
    ctx_cond = c_ctx[None, :]
    new_k, new_v = [], []
    for l in range(DEPTH):
        h, k_l, v_l = _layer(h, ctx_cond, l, W, _context_attention)
        new_k.append(k_l)
        new_v.append(v_l)
    y_prompt = _rmsnorm(h, g_final)
    new_cache_k = jnp.stack(new_k, axis=1)
    new_cache_v = jnp.stack(new_v, axis=1)

    h = x_sample
    for l in range(DEPTH):
        attend = functools.partial(_natten_latent, kc=cache_k[:, l], vc=cache_v[:, l])
        h, _, _ = _layer(h, c, l, W, attend)
    y_sample = _rmsnorm(h, g_final)

    return (y_prompt, y_sample, new_cache_k, new_cache_v)
```

```python
import math
import os
from contextlib import ExitStack

import numpy as np
import ml_dtypes

import concourse.bass as bass
import concourse.mybir as mybir
from concourse.bass_utils import run_bass_kernel_spmd

F32 = mybir.dt.float32
BF16 = mybir.dt.bfloat16
I32 = mybir.dt.int32
AF = mybir.ActivationFunctionType
ALU = mybir.AluOpType

D = 1024
NL = 4
DFF = 2816
NCH_FF = DFF // 128
INC = 5376
NTOK = 1536
NBLK = 3
GRID_W = 64
ROWS = 32
RMS_EPS = 1e-6
NSLOT = 21
AGROWS = 1792
NDS = 12


class Sched:
    def __init__(self, nc, stack):
        self.nc = nc
        self.q = {e: [] for e in ("pe", "act", "dve", "pool", "sp")}
        self.sem = {e: stack.enter_context(nc.semaphore("s_" + e)) for e in ("pe", "act", "dve", "pool")}
        self.cnt = {e: 0 for e in self.sem}
        self.dsem = {qn: [stack.enter_context(nc.semaphore("d_%s%d" % (qn, i))) for i in range(NDS)]
                     for qn in ("sp", "pool")}
        self.dn = {"sp": 0, "pool": 0}
        self.known = {e: {} for e in self.q}
        self.lw = {}
        self.rd = {}
        self.init_tok = {}
        self.out_tokens = []
        self.stack = stack
        self.nops = 0

    @staticmethod
    def flat(keys):
        out = []
        for k in keys:
            if isinstance(k[0], tuple):
                out.extend(Sched.flat(k))
            else:
                out.append(k)
        return tuple(out)

    def _deps(self, eng, reads, writes):
        need = {}

        def add(tok):
            s, v, pe = tok
            if pe == "pe" and eng == "pe":
                return
            k = id(s)
            if k not in need or need[k][1] < v:
                need[k] = (s, v)

        for k in reads:
            t = self.lw.get(k)
            if t is not None:
                add(t)
            elif k[0] in self.init_tok:
                for t2 in self.init_tok[k[0]]:
                    add(t2)
        for k in writes:
            t = self.lw.get(k)
            if t is not None:
                add(t)
            elif k[0] in self.init_tok:
                for t2 in self.init_tok[k[0]]:
                    add(t2)
            for t2 in self.rd.get(k, {}).values():
                add(t2)
        waits = []
        kn = self.known[eng]
        for k, (s, v) in need.items():
            if kn.get(k, 0) < v:
                kn[k] = v
                waits.append((s, v))
        return waits

    def _update(self, tok, reads, writes):
        for k in writes:
            self.lw[k] = tok
            self.rd[k] = {}
        for k in reads:
            self.rd.setdefault(k, {})[id(tok[0])] = tok

    def op(self, eng, fn, reads=(), writes=()):
        reads, writes = self.flat(reads), self.flat(writes)
        if eng != "pe":
            extra = tuple(k for k in reads if k[0] == "ps" and k not in writes)
            writes = writes + extra
        waits = self._deps(eng, reads, writes)
        self.cnt[eng] += 1
        sem = self.sem[eng]
        tok = (sem, self.cnt[eng], eng)

        def emit(h, waits=waits, fn=fn, sem=sem):
            for s, v in waits:
                h.wait_ge(s, v)
            fn(h).then_inc(sem, 1)

        self.q[eng].append(emit)
        self._update(tok, reads, writes)
        self.nops += 1
        return tok

    def dma(self, qn, fn, reads=(), writes=(), is_out=False):
        reads, writes = self.flat(reads), self.flat(writes)
        n = self.dn[qn]
        self.dn[qn] += 1
        i, k = n % NDS, n // NDS
        s = self.dsem[qn][i]
        waits = self._deps(qn, reads, writes)
        if k > 0 and self.known[qn].get(id(s), 0) < 16 * k:
            self.known[qn][id(s)] = 16 * k
            waits.append((s, 16 * k))
        tok = (s, 16 * (k + 1), qn)

        def emit(h, waits=waits, fn=fn, s=s):
            for s2, v in waits:
                h.wait_ge(s2, v)
            fn(h).then_inc(s, 16)

        self.q[qn].append(emit)
        self._update(tok, reads, writes)
        if is_out:
            self.out_tokens.append(tok)
        return tok

    def collective(self, fn, reads, writes):
        reads, writes = self.flat(reads), self.flat(writes)
        s = self.stack.enter_context(self.nc.semaphore("cc%d" % len(self.stack_cc)))
        self.stack_cc.append(s)
        waits = self._deps("pool", reads, writes)
        tok = (s, 1, "pool")

        def emit(h, waits=waits, fn=fn, s=s):
            for s2, v in waits:
                h.wait_ge(s2, v)
            fn(h).then_inc(s)

        self.q["pool"].append(emit)
        self._update(tok, reads, writes)
        return tok

    stack_cc = []

    def raw(self, eng, fn, reads=()):
        waits = self._deps(eng, reads, ())

        def emit(h, waits=waits, fn=fn):
            for s, v in waits:
                h.wait_ge(s, v)
            fn(h)

        self.q[eng].append(emit)

    def tokens_of(self, name):
        toks = []
        for k, t in self.lw.items():
            if k[0] == name:
                toks.append(t)
        for k, d in self.rd.items():
            if k[0] == name:
                toks.extend(d.values())
        best = {}
        for s, v, e in toks:
            if id(s) not in best or best[id(s)][1] < v:
                best[id(s)] = (s, v, e)
        return list(best.values())

    def forget(self, name):
        for dct in (self.lw, self.rd):
            for k in [k for k in dct if k[0] == name]:
                del dct[k]
        self.init_tok.pop(name, None)


class Region:
    def __init__(self, S, tensor, nbytes):
        self.S = S
        self.t = tensor
        self.n = nbytes
        self.live = {}
        self.freed = []
        self.gen = 0

    def alloc(self, base, nbytes, dtype, shape):
        nbytes = (nbytes + 63) // 64 * 64
        spans = sorted(self.live.values())
        lo = 0
        for a, b in spans:
            if a - lo >= nbytes:
                break
            lo = max(lo, b)
        assert lo + nbytes <= self.n, "region overflow allocating %s (%d bytes)" % (base, nbytes)
        hi = lo + nbytes
        self.gen += 1
        name = "%s#%d" % (base, self.gen)
        self.live[name] = (lo, hi)
        toks = []
        keep = []
        for a, b, tk in self.freed:
            if a < hi and lo < b:
                toks.extend(tk)
            keep.append((a, b, tk))
        self.S.init_tok[name] = toks
        ap = self.t[:, lo // 2:hi // 2]
        if dtype == F32:
            ap = ap.bitcast(F32)
        nel = 1
        for s_ in shape:
            nel *= s_
        esz = 4 if dtype == F32 else 2
        ap = ap[:, 0:nel]
        assert nel * esz <= nbytes
        if len(shape) == 2:
            ap = ap.rearrange("p (a b) -> p a b", a=shape[0])
        elif len(shape) == 3:
            ap = ap.rearrange("p (a b c) -> p a b c", a=shape[0], b=shape[1])
        return name, ap

    def free(self, name):
        lo, hi = self.live.pop(name)
        toks = self.S.tokens_of(name)
        self.freed = [(a, b, tk) for (a, b, tk) in self.freed] + [(lo, hi, toks)]
        self.S.forget(name)
        if len(self.freed) > 64:
            self.freed = self.freed[-64:]


class StopBuild(Exception):
    pass


def build_program(nlayers=NL, stop=None):
    nc = bass.Bass("TRN2", target_bir_lowering=False)
    dr = {}
    NLd = nlayers
    DBG = os.environ.get("KDBG", "")

    def din(name, shape, dt=F32):
        dr[name] = nc.dram_tensor(name, list(shape), dt, kind="ExternalInput").ap()

    def dout(name, shape, dt=F32):
        dr[name] = nc.dram_tensor(name, list(shape), dt, kind="ExternalOutput").ap()

    din("xT", [D, NTOK])
    din("vecs", [128, 512])
    din("condT", [128, 16])
    din("w_mod", [NLd, D, 9 * D])
    din("wg", [NLd, 2, D, DFF])
    din("wu", [NLd, 2, D, DFF])
    din("wd", [NLd, 2, DFF, D])
    din("w_in", [NLd, D, INC])
    din("p_all", [NLd, D, D])
    din("w_out", [NLd, D, D])
    din("sgu_wT", [NLd, 128, 4 * 128])
    din("sgu_nb", [128, NL * 256])
    din("sgu_bb", [128, NL * 4 * 128])
    din("fcc", [128, 4 * 128])
    din("ct256", [128, 2 * 2 * 256])
    din("kcT", [NLd, 128, 256])
    din("vc", [NLd, 256, 128])
    din("bsmp", [NLd, 2, 128, NSLOT * 128])
    din("ctm", [2, 2048, 512], BF16)
    din("info", [1, 4], I32)
    dout("yT", [D, NTOK])
    dout("nk", [NL, 1024, 512])
    dout("nv", [NL, 1024, 512])
    aginA = nc.dram_tensor("aginA", [1024, 512], BF16)
    agoutA = nc.dram_tensor("agoutA", [4096, 512], BF16)
    aginB = nc.dram_tensor("aginB", [768, 512], BF16)
    agoutB = nc.dram_tensor("agoutB", [3072, 512], BF16)
    ag2in = nc.dram_tensor("ag2in", [128, 2048], BF16)
    ag2out = nc.dram_tensor("ag2out", [512, 2048], BF16)
    groups = [[0, 1, 2, 3], [4, 5, 6, 7]]

    with ExitStack() as st:
        def sb(name, shape, dt):
            return st.enter_context(nc.sbuf_tensor("sb_" + name, list(shape), dt))

        S = Sched(nc, st)
        S.stack_cc = []
        hT = sb("hT", [128, 8, NTOK], F32)
        zT = sb("zT", [128, 8, NTOK], BF16)
        NST = 8
        wst = [sb("wst%d" % i, [128, 2048], BF16) for i in range(NST)]
        ftp = [sb("ft%d" % i, [128, 512], F32) for i in range(5)]
        sqp = tbp = sgp = ftp
        rs_t = sb("rs", [128, 512], F32)
        rstd_t = sb("rstd", [128, 512], F32)
        vecs = sb("vecs", [128, 512], F32)
        condT = sb("condT", [128, 16], F32)
        csT = sb("csT", [128, 8, 2], BF16)
        MV2 = sb("MV", [128, 2, 2, 72], F32)
        AV2 = sb("AV", [128, 2, 2, 3, 8], F32)
        GV2 = sb("GV", [128, 2, 2, 3, 8], F32)
        acc2 = [sb("sacc%d" % i, [128, 512], F32) for i in range(2)]
        onesf = sb("onesf", [128, 128], F32)
        onesb = sb("onesb", [128, 128], BF16)
        fcc = sb("fcc", [128, 4, 128], BF16)
        ct256 = sb("ct256", [128, 2, 2, 256], BF16)
        sguw = sb("sguw", [128, 1, 4, 128], BF16)
        sgun = sb("sgun", [128, 1, 256], F32)
        sgub = sb("sgub", [128, 4, 128], F32)
        small = sb("small", [128, 8], F32)
        info_t = sb("info", [1, 4], I32)
        RBYTES = 70 * 1024
        Rt = sb("R", [128, RBYTES // 2], BF16)
        R = Region(S, Rt, RBYTES)
        banks = [st.enter_context(nc.psum_tensor("ps%d" % i, [128, 512], F32)) for i in range(8)]

        V_GN, V_BMOD, V_BGATE, V_GFIN = 0, 96, 96 + 288, 96 + 288 + 96

        def gn(l, i, c):
            o = V_GN + (l * 3 + i) * 8 + c
            return vecs[:, o:o + 1]

        def bgate(l, br, dc):
            o = V_BGATE + (l * 3 + br) * 8 + dc
            return vecs[:, o:o + 1]

        rot = {"bank": 0, "wst": 0, "ft": 0, "acc": 0}
        cur = {"par": 0}

        def bank():
            i = rot["bank"]
            rot["bank"] = (i + 1) % 7
            return banks[i], ("ps", i)

        def stage():
            i = rot["wst"]
            rot["wst"] = (i + 1) % NST
            return wst[i], (("wst", i, 0), ("wst", i, 1))

        def rotp(name, pool):
            i = rot["ft"]
            rot["ft"] = (i + 1) % len(pool)
            return pool[i], ("ft", i)

        def tok(blk):
            return slice(blk * 512, (blk + 1) * 512)

        def load_w(src_ap, view, key):
            S.dma("pool", lambda h: h.dma_start(out=view, in_=src_ap), reads=(), writes=(key,))

        def mm(out, lhsT, rhs, start, stop, reads, wkey):
            S.op("pe", lambda h: h.matmul(out, lhsT, rhs, start=start, stop=stop), reads=reads, writes=(wkey,))

        def gelu_from_psum(ps, pkey, out, okey, width, npart=128):
            S.op("act", lambda h: h.activation(out=out, in_=ps, func=AF.Gelu_apprx_tanh), reads=(pkey,), writes=(okey,))

        S.dma("sp", lambda h: h.dma_start(out=info_t[:], in_=dr["info"][:, :]), writes=(("info", 0),))
        S.dma("sp", lambda h: h.dma_start(out=vecs[:], in_=dr["vecs"][:, :]), writes=(("vecs", 0),))
        S.dma("sp", lambda h: h.dma_start(out=condT[:], in_=dr["condT"][:, :]), writes=(("condT", 0),))
        xv = dr["xT"].rearrange("(c p) t -> p c t", p=128)
        for c in range(8):
            S.dma("sp", lambda h, c=c: h.dma_start(out=hT[:, c, :], in_=xv[:, c, :]),
                  writes=tuple(("h", c, b) for b in range(NBLK)))
        S.dma("pool", lambda h: h.dma_start(out=fcc[:].rearrange("p a b -> p (a b)"), in_=dr["fcc"][:, :]),
              writes=(("fcc", 0),))
        S.dma("pool", lambda h: h.dma_start(out=ct256[:].rearrange("p a b c -> p (a b c)"), in_=dr["ct256"][:, :]),
              writes=(("ct256", 0),))
        S.op("dve", lambda h: h.memset(onesf[:], 1.0 / D), writes=(("onesf", 0),))
        S.op("dve", lambda h: h.memset(onesb[:], 1.0), writes=(("onesb", 0),))
        S.op("act", lambda h: h.activation(out=csT[:].rearrange("p c t -> p (c t)"), in_=condT[:], func=AF.Silu),
             reads=(("condT", 0),), writes=(("csT", 0),))

        dyn = {}

        def load_regs(h):
            r0 = dyn["stack"].enter_context(h.register("r_hp"))
            r1 = dyn["stack"].enter_context(h.register("r_j"))
            h.reg_load(r0, info_t[0:1, 0:1])
            h.reg_load(r1, info_t[0:1, 1:2])
            dyn["off_hp"] = h.snap(r0)
            dyn["off_j"] = h.snap(r1)

        S.raw("sp", load_regs, reads=(("info", 0),))

        def mod_gen(l):
            par = l % 2
            MV, AV, GV = MV2[:, par], AV2[:, par], GV2[:, par]
            mps, mk = banks[7], ("ps", 7)
            mv = mps[:, 0:144].rearrange("p (c t) -> p c t", t=2)
            for g in range(36):
                stt, sk = stage()
                v = stt[:, 0:2048].rearrange("p (c n) -> p c n", c=8)
                load_w(dr["w_mod"][l, :, g * 256:(g + 1) * 256].rearrange("(c p) n -> p c n", p=128), v, sk)
                for sc in range(2):
                    col = g * 2 + sc
                    for kc in range(8):
                        mm(mv[:, col, :], v[:, kc, sc * 128:(sc + 1) * 128], csT[:, kc, :], kc == 0, kc == 7,
                           (sk, ("csT", 0)), mk)
                yield
            bm = vecs[:, V_BMOD + l * 72:V_BMOD + (l + 1) * 72]
            for cond in range(2):
                S.op("dve", lambda h, cond=cond: h.tensor_tensor(out=MV[:, cond, :], in0=mv[:, :, cond], in1=bm,
                                                                 op=ALU.add),
                     reads=(mk, ("vecs", 0)), writes=(("MV", par, cond),))
                for i in range(3):
                    gnv = vecs[:, V_GN + (l * 3 + i) * 8:V_GN + (l * 3 + i) * 8 + 8]
                    S.op("dve", lambda h, cond=cond, i=i, gnv=gnv: h.scalar_tensor_tensor(
                        out=AV[:, cond, i, :], in0=MV[:, cond, (3 * i + 1) * 8:(3 * i + 1) * 8 + 8], scalar=1.0,
                        in1=gnv, op0=ALU.add, op1=ALU.mult),
                        reads=(("MV", par, cond), ("vecs", 0)), writes=(("AV", par, cond, i),))
                    S.op("dve", lambda h, cond=cond, i=i: h.tensor_scalar(
                        out=GV[:, cond, i, :], in0=MV[:, cond, (3 * i + 2) * 8:(3 * i + 2) * 8 + 8],
                        scalar1=(1.0 if i == 1 else 0.5), scalar2=None, op0=ALU.mult),
                        reads=(("MV", par, cond),), writes=(("GV", par, cond, i),))

        pend = {}

        def normA(blk):
            ai = rot["acc"]
            rot["acc"] = (ai + 1) % 2
            acc_, ak = acc2[ai], ("sacc", ai)
            prev = None
            for c in range(8):
                sq, sk = rotp("sq", sqp)
                S.op("act", lambda h, c=c, sq=sq: h.activation(out=sq[:], in_=hT[:, c, tok(blk)], func=AF.Square),
                     reads=(("h", c, blk),), writes=(sk,))
                if c == 0:
                    prev = (sq, sk)
                elif c == 1:
                    p_sq, p_k = prev
                    S.op("dve", lambda h, sq=sq, p_sq=p_sq: h.tensor_tensor(out=acc_[:], in0=p_sq[:], in1=sq[:], op=ALU.add),
                         reads=(sk, p_k), writes=(ak,))
                else:
                    S.op("dve", lambda h, sq=sq: h.tensor_tensor(out=acc_[:], in0=acc_[:], in1=sq[:], op=ALU.add),
                         reads=(sk, ak), writes=(ak,))
            pend[blk] = (acc_, ak)

        def normB(blk):
            acc_, ak = pend.pop(blk)
            bs, bk = bank()
            mm(bs[:], onesf[:], acc_[:], True, True, (ak, ("onesf", 0)), bk)
            S.op("act", lambda h: h.activation(out=rs_t[:], in_=bs[:], func=AF.Sqrt, bias=small[:, 0:1], scale=1.0),
                 reads=(bk, ("small", 0)), writes=(("rs", 0),))
            S.op("dve", lambda h: h.reciprocal(out=rstd_t[:], in_=rs_t[:]), reads=(("rs", 0),), writes=(("rstd", 0),))

        def normC(l, i, blk):
            cond = 0 if blk < 2 else 1
            par = l % 2
            MV, AV = MV2[:, par], AV2[:, par]
            for c in range(8):
                tb, tk = rotp("tb", tbp)
                S.op("dve", lambda h, c=c, tb=tb: h.tensor_tensor(out=tb[:], in0=hT[:, c, tok(blk)], in1=rstd_t[:],
                                                                  op=ALU.mult),
                     reads=(("h", c, blk), ("rstd", 0)), writes=(tk,))
                S.op("act", lambda h, c=c, tb=tb: h.activation(
                    out=zT[:, c, tok(blk)], in_=tb[:], func=AF.Identity, scale=AV[:, cond, i, c:c + 1],
                    bias=MV[:, cond, 3 * i * 8 + c:3 * i * 8 + c + 1]),
                    reads=(tk, ("AV", par, cond, i), ("MV", par, cond)), writes=(("z", c, blk),))

        def finalC(blk):
            yv = dr["yT"].rearrange("(c p) t -> p c t", p=128)
            for c in range(8):
                tb, tk = rotp("tb", tbp)
                S.op("dve", lambda h, c=c, tb=tb: h.tensor_tensor(out=tb[:], in0=hT[:, c, tok(blk)], in1=rstd_t[:],
                                                                  op=ALU.mult),
                     reads=(("h", c, blk), ("rstd", 0)), writes=(tk,))
                S.op("act", lambda h, c=c, tb=tb: h.activation(
                    out=hT[:, c, tok(blk)], in_=tb[:], func=AF.Identity,
                    scale=vecs[:, V_GFIN + c:V_GFIN + c + 1]),
                    reads=(tk, ("vecs", 0)), writes=(("h", c, blk),))
                S.dma("sp", lambda h, c=c: h.dma_start(out=yv[:, c, tok(blk)], in_=hT[:, c, tok(blk)]),
                      reads=(("h", c, blk),), writes=(("yT", c, blk),), is_out=True)

        def norm(l, i, blk):
            normA(blk)
            normB(blk)
            normC(l, i, blk)

        def final_norm(blk):
            normA(blk)
            normB(blk)
            finalC(blk)

        class make_post:
            def __init__(self, cfn):
                self.cfn = cfn
                self.st = {}

            def dc(self, blk, c):
                if c == 0:
                    ai = rot["acc"]
                    rot["acc"] = (ai + 1) % 2
                    self.st[blk] = dict(acc=acc2[ai], ak=("sacc", ai), prev=None)
                d = self.st[blk]
                acc_, ak = d["acc"], d["ak"]
                sq, sk = rotp("sq", sqp)
                S.op("act", lambda h: h.activation(out=sq[:], in_=hT[:, c, tok(blk)], func=AF.Square),
                     reads=(("h", c, blk),), writes=(sk,))
                if c == 0:
                    d["prev"] = (sq, sk)
                elif c == 1:
                    p_sq, p_k = d["prev"]
                    S.op("dve", lambda h: h.tensor_tensor(out=acc_[:], in0=p_sq[:], in1=sq[:], op=ALU.add),
                         reads=(sk, p_k), writes=(ak,))
                else:
                    S.op("dve", lambda h: h.tensor_tensor(out=acc_[:], in0=acc_[:], in1=sq[:], op=ALU.add),
                         reads=(sk, ak), writes=(ak,))
                if c == 7:
                    pend[blk] = (acc_, ak)

            def __call__(self, blk):
                if blk >= 1:
                    normB(blk - 1)
                    self.cfn(blk - 1)

        def ffn(l, i, bgen=None, post_blk=None, skip_norm=False):
            ni = 0 if i == 0 else 2
            par = l % 2
            GV = GV2[:, par]

            def tick():
                if bgen is not None:
                    next(bgen, None)
            if not skip_norm:
                for blk in range(NBLK):
                    norm(l, ni, blk)
            hname, hid = R.alloc("hid", 8 * NTOK * 2, BF16, [8, NTOK])
            for (c0, c1) in ((0, 8), (8, 16), (16, 22)):
                for pair in range(c0, c1, 2):
                    sg_, gk = stage()
                    su_, uk = stage()
                    vg = sg_[:, 0:2048].rearrange("p (c n) -> p c n", c=8)
                    vu = su_[:, 0:2048].rearrange("p (c n) -> p c n", c=8)
                    load_w(dr["wg"][l, i, :, pair * 128:pair * 128 + 256].rearrange("(c p) n -> p c n", p=128), vg, gk)
                    load_w(dr["wu"][l, i, :, pair * 128:pair * 128 + 256].rearrange("(c p) n -> p c n", p=128), vu, uk)
                    tick()
                    for cc in range(2):
                        cl = pair + cc - c0
                        for blk in range(NBLK):
                            bg, bgk = bank()
                            for kc in range(8):
                                mm(bg[:], vg[:, kc, cc * 128:(cc + 1) * 128], zT[:, kc, tok(blk)], kc == 0, kc == 7,
                                   (gk, ("z", kc, blk)), bgk)
                            bu, buk = bank()
                            for kc in range(8):
                                mm(bu[:], vu[:, kc, cc * 128:(cc + 1) * 128], zT[:, kc, tok(blk)], kc == 0, kc == 7,
                                   (uk, ("z", kc, blk)), buk)
                            sgt, sgk = rotp("sg", sgp)
                            S.op("act", lambda h, sgt=sgt, bg=bg: h.activation(out=sgt[:], in_=bg[:], func=AF.Silu),
                                 reads=(bgk,), writes=(sgk,))
                            S.op("dve", lambda h, sgt=sgt, bu=bu, cl=cl, blk=blk: h.tensor_tensor(
                                out=hid[:, cl, tok(blk)], in0=sgt[:], in1=bu[:], op=ALU.mult),
                                reads=(sgk, buk), writes=((hname, cl, blk),))
                nch = c1 - c0
                if c1 == NCH_FF:
                    if bgen is not None:
                        for _ in bgen:
                            pass
                    vds = []
                    for dcp in range(4):
                        sd_, dk = stage()
                        vd = sd_[:, 0:nch * 256].rearrange("p (c n) -> p c n", c=nch)
                        load_w(dr["wd"][l, i, c0 * 128:c1 * 128, dcp * 256:(dcp + 1) * 256].rearrange("(c p) n -> p c n", p=128),
                               vd, dk)
                        vds.append((vd, dk))
                    for blk in range(NBLK):
                        cond = 0 if blk < 2 else 1
                        for dc in range(8):
                            vd, dk = vds[dc // 2]
                            cc = dc % 2
                            bd, bdk = bank()
                            for cl in range(nch):
                                mm(bd[:], vd[:, cl, cc * 128:(cc + 1) * 128], hid[:, cl, tok(blk)], cl == 0, cl == nch - 1,
                                   (dk, (hname, cl, blk)), bdk)
                            S.op("dve", lambda h, bd=bd, dc=dc, blk=blk, cond=cond: h.scalar_tensor_tensor(
                                out=hT[:, dc, tok(blk)], in0=bd[:], scalar=GV[:, cond, ni, dc:dc + 1],
                                in1=hT[:, dc, tok(blk)], op0=ALU.mult, op1=ALU.add),
                                reads=(bdk, ("GV", par, cond, ni), ("h", dc, blk)), writes=(("h", dc, blk),))
                            if post_blk is not None:
                                post_blk.dc(blk, dc)
                        if post_blk is not None:
                            post_blk(blk)
                    if post_blk is not None:
                        post_blk(NBLK)
                    continue
                for dc in range(8):
                    sd_, dk = stage()
                    vd = sd_[:, 0:nch * 128].rearrange("p (c n) -> p c n", c=nch)
                    load_w(dr["wd"][l, i, c0 * 128:c1 * 128, dc * 128:(dc + 1) * 128].rearrange("(c p) n -> p c n", p=128),
                           vd, dk)
                    tick()
                    for blk in range(NBLK):
                        cond = 0 if blk < 2 else 1
                        bd, bdk = bank()
                        for cl in range(nch):
                            mm(bd[:], vd[:, cl, :], hid[:, cl, tok(blk)], cl == 0, cl == nch - 1,
                               (dk, (hname, cl, blk)), bdk)
                        S.op("dve", lambda h, bd=bd, dc=dc, blk=blk, cond=cond: h.scalar_tensor_tensor(
                            out=hT[:, dc, tok(blk)], in0=bd[:], scalar=GV[:, cond, ni, dc:dc + 1],
                            in1=hT[:, dc, tok(blk)], op0=ALU.mult, op1=ALU.add),
                            reads=(bdk, ("GV", par, cond, ni), ("h", dc, blk)), writes=(("h", dc, blk),))
            R.free(hname)

        def chk(tag):
            if stop == tag:
                raise StopBuild()

        def mixers(l, post_blk=None):
            par = l % 2
            GV = GV2[:, par]
            oa_n, oaT = R.alloc("oaT", 4 * NTOK * 2, BF16, [4, NTOK])
            ob_n, obT = R.alloc("obT", 2 * NTOK * 2, BF16, [2, NTOK])
            fm_n, fmT = R.alloc("fmT", 2 * NTOK * 2, BF16, [2, NTOK])
            q_n, qT = R.alloc("qT", NTOK * 2, BF16, [1, NTOK])
            k_n, kT = R.alloc("kT", NTOK * 2, BF16, [1, NTOK])
            vb_n, Vb = R.alloc("Vb", 8 * 128 * 2, BF16, [8, 128])
            kv_n, KVst = R.alloc("KVst", 8 * 256 * 4, F32, [8, 256])
            agv_n, agV = R.alloc("agV", 4 * 512 * 2, BF16, [4, 512])
            pT_n, pT = R.alloc("pT", 4 * 1024 * 2, BF16, [4, 1024])
            rd_n, rden = R.alloc("rden", 2 * 256 * 4, F32, [2, 256])
            nkv = dr["nk"][l].rearrange("(c p) f -> p c f", p=128)
            nvv = dr["nv"][l].rearrange("(c p) f -> p c f", p=128)
            aginv = aginA.ap()
            aginvB = aginB.ap()
            win3 = dr["w_in"][l, :, 0:1536].rearrange("(c p) (j n) -> p c j n", p=128, j=3)
            pcount = 0
            f_n, fT = R.alloc("fT", 2 * NTOK * 2, BF16, [2, NTOK])
            aq_n, agQ = R.alloc("agQ", 4 * 512 * 2, BF16, [4, 512])
            ak_n, agK = R.alloc("agK", 4 * 512 * 2, BF16, [4, 512])
            wide = []
            for w4 in range(6):
                s_, k_ = stage()
                v_ = s_[:, 0:2048].rearrange("p (c n) -> p c n", c=8)
                load_w(dr["w_in"][l, :, w4 * 256:(w4 + 1) * 256].rearrange("(c p) n -> p c n", p=128), v_, k_)
                wide.append((v_, k_))
            stf, sfk = stage()
            vf_ = stf[:, 0:2048].rearrange("p (c n) -> p c n", c=8)
            load_w(dr["w_in"][l, :, 2048:2304].rearrange("(c p) n -> p c n", p=128), vf_, sfk)
            for hp in range(4):
                vq_, qk_ = wide[hp // 2]
                vk_, kk_ = wide[2 + hp // 2]
                cs = slice((hp % 2) * 128, (hp % 2 + 1) * 128)
                bq, bqk = bank()
                for kc in range(8):
                    mm(bq[:], vq_[:, kc, cs], zT[:, kc, tok(2)], kc == 0, kc == 7, (qk_, ("z", kc, 2)), bqk)
                S.op("act", lambda h, bq=bq, hp=hp: h.activation(out=agQ[:, hp, :], in_=bq[:], func=AF.Copy, scale=0.125),
                     reads=(bqk,), writes=((aq_n, hp),))
                bk_, bkk = bank()
                for kc in range(8):
                    mm(bk_[:], vk_[:, kc, cs], zT[:, kc, tok(2)], kc == 0, kc == 7, (kk_, ("z", kc, 2)), bkk)
                S.op("dve", lambda h, bk_=bk_, hp=hp: h.tensor_copy(out=agK[:, hp, :], in_=bk_[:]),
                     reads=(bkk,), writes=((ak_n, hp),))
            S.dma("sp", lambda h: h.dma_start(out=aginv[0:512, :].rearrange("(c p) t -> p c t", p=128), in_=agQ[:, :, :]),
                  reads=tuple((aq_n, hp) for hp in range(4)), writes=(("agin", "q"),))
            S.dma("sp", lambda h: h.dma_start(out=aginv[512:1024, :].rearrange("(c p) t -> p c t", p=128), in_=agK[:, :, :]),
                  reads=tuple((ak_n, hp) for hp in range(4)), writes=(("agin", "k"),))
            S.collective(lambda h: h.collective_compute("AllGather", ALU.bypass, replica_groups=groups,
                                                        ins=[aginA.ap().opt()], outs=[agoutA.ap().opt()]),
                         reads=(("agin", "q"), ("agin", "k")), writes=(("agoutA", 0),))
            for t4 in range(4):
                tsl = slice(1024 + t4 * 128, 1024 + (t4 + 1) * 128)
                for half in range(2):
                    vv_, vk2 = wide[4 + half]
                    b_, bkey = bank()
                    for kc in range(8):
                        mm(b_[:, 0:256], zT[:, kc, tsl], vv_[:, kc, :], kc == 0, kc == 7, (vk2, ("z", kc, 2)), bkey)
                    if half == 0:
                        S.op("dve", lambda h, b_=b_, t4=t4: h.tensor_copy(out=agV[:, t4, 0:256], in_=b_[:, 0:256]),
                             reads=(bkey,), writes=((agv_n, t4, 0),))
                    else:
                        S.op("act", lambda h, b_=b_, t4=t4: h.activation(out=agV[:, t4, 256:512], in_=b_[:, 0:256], func=AF.Copy),
                             reads=(bkey,), writes=((agv_n, t4, 1),))
            S.dma("sp", lambda h: h.dma_start(out=aginvB[0:512, :].rearrange("(c p) f -> p c f", p=128), in_=agV[:, :, :]),
                  reads=tuple((agv_n, t, hf) for t in range(4) for hf in range(2)), writes=(("agin", "v"),))
            for cc in range(2):
                bf_, bfk = bank()
                for kc in range(8):
                    mm(bf_[:], vf_[:, kc, cc * 128:(cc + 1) * 128], zT[:, kc, tok(2)], kc == 0, kc == 7,
                       (sfk, ("z", kc, 2)), bfk)
                S.op("act", lambda h, bf_=bf_, cc=cc: h.activation(out=fT[:, cc, tok(2)], in_=bf_[:], func=AF.Copy),
                     reads=(bfk,), writes=((f_n, cc, 2),))
            S.dma("sp", lambda h: h.dma_start(out=aginvB[512:768, :].rearrange("(c p) t -> p c t", p=128), in_=fT[:, :, tok(2)]),
                  reads=((f_n, 0, 2), (f_n, 1, 2)), writes=(("agin", "f"),))
            S.collective(lambda h: h.collective_compute("AllGather", ALU.bypass, replica_groups=groups,
                                                        ins=[aginB.ap().opt()], outs=[agoutB.ap().opt()]),
                         reads=(("agin", "v"), ("agin", "f")), writes=(("agoutB", 0),))
            R.free(aq_n), R.free(ak_n), R.free(agv_n)
            for hp in range(4):
                stt, sk = stage()
                vqk = stt[:, 0:2048].rearrange("p (j c n) -> p c j n", c=8, j=2)
                load_w(win3[:, :, 0, hp * 128:(hp + 1) * 128], vqk[:, :, 0, :], sk[0])
                load_w(win3[:, :, 1, hp * 128:(hp + 1) * 128], vqk[:, :, 1, :], sk[1])
                stv, svk = stage()
                vv = stv[:, 0:1024].rearrange("p (c n) -> p c n", c=8)
                load_w(win3[:, :, 2, hp * 128:(hp + 1) * 128], vv, svk)
                for blk in (0, 1):
                    bq, bqk = bank()
                    for kc in range(8):
                        mm(bq[:], vqk[:, kc, 0, :], zT[:, kc, tok(blk)], kc == 0, kc == 7, (sk, ("z", kc, blk)), bqk)
                    S.op("act", lambda h, bq=bq, blk=blk: h.activation(out=qT[:, 0, tok(blk)], in_=bq[:], func=AF.Copy,
                                                                       scale=0.125),
                         reads=(bqk,), writes=((q_n, blk),))
                    bk_, bkk = bank()
                    for kc in range(8):
                        mm(bk_[:], vqk[:, kc, 1, :], zT[:, kc, tok(blk)], kc == 0, kc == 7, (sk, ("z", kc, blk)), bkk)
                    S.op("dve", lambda h, bk_=bk_, blk=blk: h.tensor_copy(out=kT[:, 0, tok(blk)], in_=bk_[:]),
                         reads=(bkk,), writes=((k_n, blk),))
                if DBG == "A0":
                    continue
                if DBG == "A":
                    continue
                for tch in range(8):
                    tsl = slice(tch * 128, (tch + 1) * 128)
                    blk = tch // 4
                    b_, bkey = bank()
                    if tch < 8:
                        for kc in range(8):
                            mm(b_[:, 0:128], zT[:, kc, tsl], vqk[:, kc, 1, :], kc == 0, kc == 7, (sk, ("z", kc, blk)), bkey)
                    for kc in range(8):
                        mm(b_[:, 128:256], zT[:, kc, tsl], vv[:, kc, :], kc == 0, kc == 7, (svk, ("z", kc, blk)), bkey)
                    if tch < 8:
                        S.op("act", lambda h, b_=b_, tch=tch: h.activation(out=KVst[:, tch, :], in_=b_[:, 0:256],
                                                                           func=AF.Copy),
                             reads=(bkey,), writes=((kv_n, tch),))
                        S.op("dve", lambda h, b_=b_, tch=tch: h.tensor_copy(out=Vb[:, tch, :], in_=b_[:, 128:256]),
                             reads=(bkey,), writes=((vb_n, tch),))
                    else:
                        S.op("dve", lambda h, b_=b_, tch=tch, hp=hp: h.tensor_copy(
                            out=agV[:, tch - 8, hp * 128:(hp + 1) * 128], in_=b_[:, 128:256]),
                            reads=(bkey,), writes=((agv_n, tch - 8, hp),))
                if DBG == "B1":
                    continue
                S.dma("sp", lambda h, hp=hp: h.dma_start(out=nkv[:, :, hp * 128:(hp + 1) * 128], in_=KVst[:, :, 0:128]),
                      reads=tuple((kv_n, t) for t in range(8)), writes=(("nk", l, hp),), is_out=True)
                S.dma("sp", lambda h, hp=hp: h.dma_start(out=nvv[:, :, hp * 128:(hp + 1) * 128], in_=KVst[:, :, 128:256]),
                      reads=tuple((kv_n, t) for t in range(8)), writes=(("nv", l, hp),), is_out=True)
                if DBG == "B":
                    continue
                for b4 in range(4):
                    blk = b4 // 2
                    qs = slice(b4 * 256, (b4 + 1) * 256)
                    for hh in range(2):
                        ps_ = slice(hh * 64, (hh + 1) * 64)
                        bs_, bsk = bank()
                        for kc2 in range(2):
                            ks = slice(b4 * 256 + kc2 * 128, b4 * 256 + (kc2 + 1) * 128)
                            mm(bs_[:, kc2 * 256:(kc2 + 1) * 256], kT[ps_, 0, ks], qT[ps_, 0, qs], True, True,
                               ((k_n, blk), (q_n, blk)), bsk)
                        pi = pcount % 4
                        ri = pcount % 2
                        pcount += 1
                        S.op("act", lambda h, bs_=bs_, pi=pi: h.activation(out=pT[:, pi, 0:512], in_=bs_[:], func=AF.Exp),
                             reads=(bsk,), writes=((pT_n, pi),))
                        bo, bok = bank()
                        for kc2 in range(2):
                            mm(bo[:, 0:256], Vb[:, b4 * 2 + kc2, :], pT[:, pi, kc2 * 256:(kc2 + 1) * 256], kc2 == 0, kc2 == 1,
                               ((vb_n, b4 * 2 + kc2), (pT_n, pi)), bok)
                        for kc2 in range(2):
                            mm(bo[:, 256:512], onesb[:], pT[:, pi, kc2 * 256:(kc2 + 1) * 256], kc2 == 0, kc2 == 1,
                               (("onesb", 0), (pT_n, pi)), bok)
                        S.op("dve", lambda h, bo=bo, ps_=ps_, ri=ri: h.reciprocal(out=rden[ps_, ri, :], in_=bo[ps_, 256:512]),
                             reads=(bok,), writes=((rd_n, ri),))
                        S.op("dve", lambda h, bo=bo, ps_=ps_, hp=hp, qs=qs, ri=ri: h.tensor_tensor(
                            out=oaT[ps_, hp, qs], in0=bo[ps_, 0:256], in1=rden[ps_, ri, :], op=ALU.mult),
                            reads=(bok, (rd_n, ri)), writes=((oa_n, hp, b4, hh),))
            chk("hp")
            R.free(q_n), R.free(k_n), R.free(vb_n), R.free(kv_n)

            u_n, uT = R.alloc("uT", 2 * NTOK * 2, BF16, [2, NTOK])
            vn_n, vn = R.alloc("vn", 12 * 256 * 2, BF16, [12, 256])
            gv_n, gv = R.alloc("gv", 4 * 256 * 4, F32, [4, 256])
            ss_n, ss = R.alloc("ss", 16 * 4, F32, [1, 16])
            ab_n, AB = R.alloc("AB", 2 * 512 * 2, BF16, [2, 512])
            S.dma("sp", lambda h: h.dma_start(out=sgub[:].rearrange("p g q -> p (g q)"),
                                              in_=dr["sgu_bb"][:, l * 512:(l + 1) * 512]), writes=(("sgub", 0),))
            S.dma("sp", lambda h: h.dma_start(out=sgun[:, 0, :], in_=dr["sgu_nb"][:, l * 256:(l + 1) * 256]),
                  writes=(("sgun", 0),))
            S.dma("pool", lambda h: h.dma_start(out=sguw[:, 0].rearrange("p g q -> p (g q)"), in_=dr["sgu_wT"][l, :, :]),
                  writes=(("sguw", 0),))
            stu, suk = stage()
            stvs, svsk = stage()
            stf, sfk = stage()
            vf_ = stf[:, 0:2048].rearrange("p (c n) -> p c n", c=8)
            load_w(dr["w_in"][l, :, 2048:2304].rearrange("(c p) n -> p c n", p=128), vf_, sfk)
            vu_ = stu[:, 0:2048].rearrange("p (c n) -> p c n", c=8)
            vvs = stvs[:, 0:2048].rearrange("p (c n) -> p c n", c=8)
            load_w(dr["w_in"][l, :, 1536:1792].rearrange("(c p) n -> p c n", p=128), vu_, suk)
            load_w(dr["w_in"][l, :, 1792:2048].rearrange("(c p) n -> p c n", p=128), vvs, svsk)
            for blk in (2, 0, 1):
                for cc in range(2):
                    if blk == 2:
                        continue
                    bf_, bfk = bank()
                    for kc in range(8):
                        mm(bf_[:], vf_[:, kc, cc * 128:(cc + 1) * 128], zT[:, kc, tok(blk)], kc == 0, kc == 7,
                           (sfk, ("z", kc, blk)), bfk)
                    S.op("act", lambda h, bf_=bf_, cc=cc, blk=blk: h.activation(out=fT[:, cc, tok(blk)], in_=bf_[:], func=AF.Copy),
                         reads=(bfk,), writes=((f_n, cc, blk),))
                for cc in range(2):
                    bu_, buk = bank()
                    for kc in range(8):
                        mm(bu_[:], vu_[:, kc, cc * 128:(cc + 1) * 128], zT[:, kc, tok(blk)], kc == 0, kc == 7,
                           (suk, ("z", kc, blk)), buk)
                    gelu_from_psum(bu_[:], buk, uT[:, cc, tok(blk)], (u_n, cc, blk), 512)
                for t4 in range(4):
                    t12 = blk * 4 + t4
                    tsl = slice(t12 * 128, (t12 + 1) * 128)
                    bv_, bvk = bank()
                    for kc in range(8):
                        mm(bv_[:, 0:256], zT[:, kc, tsl], vvs[:, kc, :], kc == 0, kc == 7, (svsk, ("z", kc, blk)), bvk)
                    gelu_from_psum(bv_[:, 0:256], bvk, gv[:, t4, :], (gv_n, t4), 256)
                    sq, sqk = rotp("sq", sqp)
                    S.op("dve", lambda h, sq=sq, t12=t12, t4=t4: h.scalar_tensor_tensor(
                        out=sq[:, 0:256], in0=gv[:, t4, :], scalar=1.0, in1=gv[:, t4, :], op0=ALU.mult, op1=ALU.mult,
                        accum_out=ss[:, 0, t12:t12 + 1]),
                        reads=((gv_n, t4),), writes=(sqk, (ss_n, t12)))
                sl4 = slice(blk * 4, blk * 4 + 4)
                k4 = tuple((ss_n, blk * 4 + t4) for t4 in range(4))
                S.op("act", lambda h, sl4=sl4: h.activation(out=ss[:, 0, sl4], in_=ss[:, 0, sl4], func=AF.Sqrt,
                                                            bias=small[:, 0:1], scale=1.0 / 256),
                     reads=k4 + (("small", 0),), writes=k4)
                S.op("dve", lambda h, sl4=sl4: h.reciprocal(out=ss[:, 0, sl4], in_=ss[:, 0, sl4]), reads=k4, writes=k4)
                for t4 in range(4):
                    t12 = blk * 4 + t4
                    S.op("dve", lambda h, t12=t12, t4=t4: h.scalar_tensor_tensor(
                        out=vn[:, t12, :], in0=gv[:, t4, :], scalar=ss[:, 0, t12:t12 + 1], in1=sgun[:, 0, :],
                        op0=ALU.mult, op1=ALU.mult),
                        reads=((gv_n, t4), (ss_n, t12), ("sgun", 0)), writes=((vn_n, t12),))
            for blk in (0, 1):
                for b2 in range(2):
                    for t2 in range(2):
                        tl = slice(blk * 512 + b2 * 256 + t2 * 128, blk * 512 + b2 * 256 + (t2 + 1) * 128)
                        ba, bak = bank()
                        for fh in range(2):
                            mm(ba[:, fh * 128:(fh + 1) * 128], fT[:, fh, tl], fcc[:, 0, :], True, True,
                               ((f_n, fh, blk), ("fcc", 0)), bak)
                            mm(ba[:, 256 + fh * 128:256 + (fh + 1) * 128], fT[:, fh, tl], fcc[:, 1, :], True, True,
                               ((f_n, fh, blk), ("fcc", 0)), bak)
                        S.op("act", lambda h, ba=ba, t2=t2: h.activation(out=AB[:, t2, :], in_=ba[:], func=AF.Copy),
                             reads=(bak,), writes=((ab_n, t2),))
                    bo, bok = bank()
                    for fh in range(2):
                        n_ = 0
                        for t2 in range(2):
                            for cs_ in range(2):
                                mm(bo[:, fh * 256:(fh + 1) * 256], AB[:, t2, cs_ * 256 + fh * 128:cs_ * 256 + (fh + 1) * 128],
                                   ct256[:, cs_, t2, :], n_ == 0, n_ == 3, ((ab_n, t2), ("ct256", 0)), bok)
                                n_ += 1
                    for fh in range(2):
                        S.op("act", lambda h, bo=bo, fh=fh, blk=blk, b2=b2: h.activation(
                            out=fmT[:, fh, blk * 512 + b2 * 256:blk * 512 + (b2 + 1) * 256],
                            in_=bo[:, fh * 256:(fh + 1) * 256], func=AF.Copy),
                            reads=(bok,), writes=((fm_n, fh, blk, b2),))
            for blk in (2, 0, 1):
                for g in range(4):
                    pr, gh = g // 2, g % 2
                    rows = slice(gh * 64, (gh + 1) * 64)
                    bm_, bmk = bank()
                    for t4 in range(4):
                        mm(bm_[:, t4 * 128:(t4 + 1) * 128], vn[:, blk * 4 + t4, pr * 128:(pr + 1) * 128], sguw[:, 0, g, :], True, True,
                           ((vn_n, blk * 4 + t4), ("sguw", 0)), bmk)
                    sgt, sgk = rotp("sg", sgp)
                    for t4 in range(4):
                        S.op("dve", lambda h, bm_=bm_, sgt=sgt, t4=t4, g=g, rows=rows: h.tensor_tensor(
                            out=sgt[rows, t4 * 128:(t4 + 1) * 128], in0=bm_[rows, t4 * 128:(t4 + 1) * 128],
                            in1=sgub[rows, g, :], op=ALU.add),
                            reads=(bmk, ("sgub", 0)), writes=(sgk,))
                    S.op("dve", lambda h, sgt=sgt, rows=rows, pr=pr, blk=blk: h.tensor_tensor(
                        out=obT[rows, pr, tok(blk)], in0=sgt[rows, :], in1=uT[rows, pr, tok(blk)], op=ALU.mult),
                        reads=(sgk, (u_n, pr, blk)), writes=((ob_n, pr, blk, gh),))
            R.free(u_n), R.free(f_n), R.free(vn_n), R.free(gv_n), R.free(ss_n), R.free(ab_n)

            chk("sgu")
            qs_n, qTs = R.alloc("qTs", 2048 * 2, BF16, [1, 2048])
            ks_n, kTs = R.alloc("kTs", 2048 * 2, BF16, [1, 2048])
            vs_n, Vs = R.alloc("Vs", 16 * 128 * 2, BF16, [16, 128])
            kc_n, kcT = R.alloc("kcT", 256 * 2, BF16, [1, 256])
            vc_n, Vc = R.alloc("Vc", 2 * 128 * 2, BF16, [2, 128])
            bi_n, bias = R.alloc("bias", NSLOT * 128 * 4, F32, [NSLOT, 128])
            om_n, oTm = R.alloc("oTm", 2048 * 2, BF16, [1, 2048])
            sb_n, sbuf_s = R.alloc("sbs", 2 * 640 * 4, F32, [2, 640])
            ago3 = agoutA.ap().rearrange("(r x) t -> r x t", r=4)
            agoB3 = agoutB.ap().rearrange("(r x) t -> r x t", r=4)
            S.dma("sp", lambda h: h.dma_start(out=qTs[:, 0, :].rearrange("p (r t) -> p r t", r=4),
                                              in_=ago3[:, 0:512, :][:, bass.ds(dyn["off_hp"], 128), :].rearrange("r p t -> p r t")),
                  reads=(("agoutA", 0),), writes=((qs_n, 0),))
            S.dma("sp", lambda h: h.dma_start(out=kTs[:, 0, :].rearrange("p (r t) -> p r t", r=4),
                                              in_=ago3[:, 512:1024, :][:, bass.ds(dyn["off_hp"], 128), :].rearrange("r p t -> p r t")),
                  reads=(("agoutA", 0),), writes=((ks_n, 0),))
            for r in range(4):
                S.dma("sp", lambda h, r=r: h.dma_start(
                    out=Vs[:, r * 4:(r + 1) * 4, :],
                    in_=agoB3[r, 0:512, :].rearrange("(c p) f -> p c f", p=128)[:, :, bass.ds(dyn["off_hp"], 128)]),
                    reads=(("agoutB", 0),), writes=((vs_n, r),))
            S.dma("pool", lambda h: h.dma_start(out=kcT[:, 0, :], in_=dr["kcT"][l, :, :]), writes=((kc_n, 0),))
            S.dma("pool", lambda h: h.dma_start(out=Vc[:, :, :], in_=dr["vc"][l].rearrange("(c p) f -> p c f", p=128)),
                  writes=((vc_n, 0),))
            def s_part(hh, n):
                nonlocal pcount
                rows = slice(hh * 64, (hh + 1) * 64)
                if n <= 1:
                    ms, slot0 = [0, 1, 2, 3], 5 + 4 * n
                elif n >= 14:
                    ms, slot0 = [12, 13, 14, 15], 13 + 4 * (n - 14)
                else:
                    ms, slot0 = list(range(n - 2, n + 3)), 0
                nw = len(ms)
                qsl = slice(n * 128, (n + 1) * 128)
                bA, bAk = bank()
                for i_, m in enumerate(ms[:4]):
                    mm(bA[:, i_ * 128:(i_ + 1) * 128], kTs[rows, 0, m * 128:(m + 1) * 128], qTs[rows, 0, qsl], True, True,
                       ((ks_n, 0), (qs_n, 0)), bAk)
                bB, bBk = bank()
                nb = 0
                if nw == 5:
                    m = ms[4]
                    mm(bB[:, 0:128], kTs[rows, 0, m * 128:(m + 1) * 128], qTs[rows, 0, qsl], True, True,
                       ((ks_n, 0), (qs_n, 0)), bBk)
                    nb = 1
                for c2 in range(2):
                    mm(bB[:, (nb + c2) * 128:(nb + c2 + 1) * 128], kcT[rows, 0, c2 * 128:(c2 + 1) * 128], qTs[rows, 0, qsl],
                       True, True, ((kc_n, 0), (qs_n, 0)), bBk)
                pi = pcount % 4
                ri = pcount % 2
                pcount += 1
                S.op("dve", lambda h: h.tensor_tensor(
                    out=sbuf_s[:, ri, 0:512], in0=bA[:, 0:512], in1=bias[:, slot0:slot0 + 4, :].rearrange("p s q -> p (s q)"),
                    op=ALU.add), reads=(bAk, (bi_n, 0)), writes=((sb_n, ri, 0),))
                if nw == 5:
                    S.op("dve", lambda h: h.tensor_tensor(
                        out=sbuf_s[:, ri, 512:640], in0=bB[:, 0:128], in1=bias[:, slot0 + 4, :], op=ALU.add),
                        reads=(bBk, (bi_n, 0)), writes=((sb_n, ri, 1),))
                nwc = nw * 128
                S.op("act", lambda h: h.activation(out=pT[:, pi, 0:nwc], in_=sbuf_s[:, ri, 0:nwc], func=AF.Exp),
                     reads=((sb_n, ri, 0), (sb_n, ri, 1)), writes=((pT_n, pi),))
                S.op("act", lambda h: h.activation(
                    out=pT[:, pi, nwc:nwc + 256], in_=bB[:, nb * 128:(nb + 2) * 128], func=AF.Exp),
                    reads=(bBk,), writes=((pT_n, pi, "c"),))
                return dict(hh=hh, n=n, rows=rows, ms=ms, nw=nw, qsl=qsl, pi=pi, ri=ri)

            def pv_part(st_):
                hh, n, rows, ms, nw, qsl, pi, ri = (st_[k_] for k_ in ("hh", "n", "rows", "ms", "nw", "qsl", "pi", "ri"))
                bo, bok = bank()
                ntile = nw + 2
                for i_ in range(ntile):
                    lhs = Vs[:, ms[i_], :] if i_ < nw else Vc[:, i_ - nw, :]
                    mm(bo[:, 0:128], lhs, pT[:, pi, i_ * 128:(i_ + 1) * 128], i_ == 0, i_ == ntile - 1,
                       ((vs_n, 0), (vs_n, 1), (vs_n, 2), (vs_n, 3), (vc_n, 0), (pT_n, pi), (pT_n, pi, "c")), bok)
                for i_ in range(ntile):
                    mm(bo[:, 128:256], onesb[:], pT[:, pi, i_ * 128:(i_ + 1) * 128], i_ == 0, i_ == ntile - 1,
                       (("onesb", 0), (pT_n, pi), (pT_n, pi, "c")), bok)
                S.op("dve", lambda h: h.reciprocal(out=rden[rows, ri, 0:128], in_=bo[rows, 128:256]),
                     reads=(bok,), writes=((rd_n, ri),))
                S.op("dve", lambda h: h.tensor_tensor(
                    out=oTm[rows, 0, qsl], in0=bo[rows, 0:128], in1=rden[rows, ri, 0:128], op=ALU.mult),
                    reads=(bok, (rd_n, ri)), writes=((om_n, n, hh),))

            prev_st = None
            for hh in range(2):
                S.dma("sp", lambda h, hh=hh: h.dma_start(out=bias[:].rearrange("p s q -> p (s q)"), in_=dr["bsmp"][l, hh, :, :]),
                      writes=((bi_n, 0),))
                for n in range(16):
                    st_ = s_part(hh, n)
                    if prev_st is not None:
                        pv_part(prev_st)
                    prev_st = st_
            pv_part(prev_st)
            S.dma("sp", lambda h: h.dma_start(out=ag2in.ap()[:, :], in_=oTm[:, 0, :]),
                  reads=tuple((om_n, n, hh) for n in range(16) for hh in range(2)), writes=(("ag2in", 0),))
            S.collective(lambda h: h.collective_compute("AllGather", ALU.bypass, replica_groups=groups,
                                                        ins=[ag2in.ap().opt()], outs=[ag2out.ap().opt()]),
                         reads=(("ag2in", 0),), writes=(("ag2out", 0),))
            R.free(qs_n), R.free(ks_n), R.free(vs_n), R.free(kc_n), R.free(vc_n), R.free(bi_n), R.free(om_n), R.free(sb_n)
            R.free(pT_n), R.free(rd_n)

            chk("sattn")
            fa_n, fTall = R.alloc("fTall", 2 * 2048 * 2, BF16, [2, 2048])
            for fh in range(2):
                S.dma("sp", lambda h, fh=fh: h.dma_start(
                    out=fTall[:, fh, :].rearrange("p (r t) -> p r t", r=4),
                    in_=agoB3[:, 512 + fh * 128:512 + (fh + 1) * 128, :].rearrange("r p t -> p r t")),
                    reads=(("agoutB", 0),), writes=((fa_n, fh),))
            abs_n, ABs = R.alloc("ABs", 16 * 512 * 2, BF16, [16, 512])
            for t16 in range(16):
                tl = slice(t16 * 128, (t16 + 1) * 128)
                ba, bak = bank()
                for fh in range(2):
                    mm(ba[:, fh * 128:(fh + 1) * 128], fTall[:, fh, tl], fcc[:, 2, :], True, True, ((fa_n, fh), ("fcc", 0)), bak)
                    mm(ba[:, 256 + fh * 128:256 + (fh + 1) * 128], fTall[:, fh, tl], fcc[:, 3, :], True, True,
                       ((fa_n, fh), ("fcc", 0)), bak)
                if t16 % 2 == 0:
                    S.op("act", lambda h, ba=ba, t16=t16: h.activation(out=ABs[:, t16, :], in_=ba[:], func=AF.Copy),
                         reads=(bak,), writes=((abs_n, t16),))
                else:
                    S.op("dve", lambda h, ba=ba, t16=t16: h.tensor_copy(out=ABs[:, t16, :], in_=ba[:]),
                         reads=(bak,), writes=((abs_n, t16),))
            bo0, bo0k = bank()
            bo1, bo1k = bank()
            bos = ((bo0, bo0k), (bo1, bo1k))
            ctmv = dr["ctm"].rearrange("s (c p) t -> s p c t", p=128)
            for piece in range(4):
                stc, sck = stage()
                sts, ssk = stage()
                vc_ = stc[:, 0:2048].rearrange("p (c t) -> p c t", c=4)
                vs_ = sts[:, 0:2048].rearrange("p (c t) -> p c t", c=4)
                S.dma("sp", lambda h, piece=piece, vc_=vc_: h.dma_start(out=vc_, in_=ctmv[0, :, piece * 4:(piece + 1) * 4, :]),
                      writes=(sck,))
                S.dma("sp", lambda h, piece=piece, vs_=vs_: h.dma_start(out=vs_, in_=ctmv[1, :, piece * 4:(piece + 1) * 4, :]),
                      writes=(ssk,))
                for t4 in range(4):
                    t16 = piece * 4 + t4
                    for fh in range(2):
                        bo, bok = bos[fh]
                        mm(bo[:], ABs[:, t16, fh * 128:(fh + 1) * 128], vc_[:, t4, :], t16 == 0, False, ((abs_n, t16), sck), bok)
                        mm(bo[:], ABs[:, t16, 256 + fh * 128:256 + (fh + 1) * 128], vs_[:, t4, :], False, t16 == 15,
                           ((abs_n, t16), ssk), bok)
            for fh in range(2):
                bo, bok = bos[fh]
                S.op("act", lambda h, bo=bo, fh=fh: h.activation(out=fmT[:, fh, tok(2)], in_=bo[:], func=AF.Copy),
                     reads=(bok,), writes=((fm_n, fh, 2, 0),))
            S.dma("sp", lambda h: h.dma_start(
                out=oaT[:, :, tok(2)],
                in_=ag2out.ap().rearrange("(c p) t -> p c t", p=128)[:, :, bass.ds(dyn["off_j"], 512)]),
                reads=(("ag2out", 0),), writes=tuple((oa_n, hp, "s") for hp in range(4)))
            R.free(abs_n), R.free(fa_n)

            chk("sfnet")
            mx_n, mixedT = R.alloc("mixedT", 8 * NTOK * 2, BF16, [8, NTOK])
            ac_n, acc = R.alloc("acc", 512 * 4, F32, [1, 512])
            oa_keys = lambda blk: tuple((oa_n, hp, b4, hh) for hp in range(4) for b4 in (2 * blk, 2 * blk + 1) for hh in range(2)) \
                if blk < 2 else tuple((oa_n, hp, "s") for hp in range(4))
            ob_keys = lambda blk: tuple((ob_n, pr, blk, gh) for pr in range(2) for gh in range(2))
            fm_keys = lambda blk: tuple((fm_n, fh, blk, b2) for fh in range(2) for b2 in range(2)) if blk < 2 \
                else tuple((fm_n, fh, 2, 0) for fh in range(2))
            def merge_loads(dcp):
                gst = []
                for br in range(3):
                    s_, k_ = stage()
                    v_ = s_[:, 0:2048].rearrange("p (c n) -> p c n", c=8)
                    c0 = 2304 + br * 1024 + dcp * 256
                    load_w(dr["w_in"][l, :, c0:c0 + 256].rearrange("(c p) n -> p c n", p=128), v_, k_)
                    gst.append((v_, k_))
                sp_, pk = stage()
                vp = sp_[:, 0:2048].rearrange("p (c n) -> p c n", c=8)
                load_w(dr["p_all"][l, :, dcp * 256:(dcp + 1) * 256].rearrange("(c p) n -> p c n", p=128), vp, pk)
                return gst, vp, pk

            nxt = merge_loads(0)
            for step in range(8):
                dcp = step % 4
                gst, vp, pk = nxt
                if step + 1 < 8:
                    nxt = merge_loads((step + 1) % 4)
                for blk in ((0, 1) if step < 4 else (2,)):
                    for cc in range(2):
                        dc = dcp * 2 + cc
                        for br in range(3):
                            vg_, gk_ = gst[br]
                            bg, bgk = bank()
                            for kc in range(8):
                                mm(bg[:], vg_[:, kc, cc * 128:(cc + 1) * 128], zT[:, kc, tok(blk)], kc == 0, kc == 7,
                                   (gk_, ("z", kc, blk)), bgk)
                            by, byk = bank()
                            if br == 0:
                                for c4 in range(4):
                                    mm(by[:], vp[:, c4, cc * 128:(cc + 1) * 128], oaT[:, c4, tok(blk)], c4 == 0, c4 == 3,
                                       (pk,) + oa_keys(blk), byk)
                            elif br == 1:
                                for c2 in range(2):
                                    mm(by[:], vp[:, 4 + c2, cc * 128:(cc + 1) * 128], obT[:, c2, tok(blk)], c2 == 0, c2 == 1,
                                       (pk,) + ob_keys(blk), byk)
                            else:
                                for c2 in range(2):
                                    mm(by[:], vp[:, 6 + c2, cc * 128:(cc + 1) * 128], fmT[:, c2, tok(blk)], c2 == 0, c2 == 1,
                                       (pk,) + fm_keys(blk), byk)
                            sgt, sgk = rotp("sg", sgp)
                            S.op("act", lambda h, sgt=sgt, bg=bg, br=br, dc=dc: h.activation(
                                out=sgt[:], in_=bg[:], func=AF.Sigmoid, bias=bgate(l, br, dc), scale=1.0),
                                reads=(bgk, ("vecs", 0)), writes=(sgk,))
                            if br == 0:
                                S.op("dve", lambda h, sgt=sgt, by=by: h.tensor_tensor(out=acc[:, 0, :], in0=sgt[:], in1=by[:],
                                                                                      op=ALU.mult),
                                     reads=(sgk, byk), writes=((ac_n, 0),))
                            elif br == 1:
                                S.op("dve", lambda h, sgt=sgt, by=by: h.tensor_tensor(out=sgt[:], in0=sgt[:], in1=by[:],
                                                                                      op=ALU.mult),
                                     reads=(sgk, byk), writes=(sgk,))
                                S.op("dve", lambda h, sgt=sgt: h.tensor_tensor(out=acc[:, 0, :], in0=acc[:, 0, :], in1=sgt[:],
                                                                               op=ALU.add),
                                     reads=(sgk, (ac_n, 0)), writes=((ac_n, 0),))
                            else:
                                S.op("dve", lambda h, sgt=sgt, by=by: h.tensor_tensor(out=sgt[:], in0=sgt[:], in1=by[:],
                                                                                      op=ALU.mult),
                                     reads=(sgk, byk), writes=(sgk,))
                                S.op("dve", lambda h, sgt=sgt, dc=dc, blk=blk: h.tensor_tensor(
                                    out=mixedT[:, dc, tok(blk)], in0=acc[:, 0, :], in1=sgt[:], op=ALU.add),
                                    reads=(sgk, (ac_n, 0)), writes=((mx_n, dc, blk),))
            vos = []
            for dcp in range(4):
                so_, ok_ = stage()
                vo = so_[:, 0:2048].rearrange("p (c n) -> p c n", c=8)
                load_w(dr["w_out"][l, :, dcp * 256:(dcp + 1) * 256].rearrange("(c p) n -> p c n", p=128), vo, ok_)
                vos.append((vo, ok_))
            for blk in range(NBLK):
                cond = 0 if blk < 2 else 1
                for dc in range(8):
                    vo, ok_ = vos[dc // 2]
                    cc = dc % 2
                    bd, bdk = bank()
                    for kc in range(8):
                        mm(bd[:], vo[:, kc, cc * 128:(cc + 1) * 128], mixedT[:, kc, tok(blk)], kc == 0, kc == 7,
                           (ok_, (mx_n, kc, blk)), bdk)
                    S.op("dve", lambda h, bd=bd, dc=dc, blk=blk, cond=cond: h.scalar_tensor_tensor(
                        out=hT[:, dc, tok(blk)], in0=bd[:], scalar=GV[:, cond, 1, dc:dc + 1],
                        in1=hT[:, dc, tok(blk)], op0=ALU.mult, op1=ALU.add),
                        reads=(bdk, ("GV", par, cond, 1), ("h", dc, blk)), writes=(("h", dc, blk),))
                    if post_blk is not None:
                        post_blk.dc(blk, dc)
                if post_blk is not None:
                    post_blk(blk)
            if post_blk is not None:
                post_blk(NBLK)
            R.free(mx_n), R.free(ac_n), R.free(oa_n), R.free(ob_n), R.free(fm_n)

        S.op("dve", lambda h: h.memset(small[:], RMS_EPS), writes=(("small", 0),))
        try:
            for _ in mod_gen(0):
                pass
            for l in range(nlayers):
                chk("mod")
                ffn(l, 0, post_blk=make_post(lambda b_, l=l: normC(l, 1, b_)), skip_norm=(l > 0))
                chk("ffn1")
                mixers(l, post_blk=make_post(lambda b_, l=l: normC(l, 2, b_)))
                chk("mix")
                if l + 1 < nlayers:
                    ffn(l, 1, bgen=mod_gen(l + 1), post_blk=make_post(lambda b_, l=l: normC(l + 1, 0, b_)), skip_norm=True)
                else:
                    ffn(l, 1, post_blk=make_post(finalC), skip_norm=True)
        except StopBuild:
            pass
        if stop is not None:
            for blk in range(NBLK):
                final_norm(blk)

        with nc.Block() as block:
            @block.tensor
            def _(h):
                for f in S.q["pe"]:
                    f(h)

            @block.scalar
            def _(h):
                for f in S.q["act"]:
                    f(h)

            @block.vector
            def _(h):
                for f in S.q["dve"]:
                    f(h)

            @block.gpsimd
            def _(h):
                for f in S.q["pool"]:
                    f(h)

            @block.sync
            def _(h):
                with ExitStack() as rs:
                    dyn["stack"] = rs
                    for f in S.q["sp"]:
                        f(h)
                    done = {}
                    for s, v, e in S.out_tokens:
                        if id(s) not in done or done[id(s)][1] < v:
                            done[id(s)] = (s, v)
                    for s, v in done.values():
                        h.wait_ge(s, v)
    return nc


def _bias_tables(rpb_l, heads):
    cols = np.arange(GRID_W)
    col_start = np.clip(cols - 8, 0, GRID_W - 16)
    colok = (cols[None, :] >= col_start[:, None]) & (cols[None, :] < col_start[:, None] + 16)
    dcol = np.clip(cols[None, :] - cols[:, None], -15, 15) + 15
    slots = [(5, 3 + d) for d in range(5)]
    slots += [(0, m) for m in range(4)] + [(1, m) for m in range(4)]
    slots += [(14, m) for m in range(12, 16)] + [(15, m) for m in range(12, 16)]
    out = np.full((len(heads), 128, NSLOT, 128), -1e30, np.float32)
    for hi, hd in enumerate(heads):
        for si, (n, m) in enumerate(slots):
            for qr2 in range(2):
                r = 2 * n + qr2
                r0 = min(max(r - 4, 0), ROWS - 8)
                for kr2 in range(2):
                    rk = 2 * m + kr2
                    if not (r0 <= rk < r0 + 8):
                        continue
                    drow = rk - r + 7
                    blk = np.where(colok, rpb_l[hd, drow][dcol], np.float32(-1e30))
                    out[hi, kr2 * 64:(kr2 + 1) * 64, si, qr2 * 64:(qr2 + 1) * 64] = blk.T
    return out.reshape(len(heads), 128, NSLOT * 128)


def _consts():
    bf = ml_dtypes.bfloat16
    c = np.arange(64)
    ang = 2 * np.pi * np.outer(c, c) / 64.0
    fcc = np.zeros((128, 4, 128), np.float32)
    for si, T in enumerate((256, 2048)):
        sc = 1.0 / math.sqrt(T * 64.0)
        for g in range(2):
            fcc[g * 64:(g + 1) * 64, si * 2 + 0, g * 64:(g + 1) * 64] = np.cos(ang) * sc
            fcc[g * 64:(g + 1) * 64, si * 2 + 1, g * 64:(g + 1) * 64] = np.sin(ang) * sc
    t = np.arange(256)
    a = 2 * np.pi * (np.outer(t, t) % 256) / 256.0
    ct = np.stack([np.cos(a), -np.sin(a)], 0).astype(np.float32)
    ct256 = ct.reshape(2, 2, 128, 256).transpose(2, 0, 1, 3).reshape(128, 1024)
    t2 = np.arange(2048, dtype=np.int64)
    a2 = 2 * np.pi * (np.outer(t2, t2) % 2048) / 2048.0
    ctm = np.stack([np.cos(a2), -np.sin(a2)], 0).astype(bf)
    return fcc.reshape(128, 512), np.ascontiguousarray(ct256), ctm


_NC_CACHE = {}


def kernel(x_prompt, x_sample, cache_k, cache_v, c, c_ctx, w_mod, b_mod, g_norm,
           ffn_w_gate, ffn_w_up, ffn_w_down, w_in, b_gate, rpb, sgu_norm, sgu_w, sgu_b,
           p_attn, p_sgu, p_fnet, w_out, g_final):
    f32 = np.float32
    A = lambda a: np.ascontiguousarray(np.asarray(a, dtype=f32))
    x_prompt, x_sample, cache_k, cache_v = A(x_prompt), A(x_sample), A(cache_k), A(cache_v)
    c, c_ctx, b_mod, g_norm, b_gate, rpb = A(c), A(c_ctx), A(b_mod), A(g_norm), A(b_gate), A(rpb)
    sgu_norm, sgu_w, sgu_b, g_final = A(sgu_norm), A(sgu_w), A(sgu_b), A(g_final)
    w_mod, wg, wu, wd, w_in, w_out = A(w_mod), A(ffn_w_gate), A(ffn_w_up), A(ffn_w_down), A(w_in), A(w_out)
    p_all = np.ascontiguousarray(np.concatenate([A(p_attn), A(p_sgu), A(p_fnet)], axis=1))

    def pv(v):
        lead = v.shape[:-1]
        return np.moveaxis(v.reshape(lead + (8, 128)), -1, 0)

    vecs = np.zeros((128, 512), f32)
    vecs[:, 0:96] = pv(g_norm).reshape(128, 96)
    vecs[:, 96:96 + 288] = np.moveaxis(b_mod.reshape(NL, 72, 128), -1, 0).reshape(128, 288)
    vecs[:, 384:384 + 96] = pv(b_gate).reshape(128, 96)
    vecs[:, 480:488] = pv(g_final).reshape(128, 8)
    sgu_wT = np.ascontiguousarray(sgu_w.transpose(0, 3, 1, 2).reshape(NL, 128, 512))
    sgu_nb = np.ascontiguousarray(np.broadcast_to(sgu_norm.reshape(1, NL * 256), (128, NL * 256)))
    sgu_bb = np.ascontiguousarray(np.broadcast_to(sgu_b.reshape(1, NL * 512), (128, NL * 512)))
    fcc, ct256, ctm_full = _consts()

    in_maps = []
    for core in range(8):
        sq, j = core // 4, core % 4
        xs = np.concatenate([x_prompt[core * 4:(core + 1) * 4].reshape(1024, D),
                             x_sample[sq, j * 512:(j + 1) * 512]], axis=0)
        condT = np.stack([pv(c_ctx), pv(c[sq])], axis=-1).reshape(128, 16)
        hs = slice(2 * j, 2 * j + 2)
        kcT = np.ascontiguousarray(cache_k[sq, :, :, hs, :].reshape(NL, 256, 128).transpose(0, 2, 1))
        vcm = np.ascontiguousarray(cache_v[sq, :, :, hs, :].reshape(NL, 256, 128))
        bsmp = np.stack([_bias_tables(rpb[l], [2 * j, 2 * j + 1]) for l in range(NL)], 0)
        in_maps.append({
            "xT": np.ascontiguousarray(xs.T), "vecs": vecs, "condT": np.ascontiguousarray(condT),
            "w_mod": w_mod, "wg": wg, "wu": wu, "wd": wd, "w_in": w_in, "p_all": p_all, "w_out": w_out,
            "sgu_wT": sgu_wT, "sgu_nb": sgu_nb, "sgu_bb": sgu_bb, "fcc": fcc, "ct256": ct256,
            "kcT": kcT, "vc": vcm, "bsmp": np.ascontiguousarray(bsmp),
            "ctm": np.ascontiguousarray(ctm_full[:, :, j * 512:(j + 1) * 512]),
            "info": np.array([[j * 128, j * 512, 0, 0]], dtype=np.int32),
        })
    if "nc" not in _NC_CACHE:
        _NC_CACHE["nc"] = build_program()
    nld = _NC_CACHE.get("nld", NL)
    if nld != NL:
        for m in in_maps:
            for k_ in ("w_mod", "wg", "wu", "wd", "w_in", "p_all", "w_out", "sgu_wT", "kcT", "vc", "bsmp"):
                m[k_] = np.ascontiguousarray(m[k_][:nld])
    res = run_bass_kernel_spmd(_NC_CACHE["nc"], in_maps, core_ids=list(range(8)))
    y_prompt = np.zeros((32, 256, D), f32)
    y_sample = np.zeros((2, 2048, D), f32)
    nk = np.zeros((32, NL, 256, 8, 64), f32)
    nv = np.zeros((32, NL, 256, 8, 64), f32)
    for core in range(8):
        r = res.results[core]
        sq, j = core // 4, core % 4
        y = np.asarray(r["yT"]).T
        y_prompt[core * 4:(core + 1) * 4] = y[0:1024].reshape(4, 256, D)
        y_sample[sq, j * 512:(j + 1) * 512] = y[1024:1536]
        nk[core * 4:(core + 1) * 4] = np.asarray(r["nk"]).reshape(NL, 4, 256, 8, 64).transpose(1, 0, 2, 3, 4)
        nv[core * 4:(core + 1) * 4] = np.asarray(r["nv"]).reshape(NL, 4, 256, 8, 64).transpose(1, 0, 2, 3, 4)
    return (y_prompt, y_sample, nk, nv)
```

```python
import math
import os
from contextlib import ExitStack

import numpy as np
import ml_dtypes

import concourse.bass as bass
import concourse.mybir as mybir
from concourse.bass_utils import run_bass_kernel_spmd

F32 = mybir.dt.float32
BF16 = mybir.dt.bfloat16
I32 = mybir.dt.int32
AF = mybir.ActivationFunctionType
ALU = mybir.AluOpType

D = 1024
NL = 4
DFF = 2816
NCH_FF = DFF // 128
INC = 5376
NTOK = 1536
NBLK = 3
GRID_W = 64
ROWS = 32
RMS_EPS = 1e-6
NSLOT = 21
AGROWS = 1792
NDS = 12


class Sched:
    def __init__(self, nc, stack):
        self.nc = nc
        self.q = {e: [] for e in ("pe", "act", "dve", "pool", "sp")}
        self.sem = {e: stack.enter_context(nc.semaphore("s_" + e)) for e in ("pe", "act", "dve", "pool")}
        self.cnt = {e: 0 for e in self.sem}
        self.dsem = {qn: [stack.enter_context(nc.semaphore("d_%s%d" % (qn, i))) for i in range(NDS)]
                     for qn in ("sp", "pool")}
        self.dn = {"sp": 0, "pool": 0}
        self.known = {e: {} for e in self.q}
        self.lw = {}
        self.rd = {}
        self.init_tok = {}
        self.out_tokens = []
        self.stack = stack
        self.nops = 0

    @staticmethod
    def flat(keys):
        out = []
        for k in keys:
            if isinstance(k[0], tuple):
                out.extend(Sched.flat(k))
            else:
                out.append(k)
        return tuple(out)

    def _deps(self, eng, reads, writes):
        need = {}

        def add(tok):
            s, v, pe = tok
            if pe == "pe" and eng == "pe":
                return
            k = id(s)
            if k not in need or need[k][1] < v:
                need[k] = (s, v)

        for k in reads:
            t = self.lw.get(k)
            if t is not None:
                add(t)
            elif k[0] in self.init_tok:
                for t2 in self.init_tok[k[0]]:
                    add(t2)
        for k in writes:
            t = self.lw.get(k)
            if t is not None:
                add(t)
            elif k[0] in self.init_tok:
                for t2 in self.init_tok[k[0]]:
                    add(t2)
            for t2 in self.rd.get(k, {}).values():
                add(t2)
        waits = []
        kn = self.known[eng]
        for k, (s, v) in need.items():
            if kn.get(k, 0) < v:
                kn[k] = v
                waits.append((s, v))
        return waits

    def _update(self, tok, reads, writes):
        for k in writes:
            self.lw[k] = tok
            self.rd[k] = {}
        for k in reads:
            self.rd.setdefault(k, {})[id(tok[0])] = tok

    def op(self, eng, fn, reads=(), writes=()):
        reads, writes = self.flat(reads), self.flat(writes)
        if eng != "pe":
            extra = tuple(k for k in reads if k[0] == "ps" and k not in writes)
            writes = writes + extra
        waits = self._deps(eng, reads, writes)
        self.cnt[eng] += 1
        sem = self.sem[eng]
        tok = (sem, self.cnt[eng], eng)

        def emit(h, waits=waits, fn=fn, sem=sem):
            for s, v in waits:
                h.wait_ge(s, v)
            fn(h).then_inc(sem, 1)

        self.q[eng].append(emit)
        self._update(tok, reads, writes)
        self.nops += 1
        return tok

    def dma(self, qn, fn, reads=(), writes=(), is_out=False):
        reads, writes = self.flat(reads), self.flat(writes)
        n = self.dn[qn]
        self.dn[qn] += 1
        i, k = n % NDS, n // NDS
        s = self.dsem[qn][i]
        waits = self._deps(qn, reads, writes)
        if k > 0 and self.known[qn].get(id(s), 0) < 16 * k:
            self.known[qn][id(s)] = 16 * k
            waits.append((s, 16 * k))
        tok = (s, 16 * (k + 1), qn)

        def emit(h, waits=waits, fn=fn, s=s):
            for s2, v in waits:
                h.wait_ge(s2, v)
            fn(h).then_inc(s, 16)

        self.q[qn].append(emit)
        self._update(tok, reads, writes)
        if is_out:
            self.out_tokens.append(tok)
        return tok

    def collective(self, fn, reads, writes):
        reads, writes = self.flat(reads), self.flat(writes)
        s = self.stack.enter_context(self.nc.semaphore("cc%d" % len(self.stack_cc)))
        self.stack_cc.append(s)
        waits = self._deps("pool", reads, writes)
        tok = (s, 1, "pool")

        def emit(h, waits=waits, fn=fn, s=s):
            for s2, v in waits:
                h.wait_ge(s2, v)
            fn(h).then_inc(s)

        self.q["pool"].append(emit)
        self._update(tok, reads, writes)
        return tok

    stack_cc = []

    def raw(self, eng, fn, reads=()):
        waits = self._deps(eng, reads, ())

        def emit(h, waits=waits, fn=fn):
            for s, v in waits:
                h.wait_ge(s, v)
            fn(h)

        self.q[eng].append(emit)

    def tokens_of(self, name):
        toks = []
        for k, t in self.lw.items():
            if k[0] == name:
                toks.append(t)
        for k, d in self.rd.items():
            if k[0] == name:
                toks.extend(d.values())
        best = {}
        for s, v, e in toks:
            if id(s) not in best or best[id(s)][1] < v:
                best[id(s)] = (s, v, e)
        return list(best.values())

    def forget(self, name):
        for dct in (self.lw, self.rd):
            for k in [k for k in dct if k[0] == name]:
                del dct[k]
        self.init_tok.pop(name, None)


class Region:
    def __init__(self, S, tensor, nbytes):
        self.S = S
        self.t = tensor
        self.n = nbytes
        self.live = {}
        self.freed = []
        self.gen = 0

    def alloc(self, base, nbytes, dtype, shape):
        nbytes = (nbytes + 63) // 64 * 64
        spans = sorted(self.live.values())
        lo = 0
        for a, b in spans:
            if a - lo >= nbytes:
                break
            lo = max(lo, b)
        assert lo + nbytes <= self.n, "region overflow allocating %s (%d bytes)" % (base, nbytes)
        hi = lo + nbytes
        self.gen += 1
        name = "%s#%d" % (base, self.gen)
        self.live[name] = (lo, hi)
        toks = []
        keep = []
        for a, b, tk in self.freed:
            if a < hi and lo < b:
                toks.extend(tk)
            keep.append((a, b, tk))
        self.S.init_tok[name] = toks
        ap = self.t[:, lo // 2:hi // 2]
        if dtype == F32:
            ap = ap.bitcast(F32)
        nel = 1
        for s_ in shape:
            nel *= s_
        esz = 4 if dtype == F32 else 2
        ap = ap[:, 0:nel]
        assert nel * esz <= nbytes
        if len(shape) == 2:
            ap = ap.rearrange("p (a b) -> p a b", a=shape[0])
        elif len(shape) == 3:
            ap = ap.rearrange("p (a b c) -> p a b c", a=shape[0], b=shape[1])
        return name, ap

    def free(self, name):
        lo, hi = self.live.pop(name)
        toks = self.S.tokens_of(name)
        self.freed = [(a, b, tk) for (a, b, tk) in self.freed] + [(lo, hi, toks)]
        self.S.forget(name)
        if len(self.freed) > 64:
            self.freed = self.freed[-64:]


class StopBuild(Exception):
    pass


def build_program(nlayers=NL, stop=None):
    nc = bass.Bass("TRN2", target_bir_lowering=False)
    dr = {}
    NLd = nlayers
    DBG = os.environ.get("KDBG", "")

    def din(name, shape, dt=F32):
        dr[name] = nc.dram_tensor(name, list(shape), dt, kind="ExternalInput").ap()

    def dout(name, shape, dt=F32):
        dr[name] = nc.dram_tensor(name, list(shape), dt, kind="ExternalOutput").ap()

    din("xT", [D, NTOK])
    din("vecs", [128, 512])
    din("condT", [128, 16])
    din("w_mod", [NLd, D, 9 * D])
    din("wg", [NLd, 2, D, DFF])
    din("wu", [NLd, 2, D, DFF])
    din("wd", [NLd, 2, DFF, D])
    din("w_in", [NLd, D, INC])
    din("p_all", [NLd, D, D])
    din("w_out", [NLd, D, D])
    din("sgu_wT", [NLd, 128, 4 * 128])
    din("sgu_nb", [128, NL * 256])
    din("sgu_bb", [128, NL * 4 * 128])
    din("fcc", [128, 4 * 128])
    din("ct256", [128, 2 * 2 * 256])
    din("kcT", [NLd, 128, 256])
    din("vc", [NLd, 256, 128])
    din("bsmp", [NLd, 2, 128, NSLOT * 128])
    din("ctm", [2, 2048, 512], BF16)
    din("info", [1, 4], I32)
    dout("yT", [D, NTOK])
    dout("nk", [NL, 1024, 512])
    dout("nv", [NL, 1024, 512])
    aginA = nc.dram_tensor("aginA", [1024, 512], BF16)
    agoutA = nc.dram_tensor("agoutA", [4096, 512], BF16)
    aginB = nc.dram_tensor("aginB", [768, 512], BF16)
    agoutB = nc.dram_tensor("agoutB", [3072, 512], BF16)
    ag2in = nc.dram_tensor("ag2in", [128, 2048], BF16)
    ag2out = nc.dram_tensor("ag2out", [512, 2048], BF16)
    groups = [[0, 1, 2, 3], [4, 5, 6, 7]]

    with ExitStack() as st:
        def sb(name, shape, dt):
            return st.enter_context(nc.sbuf_tensor("sb_" + name, list(shape), dt))

        S = Sched(nc, st)
        S.stack_cc = []
        hT = sb("hT", [128, 8, NTOK], F32)
        zT = sb("zT", [128, 8, NTOK], BF16)
        NST = 8
        wst = [sb("wst%d" % i, [128, 2048], BF16) for i in range(NST)]
        ftp = [sb("ft%d" % i, [128, 512], F32) for i in range(5)]
        sqp = tbp = sgp = ftp
        rs_t = sb("rs", [128, 512], F32)
        rstd_t = sb("rstd", [128, 512], F32)
        vecs = sb("vecs", [128, 512], F32)
        condT = sb("condT", [128, 16], F32)
        csT = sb("csT", [128, 8, 2], BF16)
        MV2 = sb("MV", [128, 2, 2, 72], F32)
        AV2 = sb("AV", [128, 2, 2, 3, 8], F32)
        GV2 = sb("GV", [128, 2, 2, 3, 8], F32)
        acc2 = [sb("sacc%d" % i, [128, 512], F32) for i in range(2)]
        onesf = sb("onesf", [128, 128], F32)
        onesb = sb("onesb", [128, 128], BF16)
        fcc = sb("fcc", [128, 4, 128], BF16)
        ct256 = sb("ct256", [128, 2, 2, 256], BF16)
        sguw = sb("sguw", [128, 1, 4, 128], BF16)
        sgun = sb("sgun", [128, 1, 256], F32)
        sgub = sb("sgub", [128, 4, 128], F32)
        small = sb("small", [128, 8], F32)
        info_t = sb("info", [1, 4], I32)
        RBYTES = 70 * 1024
        Rt = sb("R", [128, RBYTES // 2], BF16)
        R = Region(S, Rt, RBYTES)
        banks = [st.enter_context(nc.psum_tensor("ps%d" % i, [128, 512], F32)) for i in range(8)]

        V_GN, V_BMOD, V_BGATE, V_GFIN = 0, 96, 96 + 288, 96 + 288 + 96

        def gn(l, i, c):
            o = V_GN + (l * 3 + i) * 8 + c
            return vecs[:, o:o + 1]

        def bgate(l, br, dc):
            o = V_BGATE + (l * 3 + br) * 8 + dc
            return vecs[:, o:o + 1]

        rot = {"bank": 0, "wst": 0, "ft": 0, "acc": 0}
        cur = {"par": 0}

        def bank():
            i = rot["bank"]
            rot["bank"] = (i + 1) % 7
            return banks[i], ("ps", i)

        def stage():
            i = rot["wst"]
            rot["wst"] = (i + 1) % NST
            return wst[i], (("wst", i, 0), ("wst", i, 1))

        def rotp(name, pool):
            i = rot["ft"]
            rot["ft"] = (i + 1) % len(pool)
            return pool[i], ("ft", i)

        def tok(blk):
            return slice(blk * 512, (blk + 1) * 512)

        def load_w(src_ap, view, key):
            S.dma("pool", lambda h: h.dma_start(out=view, in_=src_ap), reads=(), writes=(key,))

        def mm(out, lhsT, rhs, start, stop, reads, wkey):
            S.op("pe", lambda h: h.matmul(out, lhsT, rhs, start=start, stop=stop), reads=reads, writes=(wkey,))

        def gelu_from_psum(ps, pkey, out, okey, width, npart=128):
            S.op("act", lambda h: h.activation(out=out, in_=ps, func=AF.Gelu_apprx_tanh), reads=(pkey,), writes=(okey,))

        S.dma("sp", lambda h: h.dma_start(out=info_t[:], in_=dr["info"][:, :]), writes=(("info", 0),))
        S.dma("sp", lambda h: h.dma_start(out=vecs[:], in_=dr["vecs"][:, :]), writes=(("vecs", 0),))
        S.dma("sp", lambda h: h.dma_start(out=condT[:], in_=dr["condT"][:, :]), writes=(("condT", 0),))
        xv = dr["xT"].rearrange("(c p) t -> p c t", p=128)
        for c in range(8):
            S.dma("sp", lambda h, c=c: h.dma_start(out=hT[:, c, :], in_=xv[:, c, :]),
                  writes=tuple(("h", c, b) for b in range(NBLK)))
        S.dma("pool", lambda h: h.dma_start(out=fcc[:].rearrange("p a b -> p (a b)"), in_=dr["fcc"][:, :]),
              writes=(("fcc", 0),))
        S.dma("pool", lambda h: h.dma_start(out=ct256[:].rearrange("p a b c -> p (a b c)"), in_=dr["ct256"][:, :]),
              writes=(("ct256", 0),))
        S.op("dve", lambda h: h.memset(onesf[:], 1.0 / D), writes=(("onesf", 0),))
        S.op("dve", lambda h: h.memset(onesb[:], 1.0), writes=(("onesb", 0),))
        S.op("act", lambda h: h.activation(out=csT[:].rearrange("p c t -> p (c t)"), in_=condT[:], func=AF.Silu),
             reads=(("condT", 0),), writes=(("csT", 0),))

        dyn = {}

        def load_regs(h):
            r0 = dyn["stack"].enter_context(h.register("r_hp"))
            r1 = dyn["stack"].enter_context(h.register("r_j"))
            h.reg_load(r0, info_t[0:1, 0:1])
            h.reg_load(r1, info_t[0:1, 1:2])
            dyn["off_hp"] = h.snap(r0)
            dyn["off_j"] = h.snap(r1)

        S.raw("sp", load_regs, reads=(("info", 0),))

        def mod_gen(l, early=False):
            par = l % 2
            MV, AV, GV = MV2[:, par], AV2[:, par], GV2[:, par]
            mps, mk = banks[7], ("ps", 7)
            mv = mps[:, 0:144].rearrange("p (c t) -> p c t", t=2)

            def derive(i_list, c0, c1):
                bm = vecs[:, V_BMOD + l * 72 + c0:V_BMOD + l * 72 + c1]
                for cond in range(2):
                    S.op("dve", lambda h, cond=cond: h.tensor_tensor(out=MV[:, cond, c0:c1], in0=mv[:, c0:c1, cond], in1=bm,
                                                                     op=ALU.add),
                         reads=(mk, ("vecs", 0)), writes=tuple(("MV", par, cond, i) for i in i_list))
                    for i in i_list:
                        gnv = vecs[:, V_GN + (l * 3 + i) * 8:V_GN + (l * 3 + i) * 8 + 8]
                        S.op("dve", lambda h, cond=cond, i=i, gnv=gnv: h.scalar_tensor_tensor(
                            out=AV[:, cond, i, :], in0=MV[:, cond, (3 * i + 1) * 8:(3 * i + 1) * 8 + 8], scalar=1.0,
                            in1=gnv, op0=ALU.add, op1=ALU.mult),
                            reads=(("MV", par, cond, i), ("vecs", 0)), writes=(("AV", par, cond, i),))
                        S.op("dve", lambda h, cond=cond, i=i: h.tensor_scalar(
                            out=GV[:, cond, i, :], in0=MV[:, cond, (3 * i + 2) * 8:(3 * i + 2) * 8 + 8],
                            scalar1=(1.0 if i == 1 else 0.5), scalar2=None, op0=ALU.mult),
                            reads=(("MV", par, cond, i),), writes=(("GV", par, cond, i),))

            for g in range(36):
                stt, sk = stage()
                v = stt[:, 0:2048].rearrange("p (c n) -> p c n", c=8)
                load_w(dr["w_mod"][l, :, g * 256:(g + 1) * 256].rearrange("(c p) n -> p c n", p=128), v, sk)
                for sc in range(2):
                    col = g * 2 + sc
                    for kc in range(8):
                        mm(mv[:, col, :], v[:, kc, sc * 128:(sc + 1) * 128], csT[:, kc, :], kc == 0, kc == 7,
                           (sk, ("csT", 0)), mk)
                if early and g == 11:
                    derive([0], 0, 24)
                    yield "i0"
                else:
                    yield None
            if early:
                derive([1, 2], 24, 72)
            else:
                derive([0, 1, 2], 0, 72)

        pend = {}

        def normA(blk):
            ai = rot["acc"]
            rot["acc"] = (ai + 1) % 2
            acc_, ak = acc2[ai], ("sacc", ai)
            prev = None
            for c in range(8):
                sq, sk = rotp("sq", sqp)
                S.op("act", lambda h, c=c, sq=sq: h.activation(out=sq[:], in_=hT[:, c, tok(blk)], func=AF.Square),
                     reads=(("h", c, blk),), writes=(sk,))
                if c == 0:
                    prev = (sq, sk)
                elif c == 1:
                    p_sq, p_k = prev
                    S.op("dve", lambda h, sq=sq, p_sq=p_sq: h.tensor_tensor(out=acc_[:], in0=p_sq[:], in1=sq[:], op=ALU.add),
                         reads=(sk, p_k), writes=(ak,))
                else:
                    S.op("dve", lambda h, sq=sq: h.tensor_tensor(out=acc_[:], in0=acc_[:], in1=sq[:], op=ALU.add),
                         reads=(sk, ak), writes=(ak,))
            pend[blk] = (acc_, ak)

        def normB(blk):
            acc_, ak = pend.pop(blk)
            bs, bk = bank()
            mm(bs[:], onesf[:], acc_[:], True, True, (ak, ("onesf", 0)), bk)
            S.op("act", lambda h: h.activation(out=rs_t[:], in_=bs[:], func=AF.Sqrt, bias=small[:, 0:1], scale=1.0),
                 reads=(bk, ("small", 0)), writes=(("rs", 0),))
            S.op("dve", lambda h: h.reciprocal(out=rstd_t[:], in_=rs_t[:]), reads=(("rs", 0),), writes=(("rstd", 0),))

        def normC(l, i, blk):
            cond = 0 if blk < 2 else 1
            par = l % 2
            MV, AV = MV2[:, par], AV2[:, par]
            for c in range(8):
                tb, tk = rotp("tb", tbp)
                S.op("dve", lambda h, c=c, tb=tb: h.tensor_tensor(out=tb[:], in0=hT[:, c, tok(blk)], in1=rstd_t[:],
                                                                  op=ALU.mult),
                     reads=(("h", c, blk), ("rstd", 0)), writes=(tk,))
                S.op("act", lambda h, c=c, tb=tb: h.activation(
                    out=zT[:, c, tok(blk)], in_=tb[:], func=AF.Identity, scale=AV[:, cond, i, c:c + 1],
                    bias=MV[:, cond, 3 * i * 8 + c:3 * i * 8 + c + 1]),
                    reads=(tk, ("AV", par, cond, i), ("MV", par, cond, i)), writes=(("z", c, blk),))

        def finalC(blk):
            yv = dr["yT"].rearrange("(c p) t -> p c t", p=128)
            for c in range(8):
                tb, tk = rotp("tb", tbp)
                S.op("dve", lambda h, c=c, tb=tb: h.tensor_tensor(out=tb[:], in0=hT[:, c, tok(blk)], in1=rstd_t[:],
                                                                  op=ALU.mult),
                     reads=(("h", c, blk), ("rstd", 0)), writes=(tk,))
                S.op("act", lambda h, c=c, tb=tb: h.activation(
                    out=hT[:, c, tok(blk)], in_=tb[:], func=AF.Identity,
                    scale=vecs[:, V_GFIN + c:V_GFIN + c + 1]),
                    reads=(tk, ("vecs", 0)), writes=(("h", c, blk),))
                S.dma("sp", lambda h, c=c: h.dma_start(out=yv[:, c, tok(blk)], in_=hT[:, c, tok(blk)]),
                      reads=(("h", c, blk),), writes=(("yT", c, blk),), is_out=True)

        def norm(l, i, blk):
            normA(blk)
            normB(blk)
            normC(l, i, blk)

        def final_norm(blk):
            normA(blk)
            normB(blk)
            finalC(blk)

        class make_post:
            def __init__(self, cfn):
                self.cfn = cfn
                self.st = {}

            def dc(self, blk, c):
                if c == 0:
                    ai = rot["acc"]
                    rot["acc"] = (ai + 1) % 2
                    self.st[blk] = dict(acc=acc2[ai], ak=("sacc", ai), prev=None)
                d = self.st[blk]
                acc_, ak = d["acc"], d["ak"]
                sq, sk = rotp("sq", sqp)
                S.op("act", lambda h: h.activation(out=sq[:], in_=hT[:, c, tok(blk)], func=AF.Square),
                     reads=(("h", c, blk),), writes=(sk,))
                if c == 0:
                    d["prev"] = (sq, sk)
                elif c == 1:
                    p_sq, p_k = d["prev"]
                    S.op("dve", lambda h: h.tensor_tensor(out=acc_[:], in0=p_sq[:], in1=sq[:], op=ALU.add),
                         reads=(sk, p_k), writes=(ak,))
                else:
                    S.op("dve", lambda h: h.tensor_tensor(out=acc_[:], in0=acc_[:], in1=sq[:], op=ALU.add),
                         reads=(sk, ak), writes=(ak,))
                if c == 7:
                    pend[blk] = (acc_, ak)

            def __call__(self, blk):
                if blk >= 1:
                    normB(blk - 1)
                    self.cfn(blk - 1)

        def ffn(l, i, bgen=None, post_blk=None, skip_norm=False):
            ni = 0 if i == 0 else 2
            par = l % 2
            GV = GV2[:, par]

            def tick():
                if bgen is not None:
                    next(bgen, None)
            if not skip_norm:
                for blk in range(NBLK):
                    norm(l, ni, blk)
            hname, hid = R.alloc("hid", 8 * NTOK * 2, BF16, [8, NTOK])
            for (c0, c1) in ((0, 8), (8, 16), (16, 22)):
                for pair in range(c0, c1, 2):
                    sg_, gk = stage()
                    su_, uk = stage()
                    vg = sg_[:, 0:2048].rearrange("p (c n) -> p c n", c=8)
                    vu = su_[:, 0:2048].rearrange("p (c n) -> p c n", c=8)
                    load_w(dr["wg"][l, i, :, pair * 128:pair * 128 + 256].rearrange("(c p) n -> p c n", p=128), vg, gk)
                    load_w(dr["wu"][l, i, :, pair * 128:pair * 128 + 256].rearrange("(c p) n -> p c n", p=128), vu, uk)
                    tick()
                    for cc in range(2):
                        cl = pair + cc - c0
                        for blk in range(NBLK):
                            bg, bgk = bank()
                            for kc in range(8):
                                mm(bg[:], vg[:, kc, cc * 128:(cc + 1) * 128], zT[:, kc, tok(blk)], kc == 0, kc == 7,
                                   (gk, ("z", kc, blk)), bgk)
                            bu, buk = bank()
                            for kc in range(8):
                                mm(bu[:], vu[:, kc, cc * 128:(cc + 1) * 128], zT[:, kc, tok(blk)], kc == 0, kc == 7,
                                   (uk, ("z", kc, blk)), buk)
                            sgt, sgk = rotp("sg", sgp)
                            S.op("act", lambda h, sgt=sgt, bg=bg: h.activation(out=sgt[:], in_=bg[:], func=AF.Silu),
                                 reads=(bgk,), writes=(sgk,))
                            S.op("dve", lambda h, sgt=sgt, bu=bu, cl=cl, blk=blk: h.tensor_tensor(
                                out=hid[:, cl, tok(blk)], in0=sgt[:], in1=bu[:], op=ALU.mult),
                                reads=(sgk, buk), writes=((hname, cl, blk),))
                nch = c1 - c0
                if c1 == NCH_FF:
                    if bgen is not None:
                        for _ in bgen:
                            pass
                    vds = []
                    for dcp in range(4):
                        sd_, dk = stage()
                        vd = sd_[:, 0:nch * 256].rearrange("p (c n) -> p c n", c=nch)
                        load_w(dr["wd"][l, i, c0 * 128:c1 * 128, dcp * 256:(dcp + 1) * 256].rearrange("(c p) n -> p c n", p=128),
                               vd, dk)
                        vds.append((vd, dk))
                    for blk in range(NBLK):
                        cond = 0 if blk < 2 else 1
                        for dc in range(8):
                            vd, dk = vds[dc // 2]
                            cc = dc % 2
                            bd, bdk = bank()
                            for cl in range(nch):
                                mm(bd[:], vd[:, cl, cc * 128:(cc + 1) * 128], hid[:, cl, tok(blk)], cl == 0, cl == nch - 1,
                                   (dk, (hname, cl, blk)), bdk)
                            S.op("dve", lambda h, bd=bd, dc=dc, blk=blk, cond=cond: h.scalar_tensor_tensor(
                                out=hT[:, dc, tok(blk)], in0=bd[:], scalar=GV[:, cond, ni, dc:dc + 1],
                                in1=hT[:, dc, tok(blk)], op0=ALU.mult, op1=ALU.add),
                                reads=(bdk, ("GV", par, cond, ni), ("h", dc, blk)), writes=(("h", dc, blk),))
                            if post_blk is not None:
                                post_blk.dc(blk, dc)
                        if post_blk is not None:
                            post_blk(blk)
                    if post_blk is not None:
                        post_blk(NBLK)
                    continue
                for dc in range(8):
                    sd_, dk = stage()
                    vd = sd_[:, 0:nch * 128].rearrange("p (c n) -> p c n", c=nch)
                    load_w(dr["wd"][l, i, c0 * 128:c1 * 128, dc * 128:(dc + 1) * 128].rearrange("(c p) n -> p c n", p=128),
                           vd, dk)
                    tick()
                    for blk in range(NBLK):
                        cond = 0 if blk < 2 else 1
                        bd, bdk = bank()
                        for cl in range(nch):
                            mm(bd[:], vd[:, cl, :], hid[:, cl, tok(blk)], cl == 0, cl == nch - 1,
                               (dk, (hname, cl, blk)), bdk)
                        S.op("dve", lambda h, bd=bd, dc=dc, blk=blk, cond=cond: h.scalar_tensor_tensor(
                            out=hT[:, dc, tok(blk)], in0=bd[:], scalar=GV[:, cond, ni, dc:dc + 1],
                            in1=hT[:, dc, tok(blk)], op0=ALU.mult, op1=ALU.add),
                            reads=(bdk, ("GV", par, cond, ni), ("h", dc, blk)), writes=(("h", dc, blk),))
            R.free(hname)

        def chk(tag):
            if stop == tag:
                raise StopBuild()

        def mixers(l, post_blk=None):
            par = l % 2
            GV = GV2[:, par]
            oa_n, oaT = R.alloc("oaT", 4 * NTOK * 2, BF16, [4, NTOK])
            ob_n, obT = R.alloc("obT", 2 * NTOK * 2, BF16, [2, NTOK])
            fm_n, fmT = R.alloc("fmT", 2 * NTOK * 2, BF16, [2, NTOK])
            q_n, qT = R.alloc("qT", NTOK * 2, BF16, [1, NTOK])
            k_n, kT = R.alloc("kT", NTOK * 2, BF16, [1, NTOK])
            vb_n, Vb = R.alloc("Vb", 8 * 128 * 2, BF16, [8, 128])
            kv_n, KVst = R.alloc("KVst", 8 * 256 * 4, F32, [8, 256])
            agv_n, agV = R.alloc("agV", 4 * 512 * 2, BF16, [4, 512])
            pT_n, pT = R.alloc("pT", 4 * 1024 * 2, BF16, [4, 1024])
            rd_n, rden = R.alloc("rden", 2 * 256 * 4, F32, [2, 256])
            nkv = dr["nk"][l].rearrange("(c p) f -> p c f", p=128)
            nvv = dr["nv"][l].rearrange("(c p) f -> p c f", p=128)
            aginv = aginA.ap()
            aginvB = aginB.ap()
            win3 = dr["w_in"][l, :, 0:1536].rearrange("(c p) (j n) -> p c j n", p=128, j=3)
            pcount = 0
            f_n, fT = R.alloc("fT", 2 * NTOK * 2, BF16, [2, NTOK])
            aq_n, agQ = R.alloc("agQ", 4 * 512 * 2, BF16, [4, 512])
            ak_n, agK = R.alloc("agK", 4 * 512 * 2, BF16, [4, 512])
            wide = []
            for w4 in range(6):
                s_, k_ = stage()
                v_ = s_[:, 0:2048].rearrange("p (c n) -> p c n", c=8)
                load_w(dr["w_in"][l, :, w4 * 256:(w4 + 1) * 256].rearrange("(c p) n -> p c n", p=128), v_, k_)
                wide.append((v_, k_))
            stf, sfk = stage()
            vf_ = stf[:, 0:2048].rearrange("p (c n) -> p c n", c=8)
            load_w(dr["w_in"][l, :, 2048:2304].rearrange("(c p) n -> p c n", p=128), vf_, sfk)
            for hp in range(4):
                vq_, qk_ = wide[hp // 2]
                vk_, kk_ = wide[2 + hp // 2]
                cs = slice((hp % 2) * 128, (hp % 2 + 1) * 128)
                bq, bqk = bank()
                for kc in range(8):
                    mm(bq[:], vq_[:, kc, cs], zT[:, kc, tok(2)], kc == 0, kc == 7, (qk_, ("z", kc, 2)), bqk)
                S.op("act", lambda h, bq=bq, hp=hp: h.activation(out=agQ[:, hp, :], in_=bq[:], func=AF.Copy, scale=0.125),
                     reads=(bqk,), writes=((aq_n, hp),))
                bk_, bkk = bank()
                for kc in range(8):
                    mm(bk_[:], vk_[:, kc, cs], zT[:, kc, tok(2)], kc == 0, kc == 7, (kk_, ("z", kc, 2)), bkk)
                S.op("dve", lambda h, bk_=bk_, hp=hp: h.tensor_copy(out=agK[:, hp, :], in_=bk_[:]),
                     reads=(bkk,), writes=((ak_n, hp),))
            S.dma("sp", lambda h: h.dma_start(out=aginv[0:512, :].rearrange("(c p) t -> p c t", p=128), in_=agQ[:, :, :]),
                  reads=tuple((aq_n, hp) for hp in range(4)), writes=(("agin", "q"),))
            S.dma("sp", lambda h: h.dma_start(out=aginv[512:1024, :].rearrange("(c p) t -> p c t", p=128), in_=agK[:, :, :]),
                  reads=tuple((ak_n, hp) for hp in range(4)), writes=(("agin", "k"),))
            S.collective(lambda h: h.collective_compute("AllGather", ALU.bypass, replica_groups=groups,
                                                        ins=[aginA.ap().opt()], outs=[agoutA.ap().opt()]),
                         reads=(("agin", "q"), ("agin", "k")), writes=(("agoutA", 0),))
            for t4 in range(4):
                tsl = slice(1024 + t4 * 128, 1024 + (t4 + 1) * 128)
                for half in range(2):
                    vv_, vk2 = wide[4 + half]
                    b_, bkey = bank()
                    for kc in range(8):
                        mm(b_[:, 0:256], zT[:, kc, tsl], vv_[:, kc, :], kc == 0, kc == 7, (vk2, ("z", kc, 2)), bkey)
                    if half == 0:
                        S.op("dve", lambda h, b_=b_, t4=t4: h.tensor_copy(out=agV[:, t4, 0:256], in_=b_[:, 0:256]),
                             reads=(bkey,), writes=((agv_n, t4, 0),))
                    else:
                        S.op("act", lambda h, b_=b_, t4=t4: h.activation(out=agV[:, t4, 256:512], in_=b_[:, 0:256], func=AF.Copy),
                             reads=(bkey,), writes=((agv_n, t4, 1),))
            S.dma("sp", lambda h: h.dma_start(out=aginvB[0:512, :].rearrange("(c p) f -> p c f", p=128), in_=agV[:, :, :]),
                  reads=tuple((agv_n, t, hf) for t in range(4) for hf in range(2)), writes=(("agin", "v"),))
            for cc in range(2):
                bf_, bfk = bank()
                for kc in range(8):
                    mm(bf_[:], vf_[:, kc, cc * 128:(cc + 1) * 128], zT[:, kc, tok(2)], kc == 0, kc == 7,
                       (sfk, ("z", kc, 2)), bfk)
                S.op("act", lambda h, bf_=bf_, cc=cc: h.activation(out=fT[:, cc, tok(2)], in_=bf_[:], func=AF.Copy),
                     reads=(bfk,), writes=((f_n, cc, 2),))
            S.dma("sp", lambda h: h.dma_start(out=aginvB[512:768, :].rearrange("(c p) t -> p c t", p=128), in_=fT[:, :, tok(2)]),
                  reads=((f_n, 0, 2), (f_n, 1, 2)), writes=(("agin", "f"),))
            S.collective(lambda h: h.collective_compute("AllGather", ALU.bypass, replica_groups=groups,
                                                        ins=[aginB.ap().opt()], outs=[agoutB.ap().opt()]),
                         reads=(("agin", "v"), ("agin", "f")), writes=(("agoutB", 0),))
            R.free(aq_n), R.free(ak_n), R.free(agv_n)
            for hp in range(4):
                stt, sk = stage()
                vqk = stt[:, 0:2048].rearrange("p (j c n) -> p c j n", c=8, j=2)
                load_w(win3[:, :, 0, hp * 128:(hp + 1) * 128], vqk[:, :, 0, :], sk[0])
                load_w(win3[:, :, 1, hp * 128:(hp + 1) * 128], vqk[:, :, 1, :], sk[1])
                stv, svk = stage()
                vv = stv[:, 0:1024].rearrange("p (c n) -> p c n", c=8)
                load_w(win3[:, :, 2, hp * 128:(hp + 1) * 128], vv, svk)
                for blk in (0, 1):
                    bq, bqk = bank()
                    for kc in range(8):
                        mm(bq[:], vqk[:, kc, 0, :], zT[:, kc, tok(blk)], kc == 0, kc == 7, (sk, ("z", kc, blk)), bqk)
                    S.op("act", lambda h, bq=bq, blk=blk: h.activation(out=qT[:, 0, tok(blk)], in_=bq[:], func=AF.Copy,
                                                                       scale=0.125),
                         reads=(bqk,), writes=((q_n, blk),))
                    bk_, bkk = bank()
                    for kc in range(8):
                        mm(bk_[:], vqk[:, kc, 1, :], zT[:, kc, tok(blk)], kc == 0, kc == 7, (sk, ("z", kc, blk)), bkk)
                    S.op("dve", lambda h, bk_=bk_, blk=blk: h.tensor_copy(out=kT[:, 0, tok(blk)], in_=bk_[:]),
                         reads=(bkk,), writes=((k_n, blk),))
                if DBG == "A0":
                    continue
                if DBG == "A":
                    continue
                for tch in range(8):
                    tsl = slice(tch * 128, (tch + 1) * 128)
                    blk = tch // 4
                    b_, bkey = bank()
                    if tch < 8:
                        for kc in range(8):
                            mm(b_[:, 0:128], zT[:, kc, tsl], vqk[:, kc, 1, :], kc == 0, kc == 7, (sk, ("z", kc, blk)), bkey)
                    for kc in range(8):
                        mm(b_[:, 128:256], zT[:, kc, tsl], vv[:, kc, :], kc == 0, kc == 7, (svk, ("z", kc, blk)), bkey)
                    if tch < 8:
                        S.op("act", lambda h, b_=b_, tch=tch: h.activation(out=KVst[:, tch, :], in_=b_[:, 0:256],
                                                                           func=AF.Copy),
                             reads=(bkey,), writes=((kv_n, tch),))
                        S.op("dve", lambda h, b_=b_, tch=tch: h.tensor_copy(out=Vb[:, tch, :], in_=b_[:, 128:256]),
                             reads=(bkey,), writes=((vb_n, tch),))
                    else:
                        S.op("dve", lambda h, b_=b_, tch=tch, hp=hp: h.tensor_copy(
                            out=agV[:, tch - 8, hp * 128:(hp + 1) * 128], in_=b_[:, 128:256]),
                            reads=(bkey,), writes=((agv_n, tch - 8, hp),))
                if DBG == "B1":
                    continue
                S.dma("sp", lambda h, hp=hp: h.dma_start(out=nkv[:, :, hp * 128:(hp + 1) * 128], in_=KVst[:, :, 0:128]),
                      reads=tuple((kv_n, t) for t in range(8)), writes=(("nk", l, hp),), is_out=True)
                S.dma("sp", lambda h, hp=hp: h.dma_start(out=nvv[:, :, hp * 128:(hp + 1) * 128], in_=KVst[:, :, 128:256]),
                      reads=tuple((kv_n, t) for t in range(8)), writes=(("nv", l, hp),), is_out=True)
                if DBG == "B":
                    continue
                for b4 in range(4):
                    blk = b4 // 2
                    qs = slice(b4 * 256, (b4 + 1) * 256)
                    for hh in range(2):
                        ps_ = slice(hh * 64, (hh + 1) * 64)
                        bs_, bsk = bank()
                        for kc2 in range(2):
                            ks = slice(b4 * 256 + kc2 * 128, b4 * 256 + (kc2 + 1) * 128)
                            mm(bs_[:, kc2 * 256:(kc2 + 1) * 256], kT[ps_, 0, ks], qT[ps_, 0, qs], True, True,
                               ((k_n, blk), (q_n, blk)), bsk)
                        pi = pcount % 4
                        ri = pcount % 2
                        pcount += 1
                        S.op("act", lambda h, bs_=bs_, pi=pi: h.activation(out=pT[:, pi, 0:512], in_=bs_[:], func=AF.Exp),
                             reads=(bsk,), writes=((pT_n, pi),))
                        bo, bok = bank()
                        for kc2 in range(2):
                            mm(bo[:, 0:256], Vb[:, b4 * 2 + kc2, :], pT[:, pi, kc2 * 256:(kc2 + 1) * 256], kc2 == 0, kc2 == 1,
                               ((vb_n, b4 * 2 + kc2), (pT_n, pi)), bok)
                        for kc2 in range(2):
                            mm(bo[:, 256:512], onesb[:], pT[:, pi, kc2 * 256:(kc2 + 1) * 256], kc2 == 0, kc2 == 1,
                               (("onesb", 0), (pT_n, pi)), bok)
                        S.op("dve", lambda h, bo=bo, ps_=ps_, ri=ri: h.reciprocal(out=rden[ps_, ri, :], in_=bo[ps_, 256:512]),
                             reads=(bok,), writes=((rd_n, ri),))
                        S.op("dve", lambda h, bo=bo, ps_=ps_, hp=hp, qs=qs, ri=ri: h.tensor_tensor(
                            out=oaT[ps_, hp, qs], in0=bo[ps_, 0:256], in1=rden[ps_, ri, :], op=ALU.mult),
                            reads=(bok, (rd_n, ri)), writes=((oa_n, hp, b4, hh),))
            chk("hp")
            R.free(q_n), R.free(k_n), R.free(vb_n), R.free(kv_n)

            u_n, uT = R.alloc("uT", 2 * NTOK * 2, BF16, [2, NTOK])
            vn_n, vn = R.alloc("vn", 12 * 256 * 2, BF16, [12, 256])
            gv_n, gv = R.alloc("gv", 4 * 256 * 4, F32, [4, 256])
            ss_n, ss = R.alloc("ss", 16 * 4, F32, [1, 16])
            ab_n, AB = R.alloc("AB", 2 * 512 * 2, BF16, [2, 512])
            S.dma("sp", lambda h: h.dma_start(out=sgub[:].rearrange("p g q -> p (g q)"),
                                              in_=dr["sgu_bb"][:, l * 512:(l + 1) * 512]), writes=(("sgub", 0),))
            S.dma("sp", lambda h: h.dma_start(out=sgun[:, 0, :], in_=dr["sgu_nb"][:, l * 256:(l + 1) * 256]),
                  writes=(("sgun", 0),))
            S.dma("pool", lambda h: h.dma_start(out=sguw[:, 0].rearrange("p g q -> p (g q)"), in_=dr["sgu_wT"][l, :, :]),
                  writes=(("sguw", 0),))
            stu, suk = stage()
            stvs, svsk = stage()
            stf, sfk = stage()
            vf_ = stf[:, 0:2048].rearrange("p (c n) -> p c n", c=8)
            load_w(dr["w_in"][l, :, 2048:2304].rearrange("(c p) n -> p c n", p=128), vf_, sfk)
            vu_ = stu[:, 0:2048].rearrange("p (c n) -> p c n", c=8)
            vvs = stvs[:, 0:2048].rearrange("p (c n) -> p c n", c=8)
            load_w(dr["w_in"][l, :, 1536:1792].rearrange("(c p) n -> p c n", p=128), vu_, suk)
            load_w(dr["w_in"][l, :, 1792:2048].rearrange("(c p) n -> p c n", p=128), vvs, svsk)
            for blk in (2, 0, 1):
                for cc in range(2):
                    if blk == 2:
                        continue
                    bf_, bfk = bank()
                    for kc in range(8):
                        mm(bf_[:], vf_[:, kc, cc * 128:(cc + 1) * 128], zT[:, kc, tok(blk)], kc == 0, kc == 7,
                           (sfk, ("z", kc, blk)), bfk)
                    S.op("act", lambda h, bf_=bf_, cc=cc, blk=blk: h.activation(out=fT[:, cc, tok(blk)], in_=bf_[:], func=AF.Copy),
                         reads=(bfk,), writes=((f_n, cc, blk),))
                for cc in range(2):
                    bu_, buk = bank()
                    for kc in range(8):
                        mm(bu_[:], vu_[:, kc, cc * 128:(cc + 1) * 128], zT[:, kc, tok(blk)], kc == 0, kc == 7,
                           (suk, ("z", kc, blk)), buk)
                    gelu_from_psum(bu_[:], buk, uT[:, cc, tok(blk)], (u_n, cc, blk), 512)
                for t4 in range(4):
                    t12 = blk * 4 + t4
                    tsl = slice(t12 * 128, (t12 + 1) * 128)
                    bv_, bvk = bank()
                    for kc in range(8):
                        mm(bv_[:, 0:256], zT[:, kc, tsl], vvs[:, kc, :], kc == 0, kc == 7, (svsk, ("z", kc, blk)), bvk)
                    gelu_from_psum(bv_[:, 0:256], bvk, gv[:, t4, :], (gv_n, t4), 256)
                    sq, sqk = rotp("sq", sqp)
                    S.op("dve", lambda h, sq=sq, t12=t12, t4=t4: h.scalar_tensor_tensor(
                        out=sq[:, 0:256], in0=gv[:, t4, :], scalar=1.0, in1=gv[:, t4, :], op0=ALU.mult, op1=ALU.mult,
                        accum_out=ss[:, 0, t12:t12 + 1]),
                        reads=((gv_n, t4),), writes=(sqk, (ss_n, t12)))
                sl4 = slice(blk * 4, blk * 4 + 4)
                k4 = tuple((ss_n, blk * 4 + t4) for t4 in range(4))
                S.op("act", lambda h, sl4=sl4: h.activation(out=ss[:, 0, sl4], in_=ss[:, 0, sl4], func=AF.Sqrt,
                                                            bias=small[:, 0:1], scale=1.0 / 256),
                     reads=k4 + (("small", 0),), writes=k4)
                S.op("dve", lambda h, sl4=sl4: h.reciprocal(out=ss[:, 0, sl4], in_=ss[:, 0, sl4]), reads=k4, writes=k4)
                for t4 in range(4):
                    t12 = blk * 4 + t4
                    S.op("dve", lambda h, t12=t12, t4=t4: h.scalar_tensor_tensor(
                        out=vn[:, t12, :], in0=gv[:, t4, :], scalar=ss[:, 0, t12:t12 + 1], in1=sgun[:, 0, :],
                        op0=ALU.mult, op1=ALU.mult),
                        reads=((gv_n, t4), (ss_n, t12), ("sgun", 0)), writes=((vn_n, t12),))
            for blk in (0, 1):
                for b2 in range(2):
                    for t2 in range(2):
                        tl = slice(blk * 512 + b2 * 256 + t2 * 128, blk * 512 + b2 * 256 + (t2 + 1) * 128)
                        ba, bak = bank()
                        for fh in range(2):
                            mm(ba[:, fh * 128:(fh + 1) * 128], fT[:, fh, tl], fcc[:, 0, :], True, True,
                               ((f_n, fh, blk), ("fcc", 0)), bak)
                            mm(ba[:, 256 + fh * 128:256 + (fh + 1) * 128], fT[:, fh, tl], fcc[:, 1, :], True, True,
                               ((f_n, fh, blk), ("fcc", 0)), bak)
                        S.op("act", lambda h, ba=ba, t2=t2: h.activation(out=AB[:, t2, :], in_=ba[:], func=AF.Copy),
                             reads=(bak,), writes=((ab_n, t2),))
                    bo, bok = bank()
                    for fh in range(2):
                        n_ = 0
                        for t2 in range(2):
                            for cs_ in range(2):
                                mm(bo[:, fh * 256:(fh + 1) * 256], AB[:, t2, cs_ * 256 + fh * 128:cs_ * 256 + (fh + 1) * 128],
                                   ct256[:, cs_, t2, :], n_ == 0, n_ == 3, ((ab_n, t2), ("ct256", 0)), bok)
                                n_ += 1
                    for fh in range(2):
                        S.op("act", lambda h, bo=bo, fh=fh, blk=blk, b2=b2: h.activation(
                            out=fmT[:, fh, blk * 512 + b2 * 256:blk * 512 + (b2 + 1) * 256],
                            in_=bo[:, fh * 256:(fh + 1) * 256], func=AF.Copy),
                            reads=(bok,), writes=((fm_n, fh, blk, b2),))
            for blk in (2, 0, 1):
                for g in range(4):
                    pr, gh = g // 2, g % 2
                    rows = slice(gh * 64, (gh + 1) * 64)
                    bm_, bmk = bank()
                    for t4 in range(4):
                        mm(bm_[:, t4 * 128:(t4 + 1) * 128], vn[:, blk * 4 + t4, pr * 128:(pr + 1) * 128], sguw[:, 0, g, :], True, True,
                           ((vn_n, blk * 4 + t4), ("sguw", 0)), bmk)
                    sgt, sgk = rotp("sg", sgp)
                    for t4 in range(4):
                        S.op("dve", lambda h, bm_=bm_, sgt=sgt, t4=t4, g=g, rows=rows: h.tensor_tensor(
                            out=sgt[rows, t4 * 128:(t4 + 1) * 128], in0=bm_[rows, t4 * 128:(t4 + 1) * 128],
                            in1=sgub[rows, g, :], op=ALU.add),
                            reads=(bmk, ("sgub", 0)), writes=(sgk,))
                    S.op("dve", lambda h, sgt=sgt, rows=rows, pr=pr, blk=blk: h.tensor_tensor(
                        out=obT[rows, pr, tok(blk)], in0=sgt[rows, :], in1=uT[rows, pr, tok(blk)], op=ALU.mult),
                        reads=(sgk, (u_n, pr, blk)), writes=((ob_n, pr, blk, gh),))
            R.free(u_n), R.free(f_n), R.free(vn_n), R.free(gv_n), R.free(ss_n), R.free(ab_n)

            chk("sgu")
            qs_n, qTs = R.alloc("qTs", 2048 * 2, BF16, [1, 2048])
            ks_n, kTs = R.alloc("kTs", 2048 * 2, BF16, [1, 2048])
            vs_n, Vs = R.alloc("Vs", 16 * 128 * 2, BF16, [16, 128])
            kc_n, kcT = R.alloc("kcT", 256 * 2, BF16, [1, 256])
            vc_n, Vc = R.alloc("Vc", 2 * 128 * 2, BF16, [2, 128])
            bi_n, bias = R.alloc("bias", NSLOT * 128 * 4, F32, [NSLOT, 128])
            om_n, oTm = R.alloc("oTm", 2048 * 2, BF16, [1, 2048])
            sb_n, sbuf_s = R.alloc("sbs", 2 * 640 * 4, F32, [2, 640])
            ago3 = agoutA.ap().rearrange("(r x) t -> r x t", r=4)
            agoB3 = agoutB.ap().rearrange("(r x) t -> r x t", r=4)
            S.dma("sp", lambda h: h.dma_start(out=qTs[:, 0, :].rearrange("p (r t) -> p r t", r=4),
                                              in_=ago3[:, 0:512, :][:, bass.ds(dyn["off_hp"], 128), :].rearrange("r p t -> p r t")),
                  reads=(("agoutA", 0),), writes=((qs_n, 0),))
            S.dma("sp", lambda h: h.dma_start(out=kTs[:, 0, :].rearrange("p (r t) -> p r t", r=4),
                                              in_=ago3[:, 512:1024, :][:, bass.ds(dyn["off_hp"], 128), :].rearrange("r p t -> p r t")),
                  reads=(("agoutA", 0),), writes=((ks_n, 0),))
            for r in range(4):
                S.dma("sp", lambda h, r=r: h.dma_start(
                    out=Vs[:, r * 4:(r + 1) * 4, :],
                    in_=agoB3[r, 0:512, :].rearrange("(c p) f -> p c f", p=128)[:, :, bass.ds(dyn["off_hp"], 128)]),
                    reads=(("agoutB", 0),), writes=((vs_n, r),))
            S.dma("pool", lambda h: h.dma_start(out=kcT[:, 0, :], in_=dr["kcT"][l, :, :]), writes=((kc_n, 0),))
            S.dma("pool", lambda h: h.dma_start(out=Vc[:, :, :], in_=dr["vc"][l].rearrange("(c p) f -> p c f", p=128)),
                  writes=((vc_n, 0),))
            def s_part(hh, n):
                nonlocal pcount
                rows = slice(hh * 64, (hh + 1) * 64)
                if n <= 1:
                    ms, slot0 = [0, 1, 2, 3], 5 + 4 * n
                elif n >= 14:
                    ms, slot0 = [12, 13, 14, 15], 13 + 4 * (n - 14)
                else:
                    ms, slot0 = list(range(n - 2, n + 3)), 0
                nw = len(ms)
                qsl = slice(n * 128, (n + 1) * 128)
                bA, bAk = bank()
                for i_, m in enumerate(ms[:4]):
                    mm(bA[:, i_ * 128:(i_ + 1) * 128], kTs[rows, 0, m * 128:(m + 1) * 128], qTs[rows, 0, qsl], True, True,
                       ((ks_n, 0), (qs_n, 0)), bAk)
                bB, bBk = bank()
                nb = 0
                if nw == 5:
                    m = ms[4]
                    mm(bB[:, 0:128], kTs[rows, 0, m * 128:(m + 1) * 128], qTs[rows, 0, qsl], True, True,
                       ((ks_n, 0), (qs_n, 0)), bBk)
                    nb = 1
                for c2 in range(2):
                    mm(bB[:, (nb + c2) * 128:(nb + c2 + 1) * 128], kcT[rows, 0, c2 * 128:(c2 + 1) * 128], qTs[rows, 0, qsl],
                       True, True, ((kc_n, 0), (qs_n, 0)), bBk)
                pi = pcount % 4
                ri = pcount % 2
                pcount += 1
                S.op("dve", lambda h: h.tensor_tensor(
                    out=sbuf_s[:, ri, 0:512], in0=bA[:, 0:512], in1=bias[:, slot0:slot0 + 4, :].rearrange("p s q -> p (s q)"),
                    op=ALU.add), reads=(bAk, (bi_n, 0)), writes=((sb_n, ri, 0),))
                if nw == 5:
                    S.op("dve", lambda h: h.tensor_tensor(
                        out=sbuf_s[:, ri, 512:640], in0=bB[:, 0:128], in1=bias[:, slot0 + 4, :], op=ALU.add),
                        reads=(bBk, (bi_n, 0)), writes=((sb_n, ri, 1),))
                nwc = nw * 128
                S.op("act", lambda h: h.activation(out=pT[:, pi, 0:nwc], in_=sbuf_s[:, ri, 0:nwc], func=AF.Exp),
                     reads=((sb_n, ri, 0), (sb_n, ri, 1)), writes=((pT_n, pi),))
                S.op("act", lambda h: h.activation(
                    out=pT[:, pi, nwc:nwc + 256], in_=bB[:, nb * 128:(nb + 2) * 128], func=AF.Exp),
                    reads=(bBk,), writes=((pT_n, pi, "c"),))
                return dict(hh=hh, n=n, rows=rows, ms=ms, nw=nw, qsl=qsl, pi=pi, ri=ri)

            def pv_part(st_):
                hh, n, rows, ms, nw, qsl, pi, ri = (st_[k_] for k_ in ("hh", "n", "rows", "ms", "nw", "qsl", "pi", "ri"))
                bo, bok = bank()
                ntile = nw + 2
                for i_ in range(ntile):
                    lhs = Vs[:, ms[i_], :] if i_ < nw else Vc[:, i_ - nw, :]
                    mm(bo[:, 0:128], lhs, pT[:, pi, i_ * 128:(i_ + 1) * 128], i_ == 0, i_ == ntile - 1,
                       ((vs_n, 0), (vs_n, 1), (vs_n, 2), (vs_n, 3), (vc_n, 0), (pT_n, pi), (pT_n, pi, "c")), bok)
                for i_ in range(ntile):
                    mm(bo[:, 128:256], onesb[:], pT[:, pi, i_ * 128:(i_ + 1) * 128], i_ == 0, i_ == ntile - 1,
                       (("onesb", 0), (pT_n, pi), (pT_n, pi, "c")), bok)
                S.op("dve", lambda h: h.reciprocal(out=rden[rows, ri, 0:128], in_=bo[rows, 128:256]),
                     reads=(bok,), writes=((rd_n, ri),))
                S.op("dve", lambda h: h.tensor_tensor(
                    out=oTm[rows, 0, qsl], in0=bo[rows, 0:128], in1=rden[rows, ri, 0:128], op=ALU.mult),
                    reads=(bok, (rd_n, ri)), writes=((om_n, n, hh),))

            prev_st = None
            for hh in range(2):
                S.dma("sp", lambda h, hh=hh: h.dma_start(out=bias[:].rearrange("p s q -> p (s q)"), in_=dr["bsmp"][l, hh, :, :]),
                      writes=((bi_n, 0),))
                for n in range(16):
                    st_ = s_part(hh, n)
                    if prev_st is not None:
                        pv_part(prev_st)
                    prev_st = st_
            pv_part(prev_st)
            S.dma("sp", lambda h: h.dma_start(out=ag2in.ap()[:, :], in_=oTm[:, 0, :]),
                  reads=tuple((om_n, n, hh) for n in range(16) for hh in range(2)), writes=(("ag2in", 0),))
            S.collective(lambda h: h.collective_compute("AllGather", ALU.bypass, replica_groups=groups,
                                                        ins=[ag2in.ap().opt()], outs=[ag2out.ap().opt()]),
                         reads=(("ag2in", 0),), writes=(("ag2out", 0),))
            R.free(qs_n), R.free(ks_n), R.free(vs_n), R.free(kc_n), R.free(vc_n), R.free(bi_n), R.free(om_n), R.free(sb_n)
            R.free(pT_n), R.free(rd_n)

            chk("sattn")
            fa_n, fTall = R.alloc("fTall", 2 * 2048 * 2, BF16, [2, 2048])
            for fh in range(2):
                S.dma("sp", lambda h, fh=fh: h.dma_start(
                    out=fTall[:, fh, :].rearrange("p (r t) -> p r t", r=4),
                    in_=agoB3[:, 512 + fh * 128:512 + (fh + 1) * 128, :].rearrange("r p t -> p r t")),
                    reads=(("agoutB", 0),), writes=((fa_n, fh),))
            abs_n, ABs = R.alloc("ABs", 16 * 512 * 2, BF16, [16, 512])
            for t16 in range(16):
                tl = slice(t16 * 128, (t16 + 1) * 128)
                ba, bak = bank()
                for fh in range(2):
                    mm(ba[:, fh * 128:(fh + 1) * 128], fTall[:, fh, tl], fcc[:, 2, :], True, True, ((fa_n, fh), ("fcc", 0)), bak)
                    mm(ba[:, 256 + fh * 128:256 + (fh + 1) * 128], fTall[:, fh, tl], fcc[:, 3, :], True, True,
                       ((fa_n, fh), ("fcc", 0)), bak)
                if t16 % 2 == 0:
                    S.op("act", lambda h, ba=ba, t16=t16: h.activation(out=ABs[:, t16, :], in_=ba[:], func=AF.Copy),
                         reads=(bak,), writes=((abs_n, t16),))
                else:
                    S.op("dve", lambda h, ba=ba, t16=t16: h.tensor_copy(out=ABs[:, t16, :], in_=ba[:]),
                         reads=(bak,), writes=((abs_n, t16),))
            bo0, bo0k = bank()
            bo1, bo1k = bank()
            bos = ((bo0, bo0k), (bo1, bo1k))
            ctmv = dr["ctm"].rearrange("s (c p) t -> s p c t", p=128)
            for piece in range(4):
                stc, sck = stage()
                sts, ssk = stage()
                vc_ = stc[:, 0:2048].rearrange("p (c t) -> p c t", c=4)
                vs_ = sts[:, 0:2048].rearrange("p (c t) -> p c t", c=4)
                S.dma("sp", lambda h, piece=piece, vc_=vc_: h.dma_start(out=vc_, in_=ctmv[0, :, piece * 4:(piece + 1) * 4, :]),
                      writes=(sck,))
                S.dma("sp", lambda h, piece=piece, vs_=vs_: h.dma_start(out=vs_, in_=ctmv[1, :, piece * 4:(piece + 1) * 4, :]),
                      writes=(ssk,))
                for t4 in range(4):
                    t16 = piece * 4 + t4
                    for fh in range(2):
                        bo, bok = bos[fh]
                        mm(bo[:], ABs[:, t16, fh * 128:(fh + 1) * 128], vc_[:, t4, :], t16 == 0, False, ((abs_n, t16), sck), bok)
                        mm(bo[:], ABs[:, t16, 256 + fh * 128:256 + (fh + 1) * 128], vs_[:, t4, :], False, t16 == 15,
                           ((abs_n, t16), ssk), bok)
            for fh in range(2):
                bo, bok = bos[fh]
                S.op("act", lambda h, bo=bo, fh=fh: h.activation(out=fmT[:, fh, tok(2)], in_=bo[:], func=AF.Copy),
                     reads=(bok,), writes=((fm_n, fh, 2, 0),))
            S.dma("sp", lambda h: h.dma_start(
                out=oaT[:, :, tok(2)],
                in_=ag2out.ap().rearrange("(c p) t -> p c t", p=128)[:, :, bass.ds(dyn["off_j"], 512)]),
                reads=(("ag2out", 0),), writes=tuple((oa_n, hp, "s") for hp in range(4)))
            R.free(abs_n), R.free(fa_n)

            chk("sfnet")
            mx_n, mixedT = R.alloc("mixedT", 8 * NTOK * 2, BF16, [8, NTOK])
            ac_n, acc = R.alloc("acc", 512 * 4, F32, [1, 512])
            oa_keys = lambda blk: tuple((oa_n, hp, b4, hh) for hp in range(4) for b4 in (2 * blk, 2 * blk + 1) for hh in range(2)) \
                if blk < 2 else tuple((oa_n, hp, "s") for hp in range(4))
            ob_keys = lambda blk: tuple((ob_n, pr, blk, gh) for pr in range(2) for gh in range(2))
            fm_keys = lambda blk: tuple((fm_n, fh, blk, b2) for fh in range(2) for b2 in range(2)) if blk < 2 \
                else tuple((fm_n, fh, 2, 0) for fh in range(2))
            def merge_loads(dcp):
                gst = []
                for br in range(3):
                    s_, k_ = stage()
                    v_ = s_[:, 0:2048].rearrange("p (c n) -> p c n", c=8)
                    c0 = 2304 + br * 1024 + dcp * 256
                    load_w(dr["w_in"][l, :, c0:c0 + 256].rearrange("(c p) n -> p c n", p=128), v_, k_)
                    gst.append((v_, k_))
                sp_, pk = stage()
                vp = sp_[:, 0:2048].rearrange("p (c n) -> p c n", c=8)
                load_w(dr["p_all"][l, :, dcp * 256:(dcp + 1) * 256].rearrange("(c p) n -> p c n", p=128), vp, pk)
                return gst, vp, pk

            nxt = merge_loads(0)
            for step in range(8):
                dcp = step % 4
                gst, vp, pk = nxt
                if step + 1 < 8:
                    nxt = merge_loads((step + 1) % 4)
                for blk in ((0, 1) if step < 4 else (2,)):
                    for cc in range(2):
                        dc = dcp * 2 + cc
                        for br in range(3):
                            vg_, gk_ = gst[br]
                            bg, bgk = bank()
                            for kc in range(8):
                                mm(bg[:], vg_[:, kc, cc * 128:(cc + 1) * 128], zT[:, kc, tok(blk)], kc == 0, kc == 7,
                                   (gk_, ("z", kc, blk)), bgk)
                            by, byk = bank()
                            if br == 0:
                                for c4 in range(4):
                                    mm(by[:], vp[:, c4, cc * 128:(cc + 1) * 128], oaT[:, c4, tok(blk)], c4 == 0, c4 == 3,
                                       (pk,) + oa_keys(blk), byk)
                            elif br == 1:
                                for c2 in range(2):
                                    mm(by[:], vp[:, 4 + c2, cc * 128:(cc + 1) * 128], obT[:, c2, tok(blk)], c2 == 0, c2 == 1,
                                       (pk,) + ob_keys(blk), byk)
                            else:
                                for c2 in range(2):
                                    mm(by[:], vp[:, 6 + c2, cc * 128:(cc + 1) * 128], fmT[:, c2, tok(blk)], c2 == 0, c2 == 1,
                                       (pk,) + fm_keys(blk), byk)
                            sgt, sgk = rotp("sg", sgp)
                            S.op("act", lambda h, sgt=sgt, bg=bg, br=br, dc=dc: h.activation(
                                out=sgt[:], in_=bg[:], func=AF.Sigmoid, bias=bgate(l, br, dc), scale=1.0),
                                reads=(bgk, ("vecs", 0)), writes=(sgk,))
                            if br == 0:
                                S.op("dve", lambda h, sgt=sgt, by=by: h.tensor_tensor(out=acc[:, 0, :], in0=sgt[:], in1=by[:],
                                                                                      op=ALU.mult),
                                     reads=(sgk, byk), writes=((ac_n, 0),))
                            elif br == 1:
                                S.op("dve", lambda h, sgt=sgt, by=by: h.tensor_tensor(out=sgt[:], in0=sgt[:], in1=by[:],
                                                                                      op=ALU.mult),
                                     reads=(sgk, byk), writes=(sgk,))
                                S.op("dve", lambda h, sgt=sgt: h.tensor_tensor(out=acc[:, 0, :], in0=acc[:, 0, :], in1=sgt[:],
                                                                               op=ALU.add),
                                     reads=(sgk, (ac_n, 0)), writes=((ac_n, 0),))
                            else:
                                S.op("dve", lambda h, sgt=sgt, by=by: h.tensor_tensor(out=sgt[:], in0=sgt[:], in1=by[:],
                                                                                      op=ALU.mult),
                                     reads=(sgk, byk), writes=(sgk,))
                                S.op("dve", lambda h, sgt=sgt, dc=dc, blk=blk: h.tensor_tensor(
                                    out=mixedT[:, dc, tok(blk)], in0=acc[:, 0, :], in1=sgt[:], op=ALU.add),
                                    reads=(sgk, (ac_n, 0)), writes=((mx_n, dc, blk),))
            vos = []
            for dcp in range(4):
                so_, ok_ = stage()
                vo = so_[:, 0:2048].rearrange("p (c n) -> p c n", c=8)
                load_w(dr["w_out"][l, :, dcp * 256:(dcp + 1) * 256].rearrange("(c p) n -> p c n", p=128), vo, ok_)
                vos.append((vo, ok_))
            for blk in range(NBLK):
                cond = 0 if blk < 2 else 1
                for dc in range(8):
                    vo, ok_ = vos[dc // 2]
                    cc = dc % 2
                    bd, bdk = bank()
                    for kc in range(8):
                        mm(bd[:], vo[:, kc, cc * 128:(cc + 1) * 128], mixedT[:, kc, tok(blk)], kc == 0, kc == 7,
                           (ok_, (mx_n, kc, blk)), bdk)
                    S.op("dve", lambda h, bd=bd, dc=dc, blk=blk, cond=cond: h.scalar_tensor_tensor(
                        out=hT[:, dc, tok(blk)], in0=bd[:], scalar=GV[:, cond, 1, dc:dc + 1],
                        in1=hT[:, dc, tok(blk)], op0=ALU.mult, op1=ALU.add),
                        reads=(bdk, ("GV", par, cond, 1), ("h", dc, blk)), writes=(("h", dc, blk),))
                    if post_blk is not None:
                        post_blk.dc(blk, dc)
                if post_blk is not None:
                    post_blk(blk)
            if post_blk is not None:
                post_blk(NBLK)
            R.free(mx_n), R.free(ac_n), R.free(oa_n), R.free(ob_n), R.free(fm_n)

        S.op("dve", lambda h: h.memset(small[:], RMS_EPS), writes=(("small", 0),))
        try:
            g0 = mod_gen(0, early=True)
            for mark in g0:
                if mark == "i0":
                    break
            for l in range(nlayers):
                chk("mod")
                ffn(l, 0, bgen=(g0 if l == 0 else None), post_blk=make_post(lambda b_, l=l: normC(l, 1, b_)), skip_norm=(l > 0))
                chk("ffn1")
                mixers(l, post_blk=make_post(lambda b_, l=l: normC(l, 2, b_)))
                chk("mix")
                if l + 1 < nlayers:
                    ffn(l, 1, bgen=mod_gen(l + 1), post_blk=make_post(lambda b_, l=l: normC(l + 1, 0, b_)), skip_norm=True)
                else:
                    ffn(l, 1, post_blk=make_post(finalC), skip_norm=True)
        except StopBuild:
            pass
        if stop is not None:
            for blk in range(NBLK):
                final_norm(blk)

        with nc.Block() as block:
            @block.tensor
            def _(h):
                for f in S.q["pe"]:
                    f(h)

            @block.scalar
            def _(h):
                for f in S.q["act"]:
                    f(h)

            @block.vector
            def _(h):
                for f in S.q["dve"]:
                    f(h)

            @block.gpsimd
            def _(h):
                for f in S.q["pool"]:
                    f(h)

            @block.sync
            def _(h):
                with ExitStack() as rs:
                    dyn["stack"] = rs
                    for f in S.q["sp"]:
                        f(h)
                    done = {}
                    for s, v, e in S.out_tokens:
                        if id(s) not in done or done[id(s)][1] < v:
                            done[id(s)] = (s, v)
                    for s, v in done.values():
                        h.wait_ge(s, v)
    return nc


def _bias_tables(rpb_l, heads):
    cols = np.arange(GRID_W)
    col_start = np.clip(cols - 8, 0, GRID_W - 16)
    colok = (cols[None, :] >= col_start[:, None]) & (cols[None, :] < col_start[:, None] + 16)
    dcol = np.clip(cols[None, :] - cols[:, None], -15, 15) + 15
    slots = [(5, 3 + d) for d in range(5)]
    slots += [(0, m) for m in range(4)] + [(1, m) for m in range(4)]
    slots += [(14, m) for m in range(12, 16)] + [(15, m) for m in range(12, 16)]
    out = np.full((len(heads), 128, NSLOT, 128), -1e30, np.float32)
    for hi, hd in enumerate(heads):
        for si, (n, m) in enumerate(slots):
            for qr2 in range(2):
                r = 2 * n + qr2
                r0 = min(max(r - 4, 0), ROWS - 8)
                for kr2 in range(2):
                    rk = 2 * m + kr2
                    if not (r0 <= rk < r0 + 8):
                        continue
                    drow = rk - r + 7
                    blk = np.where(colok, rpb_l[hd, drow][dcol], np.float32(-1e30))
                    out[hi, kr2 * 64:(kr2 + 1) * 64, si, qr2 * 64:(qr2 + 1) * 64] = blk.T
    return out.reshape(len(heads), 128, NSLOT * 128)


def _consts():
    bf = ml_dtypes.bfloat16
    c = np.arange(64)
    ang = 2 * np.pi * np.outer(c, c) / 64.0
    fcc = np.zeros((128, 4, 128), np.float32)
    for si, T in enumerate((256, 2048)):
        sc = 1.0 / math.sqrt(T * 64.0)
        for g in range(2):
            fcc[g * 64:(g + 1) * 64, si * 2 + 0, g * 64:(g + 1) * 64] = np.cos(ang) * sc
            fcc[g * 64:(g + 1) * 64, si * 2 + 1, g * 64:(g + 1) * 64] = np.sin(ang) * sc
    t = np.arange(256)
    a = 2 * np.pi * (np.outer(t, t) % 256) / 256.0
    ct = np.stack([np.cos(a), -np.sin(a)], 0).astype(np.float32)
    ct256 = ct.reshape(2, 2, 128, 256).transpose(2, 0, 1, 3).reshape(128, 1024)
    t2 = np.arange(2048, dtype=np.int64)
    a2 = 2 * np.pi * (np.outer(t2, t2) % 2048) / 2048.0
    ctm = np.stack([np.cos(a2), -np.sin(a2)], 0).astype(bf)
    return fcc.reshape(128, 512), np.ascontiguousarray(ct256), ctm


_NC_CACHE = {}


def kernel(x_prompt, x_sample, cache_k, cache_v, c, c_ctx, w_mod, b_mod, g_norm,
           ffn_w_gate, ffn_w_up, ffn_w_down, w_in, b_gate, rpb, sgu_norm, sgu_w, sgu_b,
           p_attn, p_sgu, p_fnet, w_out, g_final):
    f32 = np.float32
    A = lambda a: np.ascontiguousarray(np.asarray(a, dtype=f32))
    x_prompt, x_sample, cache_k, cache_v = A(x_prompt), A(x_sample), A(cache_k), A(cache_v)
    c, c_ctx, b_mod, g_norm, b_gate, rpb = A(c), A(c_ctx), A(b_mod), A(g_norm), A(b_gate), A(rpb)
    sgu_norm, sgu_w, sgu_b, g_final = A(sgu_norm), A(sgu_w), A(sgu_b), A(g_final)
    w_mod, wg, wu, wd, w_in, w_out = A(w_mod), A(ffn_w_gate), A(ffn_w_up), A(ffn_w_down), A(w_in), A(w_out)
    p_all = np.ascontiguousarray(np.concatenate([A(p_attn), A(p_sgu), A(p_fnet)], axis=1))

    def pv(v):
        lead = v.shape[:-1]
        return np.moveaxis(v.reshape(lead + (8, 128)), -1, 0)

    vecs = np.zeros((128, 512), f32)
    vecs[:, 0:96] = pv(g_norm).reshape(128, 96)
    vecs[:, 96:96 + 288] = np.moveaxis(b_mod.reshape(NL, 72, 128), -1, 0).reshape(128, 288)
    vecs[:, 384:384 + 96] = pv(b_gate).reshape(128, 96)
    vecs[:, 480:488] = pv(g_final).reshape(128, 8)
    sgu_wT = np.ascontiguousarray(sgu_w.transpose(0, 3, 1, 2).reshape(NL, 128, 512))
    sgu_nb = np.ascontiguousarray(np.broadcast_to(sgu_norm.reshape(1, NL * 256), (128, NL * 256)))
    sgu_bb = np.ascontiguousarray(np.broadcast_to(sgu_b.reshape(1, NL * 512), (128, NL * 512)))
    fcc, ct256, ctm_full = _consts()

    in_maps = []
    for core in range(8):
        sq, j = core // 4, core % 4
        xs = np.concatenate([x_prompt[core * 4:(core + 1) * 4].reshape(1024, D),
                             x_sample[sq, j * 512:(j + 1) * 512]], axis=0)
        condT = np.stack([pv(c_ctx), pv(c[sq])], axis=-1).reshape(128, 16)
        hs = slice(2 * j, 2 * j + 2)
        kcT = np.ascontiguousarray(cache_k[sq, :, :, hs, :].reshape(NL, 256, 128).transpose(0, 2, 1))
        vcm = np.ascontiguousarray(cache_v[sq, :, :, hs, :].reshape(NL, 256, 128))
        bsmp = np.stack([_bias_tables(rpb[l], [2 * j, 2 * j + 1]) for l in range(NL)], 0)
        in_maps.append({
            "xT": np.ascontiguousarray(xs.T), "vecs": vecs, "condT": np.ascontiguousarray(condT),
            "w_mod": w_mod, "wg": wg, "wu": wu, "wd": wd, "w_in": w_in, "p_all": p_all, "w_out": w_out,
            "sgu_wT": sgu_wT, "sgu_nb": sgu_nb, "sgu_bb": sgu_bb, "fcc": fcc, "ct256": ct256,
            "kcT": kcT, "vc": vcm, "bsmp": np.ascontiguousarray(bsmp),
            "ctm": np.ascontiguousarray(ctm_full[:, :, j * 512:(j + 1) * 512]),
            "info": np.array([[j * 128, j * 512, 0, 0]], dtype=np.int32),
        })
    if "nc" not in _NC_CACHE:
        _NC_CACHE["nc"] = build_program()
    nld = _NC_CACHE.get("nld", NL)
    if nld != NL:
        for m in in_maps:
            for k_ in ("w_mod", "wg", "wu", "wd", "w_in", "p_all", "w_out", "sgu_wT", "kcT", "vc", "bsmp"):
                m[k_] = np.ascontiguousarray(m[k_][:nld])
    res = run_bass_kernel_spmd(_NC_CACHE["nc"], in_maps, core_ids=list(range(8)))
    y_prompt = np.zeros((32, 256, D), f32)
    y_sample = np.zeros((2, 2048, D), f32)
    nk = np.zeros((32, NL, 256, 8, 64), f32)
    nv = np.zeros((32, NL, 256, 8, 64), f32)
    for core in range(8):
        r = res.results[core]
        sq, j = core // 4, core % 4
        y = np.asarray(r["yT"]).T
        y_prompt[core * 4:(core + 1) * 4] = y[0:1024].reshape(4, 256, D)
        y_sample[sq, j * 512:(j + 1) * 512] = y[1024:1536]
        nk[core * 4:(core + 1) * 4] = np.asarray(r["nk"]).reshape(NL, 4, 256, 8, 64).transpose(1, 0, 2, 3, 4)
        nv[core * 4:(core + 1) * 4] = np.asarray(r["nv"]).reshape(NL, 4, 256, 8, 64).transpose(1, 0, 2, 3, 4)
    return (y_prompt, y_sample, nk, nv)
```

```python
import math
import os
from contextlib import ExitStack

import numpy as np
import ml_dtypes

import concourse.bass as bass
import concourse.mybir as mybir
from concourse.bass_utils import run_bass_kernel_spmd

F32 = mybir.dt.float32
BF16 = mybir.dt.bfloat16
I32 = mybir.dt.int32
AF = mybir.ActivationFunctionType
ALU = mybir.AluOpType

D = 1024
NL = 4
DFF = 2816
NCH_FF = DFF // 128
INC = 5376
NTOK = 1536
NBLK = 3
GRID_W = 64
ROWS = 32
RMS_EPS = 1e-6
NSLOT = 21
AGROWS = 1792
NDS = 12


class Sched:
    def __init__(self, nc, stack):
        self.nc = nc
        self.q = {e: [] for e in ("pe", "act", "dve", "pool", "sp")}
        self.sem = {e: stack.enter_context(nc.semaphore("s_" + e)) for e in ("pe", "act", "dve", "pool")}
        self.cnt = {e: 0 for e in self.sem}
        self.dsem = {qn: [stack.enter_context(nc.semaphore("d_%s%d" % (qn, i))) for i in range(NDS)]
                     for qn in ("sp", "pool")}
        self.dn = {"sp": 0, "pool": 0}
        self.known = {e: {} for e in self.q}
        self.lw = {}
        self.rd = {}
        self.init_tok = {}
        self.out_tokens = []
        self.stack = stack
        self.nops = 0

    @staticmethod
    def flat(keys):
        out = []
        for k in keys:
            if isinstance(k[0], tuple):
                out.extend(Sched.flat(k))
            else:
                out.append(k)
        return tuple(out)

    def _deps(self, eng, reads, writes):
        need = {}

        def add(tok):
            s, v, pe = tok
            if pe == "pe" and eng == "pe":
                return
            k = id(s)
            if k not in need or need[k][1] < v:
                need[k] = (s, v)

        for k in reads:
            t = self.lw.get(k)
            if t is not None:
                add(t)
            elif k[0] in self.init_tok:
                for t2 in self.init_tok[k[0]]:
                    add(t2)
        for k in writes:
            t = self.lw.get(k)
            if t is not None:
                add(t)
            elif k[0] in self.init_tok:
                for t2 in self.init_tok[k[0]]:
                    add(t2)
            for t2 in self.rd.get(k, {}).values():
                add(t2)
        waits = []
        kn = self.known[eng]
        for k, (s, v) in need.items():
            if kn.get(k, 0) < v:
                kn[k] = v
                waits.append((s, v))
        return waits

    def _update(self, tok, reads, writes):
        for k in writes:
            self.lw[k] = tok
            self.rd[k] = {}
        for k in reads:
            self.rd.setdefault(k, {})[id(tok[0])] = tok

    def op(self, eng, fn, reads=(), writes=()):
        reads, writes = self.flat(reads), self.flat(writes)
        if eng != "pe":
            extra = tuple(k for k in reads if k[0] == "ps" and k not in writes)
            writes = writes + extra
        waits = self._deps(eng, reads, writes)
        self.cnt[eng] += 1
        sem = self.sem[eng]
        tok = (sem, self.cnt[eng], eng)

        def emit(h, waits=waits, fn=fn, sem=sem):
            for s, v in waits:
                h.wait_ge(s, v)
            fn(h).then_inc(sem, 1)

        self.q[eng].append(emit)
        self._update(tok, reads, writes)
        self.nops += 1
        return tok

    def dma(self, qn, fn, reads=(), writes=(), is_out=False):
        reads, writes = self.flat(reads), self.flat(writes)
        n = self.dn[qn]
        self.dn[qn] += 1
        i, k = n % NDS, n // NDS
        s = self.dsem[qn][i]
        waits = self._deps(qn, reads, writes)
        if k > 0 and self.known[qn].get(id(s), 0) < 16 * k:
            self.known[qn][id(s)] = 16 * k
            waits.append((s, 16 * k))
        tok = (s, 16 * (k + 1), qn)

        def emit(h, waits=waits, fn=fn, s=s):
            for s2, v in waits:
                h.wait_ge(s2, v)
            fn(h).then_inc(s, 16)

        self.q[qn].append(emit)
        self._update(tok, reads, writes)
        if is_out:
            self.out_tokens.append(tok)
        return tok

    def collective(self, fn, reads, writes):
        reads, writes = self.flat(reads), self.flat(writes)
        s = self.stack.enter_context(self.nc.semaphore("cc%d" % len(self.stack_cc)))
        self.stack_cc.append(s)
        waits = self._deps("pool", reads, writes)
        tok = (s, 1, "pool")

        def emit(h, waits=waits, fn=fn, s=s):
            for s2, v in waits:
                h.wait_ge(s2, v)
            fn(h).then_inc(s)

        self.q["pool"].append(emit)
        self._update(tok, reads, writes)
        return tok

    stack_cc = []

    def raw(self, eng, fn, reads=()):
        waits = self._deps(eng, reads, ())

        def emit(h, waits=waits, fn=fn):
            for s, v in waits:
                h.wait_ge(s, v)
            fn(h)

        self.q[eng].append(emit)

    def tokens_of(self, name):
        toks = []
        for k, t in self.lw.items():
            if k[0] == name:
                toks.append(t)
        for k, d in self.rd.items():
            if k[0] == name:
                toks.extend(d.values())
        best = {}
        for s, v, e in toks:
            if id(s) not in best or best[id(s)][1] < v:
                best[id(s)] = (s, v, e)
        return list(best.values())

    def forget(self, name):
        for dct in (self.lw, self.rd):
            for k in [k for k in dct if k[0] == name]:
                del dct[k]
        self.init_tok.pop(name, None)


class Region:
    def __init__(self, S, tensor, nbytes):
        self.S = S
        self.t = tensor
        self.n = nbytes
        self.live = {}
        self.freed = []
        self.gen = 0

    def alloc(self, base, nbytes, dtype, shape):
        nbytes = (nbytes + 63) // 64 * 64
        spans = sorted(self.live.values())
        lo = 0
        for a, b in spans:
            if a - lo >= nbytes:
                break
            lo = max(lo, b)
        assert lo + nbytes <= self.n, "region overflow allocating %s (%d bytes)" % (base, nbytes)
        hi = lo + nbytes
        self.gen += 1
        name = "%s#%d" % (base, self.gen)
        self.live[name] = (lo, hi)
        toks = []
        keep = []
        for a, b, tk in self.freed:
            if a < hi and lo < b:
                toks.extend(tk)
            keep.append((a, b, tk))
        self.S.init_tok[name] = toks
        ap = self.t[:, lo // 2:hi // 2]
        if dtype == F32:
            ap = ap.bitcast(F32)
        nel = 1
        for s_ in shape:
            nel *= s_
        esz = 4 if dtype == F32 else 2
        ap = ap[:, 0:nel]
        assert nel * esz <= nbytes
        if len(shape) == 2:
            ap = ap.rearrange("p (a b) -> p a b", a=shape[0])
        elif len(shape) == 3:
            ap = ap.rearrange("p (a b c) -> p a b c", a=shape[0], b=shape[1])
        return name, ap

    def free(self, name):
        lo, hi = self.live.pop(name)
        toks = self.S.tokens_of(name)
        self.freed = [(a, b, tk) for (a, b, tk) in self.freed] + [(lo, hi, toks)]
        self.S.forget(name)
        if len(self.freed) > 64:
            self.freed = self.freed[-64:]


class StopBuild(Exception):
    pass


def build_program(nlayers=NL, stop=None):
    nc = bass.Bass("TRN2", target_bir_lowering=False)
    dr = {}
    NLd = nlayers
    DBG = os.environ.get("KDBG", "")

    def din(name, shape, dt=F32):
        dr[name] = nc.dram_tensor(name, list(shape), dt, kind="ExternalInput").ap()

    def dout(name, shape, dt=F32):
        dr[name] = nc.dram_tensor(name, list(shape), dt, kind="ExternalOutput").ap()

    din("xT", [D, NTOK])
    din("vecs", [128, 512])
    din("condT", [128, 16])
    din("w_mod", [NLd, D, 9 * D])
    din("wg", [NLd, 2, D, DFF])
    din("wu", [NLd, 2, D, DFF])
    din("wd", [NLd, 2, DFF, D])
    din("w_in", [NLd, D, INC])
    din("p_all", [NLd, D, D])
    din("w_out", [NLd, D, D])
    din("sgu_wT", [NLd, 128, 4 * 128])
    din("sgu_nb", [128, NL * 256])
    din("sgu_bb", [128, NL * 4 * 128])
    din("fcc", [128, 4 * 128])
    din("ct256", [128, 2 * 2 * 256])
    din("kcT", [NLd, 128, 256])
    din("vc", [NLd, 256, 128])
    din("bsmp", [NLd, 2, 128, NSLOT * 128])
    din("ctm", [2, 2048, 512], BF16)
    din("info", [1, 4], I32)
    dout("yT", [D, NTOK])
    dout("nk", [NL, 1024, 512])
    dout("nv", [NL, 1024, 512])
    aginA = nc.dram_tensor("aginA", [1024, 512], BF16)
    agoutA = nc.dram_tensor("agoutA", [4096, 512], BF16)
    aginB = nc.dram_tensor("aginB", [768, 512], BF16)
    agoutB = nc.dram_tensor("agoutB", [3072, 512], BF16)
    ag2in = nc.dram_tensor("ag2in", [128, 2048], BF16)
    ag2out = nc.dram_tensor("ag2out", [512, 2048], BF16)
    groups = [[0, 1, 2, 3], [4, 5, 6, 7]]

    with ExitStack() as st:
        def sb(name, shape, dt):
            return st.enter_context(nc.sbuf_tensor("sb_" + name, list(shape), dt))

        S = Sched(nc, st)
        S.stack_cc = []
        hT = sb("hT", [128, 8, NTOK], F32)
        zT = sb("zT", [128, 8, NTOK], BF16)
        NST = 8
        wst = [sb("wst%d" % i, [128, 2048], BF16) for i in range(NST)]
        ftp = [sb("ft%d" % i, [128, 512], F32) for i in range(5)]
        sqp = tbp = sgp = ftp
        rs_t = sb("rs", [128, 512], F32)
        rstd_t = sb("rstd", [128, 512], F32)
        vecs = sb("vecs", [128, 512], F32)
        condT = sb("condT", [128, 16], F32)
        csT = sb("csT", [128, 8, 2], BF16)
        MV2 = sb("MV", [128, 2, 2, 72], F32)
        AV2 = sb("AV", [128, 2, 2, 3, 8], F32)
        GV2 = sb("GV", [128, 2, 2, 3, 8], F32)
        acc2 = [sb("sacc%d" % i, [128, 512], F32) for i in range(2)]
        onesf = sb("onesf", [128, 128], F32)
        onesb = sb("onesb", [128, 128], BF16)
        fcc = sb("fcc", [128, 4, 128], BF16)
        ct256 = sb("ct256", [128, 2, 2, 256], BF16)
        sguw = sb("sguw", [128, 1, 4, 128], BF16)
        sgun = sb("sgun", [128, 1, 256], F32)
        sgub = sb("sgub", [128, 4, 128], F32)
        small = sb("small", [128, 8], F32)
        info_t = sb("info", [1, 4], I32)
        RBYTES = 70 * 1024
        Rt = sb("R", [128, RBYTES // 2], BF16)
        R = Region(S, Rt, RBYTES)
        banks = [st.enter_context(nc.psum_tensor("ps%d" % i, [128, 512], F32)) for i in range(8)]

        V_GN, V_BMOD, V_BGATE, V_GFIN = 0, 96, 96 + 288, 96 + 288 + 96

        def gn(l, i, c):
            o = V_GN + (l * 3 + i) * 8 + c
            return vecs[:, o:o + 1]

        def bgate(l, br, dc):
            o = V_BGATE + (l * 3 + br) * 8 + dc
            return vecs[:, o:o + 1]

        rot = {"bank": 0, "wst": 0, "ft": 0, "acc": 0}
        cur = {"par": 0}

        def bank():
            i = rot["bank"]
            rot["bank"] = (i + 1) % 7
            return banks[i], ("ps", i)

        def stage():
            i = rot["wst"]
            rot["wst"] = (i + 1) % NST
            return wst[i], (("wst", i, 0), ("wst", i, 1))

        def rotp(name, pool):
            i = rot["ft"]
            rot["ft"] = (i + 1) % len(pool)
            return pool[i], ("ft", i)

        def tok(blk):
            return slice(blk * 512, (blk + 1) * 512)

        def load_w(src_ap, view, key):
            S.dma("pool", lambda h: h.dma_start(out=view, in_=src_ap), reads=(), writes=(key,))

        def mm(out, lhsT, rhs, start, stop, reads, wkey):
            S.op("pe", lambda h: h.matmul(out, lhsT, rhs, start=start, stop=stop), reads=reads, writes=(wkey,))

        def gelu_from_psum(ps, pkey, out, okey, width, npart=128):
            S.op("act", lambda h: h.activation(out=out, in_=ps, func=AF.Gelu_apprx_tanh), reads=(pkey,), writes=(okey,))

        S.dma("sp", lambda h: h.dma_start(out=info_t[:], in_=dr["info"][:, :]), writes=(("info", 0),))
        S.dma("sp", lambda h: h.dma_start(out=vecs[:], in_=dr["vecs"][:, :]), writes=(("vecs", 0),))
        S.dma("sp", lambda h: h.dma_start(out=condT[:], in_=dr["condT"][:, :]), writes=(("condT", 0),))
        xv = dr["xT"].rearrange("(c p) t -> p c t", p=128)
        for c in range(8):
            S.dma("sp", lambda h, c=c: h.dma_start(out=hT[:, c, :], in_=xv[:, c, :]),
                  writes=tuple(("h", c, b) for b in range(NBLK)))
        S.dma("pool", lambda h: h.dma_start(out=fcc[:].rearrange("p a b -> p (a b)"), in_=dr["fcc"][:, :]),
              writes=(("fcc", 0),))
        S.dma("pool", lambda h: h.dma_start(out=ct256[:].rearrange("p a b c -> p (a b c)"), in_=dr["ct256"][:, :]),
              writes=(("ct256", 0),))
        S.op("dve", lambda h: h.memset(onesf[:], 1.0 / D), writes=(("onesf", 0),))
        S.op("dve", lambda h: h.memset(onesb[:], 1.0), writes=(("onesb", 0),))
        S.op("act", lambda h: h.activation(out=csT[:].rearrange("p c t -> p (c t)"), in_=condT[:], func=AF.Silu),
             reads=(("condT", 0),), writes=(("csT", 0),))

        dyn = {}

        def load_regs(h):
            r0 = dyn["stack"].enter_context(h.register("r_hp"))
            r1 = dyn["stack"].enter_context(h.register("r_j"))
            h.reg_load(r0, info_t[0:1, 0:1])
            h.reg_load(r1, info_t[0:1, 1:2])
            dyn["off_hp"] = h.snap(r0)
            dyn["off_j"] = h.snap(r1)

        S.raw("sp", load_regs, reads=(("info", 0),))

        def mod_gen(l, early=False):
            par = l % 2
            MV, AV, GV = MV2[:, par], AV2[:, par], GV2[:, par]
            mps, mk = banks[7], ("ps", 7)
            mv = mps[:, 0:144].rearrange("p (c t) -> p c t", t=2)

            def derive(i_list, c0, c1):
                bm = vecs[:, V_BMOD + l * 72 + c0:V_BMOD + l * 72 + c1]
                for cond in range(2):
                    S.op("dve", lambda h, cond=cond: h.tensor_tensor(out=MV[:, cond, c0:c1], in0=mv[:, c0:c1, cond], in1=bm,
                                                                     op=ALU.add),
                         reads=(mk, ("vecs", 0)), writes=tuple(("MV", par, cond, i) for i in i_list))
                    for i in i_list:
                        gnv = vecs[:, V_GN + (l * 3 + i) * 8:V_GN + (l * 3 + i) * 8 + 8]
                        S.op("dve", lambda h, cond=cond, i=i, gnv=gnv: h.scalar_tensor_tensor(
                            out=AV[:, cond, i, :], in0=MV[:, cond, (3 * i + 1) * 8:(3 * i + 1) * 8 + 8], scalar=1.0,
                            in1=gnv, op0=ALU.add, op1=ALU.mult),
                            reads=(("MV", par, cond, i), ("vecs", 0)), writes=(("AV", par, cond, i),))
                        S.op("dve", lambda h, cond=cond, i=i: h.tensor_scalar(
                            out=GV[:, cond, i, :], in0=MV[:, cond, (3 * i + 2) * 8:(3 * i + 2) * 8 + 8],
                            scalar1=(1.0 if i == 1 else 0.5), scalar2=None, op0=ALU.mult),
                            reads=(("MV", par, cond, i),), writes=(("GV", par, cond, i),))

            for g in range(36):
                stt, sk = stage()
                v = stt[:, 0:2048].rearrange("p (c n) -> p c n", c=8)
                load_w(dr["w_mod"][l, :, g * 256:(g + 1) * 256].rearrange("(c p) n -> p c n", p=128), v, sk)
                for sc in range(2):
                    col = g * 2 + sc
                    for kc in range(8):
                        mm(mv[:, col, :], v[:, kc, sc * 128:(sc + 1) * 128], csT[:, kc, :], kc == 0, kc == 7,
                           (sk, ("csT", 0)), mk)
                if early and g == 11:
                    derive([0], 0, 24)
                    yield "i0"
                else:
                    yield None
            if early:
                derive([1, 2], 24, 72)
            else:
                derive([0, 1, 2], 0, 72)

        pend = {}

        def normA(blk):
            ai = rot["acc"]
            rot["acc"] = (ai + 1) % 2
            acc_, ak = acc2[ai], ("sacc", ai)
            prev = None
            for c in range(8):
                sq, sk = rotp("sq", sqp)
                S.op("act", lambda h, c=c, sq=sq: h.activation(out=sq[:], in_=hT[:, c, tok(blk)], func=AF.Square),
                     reads=(("h", c, blk),), writes=(sk,))
                if c == 0:
                    prev = (sq, sk)
                elif c == 1:
                    p_sq, p_k = prev
                    S.op("dve", lambda h, sq=sq, p_sq=p_sq: h.tensor_tensor(out=acc_[:], in0=p_sq[:], in1=sq[:], op=ALU.add),
                         reads=(sk, p_k), writes=(ak,))
                else:
                    S.op("dve", lambda h, sq=sq: h.tensor_tensor(out=acc_[:], in0=acc_[:], in1=sq[:], op=ALU.add),
                         reads=(sk, ak), writes=(ak,))
            pend[blk] = (acc_, ak)

        def normB(blk):
            acc_, ak = pend.pop(blk)
            bs, bk = bank()
            mm(bs[:], onesf[:], acc_[:], True, True, (ak, ("onesf", 0)), bk)
            S.op("act", lambda h: h.activation(out=rs_t[:], in_=bs[:], func=AF.Sqrt, bias=small[:, 0:1], scale=1.0),
                 reads=(bk, ("small", 0)), writes=(("rs", 0),))
            S.op("dve", lambda h: h.reciprocal(out=rstd_t[:], in_=rs_t[:]), reads=(("rs", 0),), writes=(("rstd", 0),))

        def normC(l, i, blk):
            cond = 0 if blk < 2 else 1
            par = l % 2
            MV, AV = MV2[:, par], AV2[:, par]
            for c in range(8):
                tb, tk = rotp("tb", tbp)
                S.op("dve", lambda h, c=c, tb=tb: h.tensor_tensor(out=tb[:], in0=hT[:, c, tok(blk)], in1=rstd_t[:],
                                                                  op=ALU.mult),
                     reads=(("h", c, blk), ("rstd", 0)), writes=(tk,))
                S.op("act", lambda h, c=c, tb=tb: h.activation(
                    out=zT[:, c, tok(blk)], in_=tb[:], func=AF.Identity, scale=AV[:, cond, i, c:c + 1],
                    bias=MV[:, cond, 3 * i * 8 + c:3 * i * 8 + c + 1]),
                    reads=(tk, ("AV", par, cond, i), ("MV", par, cond, i)), writes=(("z", c, blk),))

        def finalC(blk):
            yv = dr["yT"].rearrange("(c p) t -> p c t", p=128)
            for c in range(8):
                tb, tk = rotp("tb", tbp)
                S.op("dve", lambda h, c=c, tb=tb: h.tensor_tensor(out=tb[:], in0=hT[:, c, tok(blk)], in1=rstd_t[:],
                                                                  op=ALU.mult),
                     reads=(("h", c, blk), ("rstd", 0)), writes=(tk,))
                S.op("act", lambda h, c=c, tb=tb: h.activation(
                    out=hT[:, c, tok(blk)], in_=tb[:], func=AF.Identity,
                    scale=vecs[:, V_GFIN + c:V_GFIN + c + 1]),
                    reads=(tk, ("vecs", 0)), writes=(("h", c, blk),))
                S.dma("sp", lambda h, c=c: h.dma_start(out=yv[:, c, tok(blk)], in_=hT[:, c, tok(blk)]),
                      reads=(("h", c, blk),), writes=(("yT", c, blk),), is_out=True)

        def norm(l, i, blk):
            normA(blk)
            normB(blk)
            normC(l, i, blk)

        def final_norm(blk):
            normA(blk)
            normB(blk)
            finalC(blk)

        class make_post:
            def __init__(self, cfn, order=(0, 1, 2)):
                self.cfn = cfn
                self.st = {}
                self.order = order

            def dc(self, blk, c):
                if c == 0:
                    ai = rot["acc"]
                    rot["acc"] = (ai + 1) % 2
                    self.st[blk] = dict(acc=acc2[ai], ak=("sacc", ai), prev=None)
                d = self.st[blk]
                acc_, ak = d["acc"], d["ak"]
                sq, sk = rotp("sq", sqp)
                S.op("act", lambda h: h.activation(out=sq[:], in_=hT[:, c, tok(blk)], func=AF.Square),
                     reads=(("h", c, blk),), writes=(sk,))
                if c == 0:
                    d["prev"] = (sq, sk)
                elif c == 1:
                    p_sq, p_k = d["prev"]
                    S.op("dve", lambda h: h.tensor_tensor(out=acc_[:], in0=p_sq[:], in1=sq[:], op=ALU.add),
                         reads=(sk, p_k), writes=(ak,))
                else:
                    S.op("dve", lambda h: h.tensor_tensor(out=acc_[:], in0=acc_[:], in1=sq[:], op=ALU.add),
                         reads=(sk, ak), writes=(ak,))
                if c == 7:
                    pend[blk] = (acc_, ak)

            def __call__(self, j):
                if j >= 1:
                    normB(self.order[j - 1])
                    self.cfn(self.order[j - 1])

        def ffn(l, i, bgen=None, post_blk=None, skip_norm=False):
            ni = 0 if i == 0 else 2
            par = l % 2
            GV = GV2[:, par]

            def tick():
                if bgen is not None:
                    next(bgen, None)
            if not skip_norm:
                for blk in range(NBLK):
                    norm(l, ni, blk)
            hname, hid = R.alloc("hid", 8 * NTOK * 2, BF16, [8, NTOK])
            for (c0, c1) in ((0, 8), (8, 16), (16, 22)):
                for pair in range(c0, c1, 2):
                    sg_, gk = stage()
                    su_, uk = stage()
                    vg = sg_[:, 0:2048].rearrange("p (c n) -> p c n", c=8)
                    vu = su_[:, 0:2048].rearrange("p (c n) -> p c n", c=8)
                    load_w(dr["wg"][l, i, :, pair * 128:pair * 128 + 256].rearrange("(c p) n -> p c n", p=128), vg, gk)
                    load_w(dr["wu"][l, i, :, pair * 128:pair * 128 + 256].rearrange("(c p) n -> p c n", p=128), vu, uk)
                    tick()
                    combos = [(cc, blk) for cc in range(2) for blk in range(NBLK)]
                    if pair == 0:
                        combos = [(cc, blk) for blk in range(NBLK) for cc in range(2)]
                    for cc, blk in combos:
                        cl = pair + cc - c0
                        if True:
                            bg, bgk = bank()
                            for kc in range(8):
                                mm(bg[:], vg[:, kc, cc * 128:(cc + 1) * 128], zT[:, kc, tok(blk)], kc == 0, kc == 7,
                                   (gk, ("z", kc, blk)), bgk)
                            bu, buk = bank()
                            for kc in range(8):
                                mm(bu[:], vu[:, kc, cc * 128:(cc + 1) * 128], zT[:, kc, tok(blk)], kc == 0, kc == 7,
                                   (uk, ("z", kc, blk)), buk)
                            sgt, sgk = rotp("sg", sgp)
                            S.op("act", lambda h, sgt=sgt, bg=bg: h.activation(out=sgt[:], in_=bg[:], func=AF.Silu),
                                 reads=(bgk,), writes=(sgk,))
                            S.op("dve", lambda h, sgt=sgt, bu=bu, cl=cl, blk=blk: h.tensor_tensor(
                                out=hid[:, cl, tok(blk)], in0=sgt[:], in1=bu[:], op=ALU.mult),
                                reads=(sgk, buk), writes=((hname, cl, blk),))
                nch = c1 - c0
                if c1 == NCH_FF:
                    if bgen is not None:
                        for _ in bgen:
                            pass
                    vds = []
                    for dcp in range(4):
                        sd_, dk = stage()
                        vd = sd_[:, 0:nch * 256].rearrange("p (c n) -> p c n", c=nch)
                        load_w(dr["wd"][l, i, c0 * 128:c1 * 128, dcp * 256:(dcp + 1) * 256].rearrange("(c p) n -> p c n", p=128),
                               vd, dk)
                        vds.append((vd, dk))
                    t_order = post_blk.order if post_blk is not None else (0, 1, 2)
                    for j_, blk in enumerate(t_order):
                        cond = 0 if blk < 2 else 1
                        for dc in range(8):
                            vd, dk = vds[dc // 2]
                            cc = dc % 2
                            bd, bdk = bank()
                            for cl in range(nch):
                                mm(bd[:], vd[:, cl, cc * 128:(cc + 1) * 128], hid[:, cl, tok(blk)], cl == 0, cl == nch - 1,
                                   (dk, (hname, cl, blk)), bdk)
                            S.op("dve", lambda h, bd=bd, dc=dc, blk=blk, cond=cond: h.scalar_tensor_tensor(
                                out=hT[:, dc, tok(blk)], in0=bd[:], scalar=GV[:, cond, ni, dc:dc + 1],
                                in1=hT[:, dc, tok(blk)], op0=ALU.mult, op1=ALU.add),
                                reads=(bdk, ("GV", par, cond, ni), ("h", dc, blk)), writes=(("h", dc, blk),))
                            if post_blk is not None:
                                post_blk.dc(blk, dc)
                        if post_blk is not None:
                            post_blk(j_)
                    if post_blk is not None:
                        post_blk(NBLK)
                    continue
                for dc in range(8):
                    sd_, dk = stage()
                    vd = sd_[:, 0:nch * 128].rearrange("p (c n) -> p c n", c=nch)
                    load_w(dr["wd"][l, i, c0 * 128:c1 * 128, dc * 128:(dc + 1) * 128].rearrange("(c p) n -> p c n", p=128),
                           vd, dk)
                    tick()
                    for blk in range(NBLK):
                        cond = 0 if blk < 2 else 1
                        bd, bdk = bank()
                        for cl in range(nch):
                            mm(bd[:], vd[:, cl, :], hid[:, cl, tok(blk)], cl == 0, cl == nch - 1,
                               (dk, (hname, cl, blk)), bdk)
                        S.op("dve", lambda h, bd=bd, dc=dc, blk=blk, cond=cond: h.scalar_tensor_tensor(
                            out=hT[:, dc, tok(blk)], in0=bd[:], scalar=GV[:, cond, ni, dc:dc + 1],
                            in1=hT[:, dc, tok(blk)], op0=ALU.mult, op1=ALU.add),
                            reads=(bdk, ("GV", par, cond, ni), ("h", dc, blk)), writes=(("h", dc, blk),))
            R.free(hname)

        def chk(tag):
            if stop == tag:
                raise StopBuild()

        def mixers(l, post_blk=None):
            par = l % 2
            GV = GV2[:, par]
            oa_n, oaT = R.alloc("oaT", 4 * NTOK * 2, BF16, [4, NTOK])
            ob_n, obT = R.alloc("obT", 2 * NTOK * 2, BF16, [2, NTOK])
            fm_n, fmT = R.alloc("fmT", 2 * NTOK * 2, BF16, [2, NTOK])
            q_n, qT = R.alloc("qT", NTOK * 2, BF16, [1, NTOK])
            k_n, kT = R.alloc("kT", NTOK * 2, BF16, [1, NTOK])
            vb_n, Vb = R.alloc("Vb", 8 * 128 * 2, BF16, [8, 128])
            kv_n, KVst = R.alloc("KVst", 8 * 256 * 4, F32, [8, 256])
            agv_n, agV = R.alloc("agV", 4 * 512 * 2, BF16, [4, 512])
            pT_n, pT = R.alloc("pT", 4 * 1024 * 2, BF16, [4, 1024])
            rd_n, rden = R.alloc("rden", 2 * 256 * 4, F32, [2, 256])
            nkv = dr["nk"][l].rearrange("(c p) f -> p c f", p=128)
            nvv = dr["nv"][l].rearrange("(c p) f -> p c f", p=128)
            aginv = aginA.ap()
            aginvB = aginB.ap()
            win3 = dr["w_in"][l, :, 0:1536].rearrange("(c p) (j n) -> p c j n", p=128, j=3)
            pcount = 0
            f_n, fT = R.alloc("fT", 2 * NTOK * 2, BF16, [2, NTOK])
            aq_n, agQ = R.alloc("agQ", 4 * 512 * 2, BF16, [4, 512])
            ak_n, agK = R.alloc("agK", 4 * 512 * 2, BF16, [4, 512])
            wide = []
            for w4 in range(6):
                s_, k_ = stage()
                v_ = s_[:, 0:2048].rearrange("p (c n) -> p c n", c=8)
                load_w(dr["w_in"][l, :, w4 * 256:(w4 + 1) * 256].rearrange("(c p) n -> p c n", p=128), v_, k_)
                wide.append((v_, k_))
            stf, sfk = stage()
            vf_ = stf[:, 0:2048].rearrange("p (c n) -> p c n", c=8)
            load_w(dr["w_in"][l, :, 2048:2304].rearrange("(c p) n -> p c n", p=128), vf_, sfk)
            for hp in range(4):
                vq_, qk_ = wide[hp // 2]
                vk_, kk_ = wide[2 + hp // 2]
                cs = slice((hp % 2) * 128, (hp % 2 + 1) * 128)
                bq, bqk = bank()
                for kc in range(8):
                    mm(bq[:], vq_[:, kc, cs], zT[:, kc, tok(2)], kc == 0, kc == 7, (qk_, ("z", kc, 2)), bqk)
                S.op("act", lambda h, bq=bq, hp=hp: h.activation(out=agQ[:, hp, :], in_=bq[:], func=AF.Copy, scale=0.125),
                     reads=(bqk,), writes=((aq_n, hp),))
                bk_, bkk = bank()
                for kc in range(8):
                    mm(bk_[:], vk_[:, kc, cs], zT[:, kc, tok(2)], kc == 0, kc == 7, (kk_, ("z", kc, 2)), bkk)
                S.op("dve", lambda h, bk_=bk_, hp=hp: h.tensor_copy(out=agK[:, hp, :], in_=bk_[:]),
                     reads=(bkk,), writes=((ak_n, hp),))
            S.dma("sp", lambda h: h.dma_start(out=aginv[0:512, :].rearrange("(c p) t -> p c t", p=128), in_=agQ[:, :, :]),
                  reads=tuple((aq_n, hp) for hp in range(4)), writes=(("agin", "q"),))
            S.dma("sp", lambda h: h.dma_start(out=aginv[512:1024, :].rearrange("(c p) t -> p c t", p=128), in_=agK[:, :, :]),
                  reads=tuple((ak_n, hp) for hp in range(4)), writes=(("agin", "k"),))
            S.collective(lambda h: h.collective_compute("AllGather", ALU.bypass, replica_groups=groups,
                                                        ins=[aginA.ap().opt()], outs=[agoutA.ap().opt()]),
                         reads=(("agin", "q"), ("agin", "k")), writes=(("agoutA", 0),))
            for t4 in range(4):
                tsl = slice(1024 + t4 * 128, 1024 + (t4 + 1) * 128)
                for half in range(2):
                    vv_, vk2 = wide[4 + half]
                    b_, bkey = bank()
                    for kc in range(8):
                        mm(b_[:, 0:256], zT[:, kc, tsl], vv_[:, kc, :], kc == 0, kc == 7, (vk2, ("z", kc, 2)), bkey)
                    if half == 0:
                        S.op("dve", lambda h, b_=b_, t4=t4: h.tensor_copy(out=agV[:, t4, 0:256], in_=b_[:, 0:256]),
                             reads=(bkey,), writes=((agv_n, t4, 0),))
                    else:
                        S.op("act", lambda h, b_=b_, t4=t4: h.activation(out=agV[:, t4, 256:512], in_=b_[:, 0:256], func=AF.Copy),
                             reads=(bkey,), writes=((agv_n, t4, 1),))
            S.dma("sp", lambda h: h.dma_start(out=aginvB[0:512, :].rearrange("(c p) f -> p c f", p=128), in_=agV[:, :, :]),
                  reads=tuple((agv_n, t, hf) for t in range(4) for hf in range(2)), writes=(("agin", "v"),))
            for cc in range(2):
                bf_, bfk = bank()
                for kc in range(8):
                    mm(bf_[:], vf_[:, kc, cc * 128:(cc + 1) * 128], zT[:, kc, tok(2)], kc == 0, kc == 7,
                       (sfk, ("z", kc, 2)), bfk)
                S.op("act", lambda h, bf_=bf_, cc=cc: h.activation(out=fT[:, cc, tok(2)], in_=bf_[:], func=AF.Copy),
                     reads=(bfk,), writes=((f_n, cc, 2),))
            S.dma("sp", lambda h: h.dma_start(out=aginvB[512:768, :].rearrange("(c p) t -> p c t", p=128), in_=fT[:, :, tok(2)]),
                  reads=((f_n, 0, 2), (f_n, 1, 2)), writes=(("agin", "f"),))
            S.collective(lambda h: h.collective_compute("AllGather", ALU.bypass, replica_groups=groups,
                                                        ins=[aginB.ap().opt()], outs=[agoutB.ap().opt()]),
                         reads=(("agin", "v"), ("agin", "f")), writes=(("agoutB", 0),))
            R.free(aq_n), R.free(ak_n), R.free(agv_n)
            for hp in range(4):
                stt, sk = stage()
                vqk = stt[:, 0:2048].rearrange("p (j c n) -> p c j n", c=8, j=2)
                load_w(win3[:, :, 0, hp * 128:(hp + 1) * 128], vqk[:, :, 0, :], sk[0])
                load_w(win3[:, :, 1, hp * 128:(hp + 1) * 128], vqk[:, :, 1, :], sk[1])
                stv, svk = stage()
                vv = stv[:, 0:1024].rearrange("p (c n) -> p c n", c=8)
                load_w(win3[:, :, 2, hp * 128:(hp + 1) * 128], vv, svk)
                for blk in (0, 1):
                    bq, bqk = bank()
                    for kc in range(8):
                        mm(bq[:], vqk[:, kc, 0, :], zT[:, kc, tok(blk)], kc == 0, kc == 7, (sk, ("z", kc, blk)), bqk)
                    S.op("act", lambda h, bq=bq, blk=blk: h.activation(out=qT[:, 0, tok(blk)], in_=bq[:], func=AF.Copy,
                                                                       scale=0.125),
                         reads=(bqk,), writes=((q_n, blk),))
                    bk_, bkk = bank()
                    for kc in range(8):
                        mm(bk_[:], vqk[:, kc, 1, :], zT[:, kc, tok(blk)], kc == 0, kc == 7, (sk, ("z", kc, blk)), bkk)
                    S.op("dve", lambda h, bk_=bk_, blk=blk: h.tensor_copy(out=kT[:, 0, tok(blk)], in_=bk_[:]),
                         reads=(bkk,), writes=((k_n, blk),))
                if DBG == "A0":
                    continue
                if DBG == "A":
                    continue
                for tch in range(8):
                    tsl = slice(tch * 128, (tch + 1) * 128)
                    blk = tch // 4
                    b_, bkey = bank()
                    if tch < 8:
                        for kc in range(8):
                            mm(b_[:, 0:128], zT[:, kc, tsl], vqk[:, kc, 1, :], kc == 0, kc == 7, (sk, ("z", kc, blk)), bkey)
                    for kc in range(8):
                        mm(b_[:, 128:256], zT[:, kc, tsl], vv[:, kc, :], kc == 0, kc == 7, (svk, ("z", kc, blk)), bkey)
                    if tch < 8:
                        S.op("act", lambda h, b_=b_, tch=tch: h.activation(out=KVst[:, tch, :], in_=b_[:, 0:256],
                                                                           func=AF.Copy),
                             reads=(bkey,), writes=((kv_n, tch),))
                        S.op("dve", lambda h, b_=b_, tch=tch: h.tensor_copy(out=Vb[:, tch, :], in_=b_[:, 128:256]),
                             reads=(bkey,), writes=((vb_n, tch),))
                    else:
                        S.op("dve", lambda h, b_=b_, tch=tch, hp=hp: h.tensor_copy(
                            out=agV[:, tch - 8, hp * 128:(hp + 1) * 128], in_=b_[:, 128:256]),
                            reads=(bkey,), writes=((agv_n, tch - 8, hp),))
                if DBG == "B1":
                    continue
                S.dma("sp", lambda h, hp=hp: h.dma_start(out=nkv[:, :, hp * 128:(hp + 1) * 128], in_=KVst[:, :, 0:128]),
                      reads=tuple((kv_n, t) for t in range(8)), writes=(("nk", l, hp),), is_out=True)
                S.dma("sp", lambda h, hp=hp: h.dma_start(out=nvv[:, :, hp * 128:(hp + 1) * 128], in_=KVst[:, :, 128:256]),
                      reads=tuple((kv_n, t) for t in range(8)), writes=(("nv", l, hp),), is_out=True)
                if DBG == "B":
                    continue
                for b4 in range(4):
                    blk = b4 // 2
                    qs = slice(b4 * 256, (b4 + 1) * 256)
                    for hh in range(2):
                        ps_ = slice(hh * 64, (hh + 1) * 64)
                        bs_, bsk = bank()
                        for kc2 in range(2):
                            ks = slice(b4 * 256 + kc2 * 128, b4 * 256 + (kc2 + 1) * 128)
                            mm(bs_[:, kc2 * 256:(kc2 + 1) * 256], kT[ps_, 0, ks], qT[ps_, 0, qs], True, True,
                               ((k_n, blk), (q_n, blk)), bsk)
                        pi = pcount % 4
                        ri = pcount % 2
                        pcount += 1
                        S.op("act", lambda h, bs_=bs_, pi=pi: h.activation(out=pT[:, pi, 0:512], in_=bs_[:], func=AF.Exp),
                             reads=(bsk,), writes=((pT_n, pi),))
                        bo, bok = bank()
                        for kc2 in range(2):
                            mm(bo[:, 0:256], Vb[:, b4 * 2 + kc2, :], pT[:, pi, kc2 * 256:(kc2 + 1) * 256], kc2 == 0, kc2 == 1,
                               ((vb_n, b4 * 2 + kc2), (pT_n, pi)), bok)
                        for kc2 in range(2):
                            mm(bo[:, 256:512], onesb[:], pT[:, pi, kc2 * 256:(kc2 + 1) * 256], kc2 == 0, kc2 == 1,
                               (("onesb", 0), (pT_n, pi)), bok)
                        S.op("dve", lambda h, bo=bo, ps_=ps_, ri=ri: h.reciprocal(out=rden[ps_, ri, :], in_=bo[ps_, 256:512]),
                             reads=(bok,), writes=((rd_n, ri),))
                        S.op("dve", lambda h, bo=bo, ps_=ps_, hp=hp, qs=qs, ri=ri: h.tensor_tensor(
                            out=oaT[ps_, hp, qs], in0=bo[ps_, 0:256], in1=rden[ps_, ri, :], op=ALU.mult),
                            reads=(bok, (rd_n, ri)), writes=((oa_n, hp, b4, hh),))
            chk("hp")
            R.free(q_n), R.free(k_n), R.free(vb_n), R.free(kv_n)

            u_n, uT = R.alloc("uT", 2 * NTOK * 2, BF16, [2, NTOK])
            vn_n, vn = R.alloc("vn", 12 * 256 * 2, BF16, [12, 256])
            gv_n, gv = R.alloc("gv", 4 * 256 * 4, F32, [4, 256])
            ss_n, ss = R.alloc("ss", 16 * 4, F32, [1, 16])
            ab_n, AB = R.alloc("AB", 2 * 512 * 2, BF16, [2, 512])
            S.dma("sp", lambda h: h.dma_start(out=sgub[:].rearrange("p g q -> p (g q)"),
                                              in_=dr["sgu_bb"][:, l * 512:(l + 1) * 512]), writes=(("sgub", 0),))
            S.dma("sp", lambda h: h.dma_start(out=sgun[:, 0, :], in_=dr["sgu_nb"][:, l * 256:(l + 1) * 256]),
                  writes=(("sgun", 0),))
            S.dma("pool", lambda h: h.dma_start(out=sguw[:, 0].rearrange("p g q -> p (g q)"), in_=dr["sgu_wT"][l, :, :]),
                  writes=(("sguw", 0),))
            stu, suk = stage()
            stvs, svsk = stage()
            stf, sfk = stage()
            vf_ = stf[:, 0:2048].rearrange("p (c n) -> p c n", c=8)
            load_w(dr["w_in"][l, :, 2048:2304].rearrange("(c p) n -> p c n", p=128), vf_, sfk)
            vu_ = stu[:, 0:2048].rearrange("p (c n) -> p c n", c=8)
            vvs = stvs[:, 0:2048].rearrange("p (c n) -> p c n", c=8)
            load_w(dr["w_in"][l, :, 1536:1792].rearrange("(c p) n -> p c n", p=128), vu_, suk)
            load_w(dr["w_in"][l, :, 1792:2048].rearrange("(c p) n -> p c n", p=128), vvs, svsk)
            for blk in (2, 0, 1):
                for cc in range(2):
                    if blk == 2:
                        continue
                    bf_, bfk = bank()
                    for kc in range(8):
                        mm(bf_[:], vf_[:, kc, cc * 128:(cc + 1) * 128], zT[:, kc, tok(blk)], kc == 0, kc == 7,
                           (sfk, ("z", kc, blk)), bfk)
                    S.op("act", lambda h, bf_=bf_, cc=cc, blk=blk: h.activation(out=fT[:, cc, tok(blk)], in_=bf_[:], func=AF.Copy),
                         reads=(bfk,), writes=((f_n, cc, blk),))
                for cc in range(2):
                    bu_, buk = bank()
                    for kc in range(8):
                        mm(bu_[:], vu_[:, kc, cc * 128:(cc + 1) * 128], zT[:, kc, tok(blk)], kc == 0, kc == 7,
                           (suk, ("z", kc, blk)), buk)
                    gelu_from_psum(bu_[:], buk, uT[:, cc, tok(blk)], (u_n, cc, blk), 512)
                for t4 in range(4):
                    t12 = blk * 4 + t4
                    tsl = slice(t12 * 128, (t12 + 1) * 128)
                    bv_, bvk = bank()
                    for kc in range(8):
                        mm(bv_[:, 0:256], zT[:, kc, tsl], vvs[:, kc, :], kc == 0, kc == 7, (svsk, ("z", kc, blk)), bvk)
                    gelu_from_psum(bv_[:, 0:256], bvk, gv[:, t4, :], (gv_n, t4), 256)
                    sq, sqk = rotp("sq", sqp)
                    S.op("dve", lambda h, sq=sq, t12=t12, t4=t4: h.scalar_tensor_tensor(
                        out=sq[:, 0:256], in0=gv[:, t4, :], scalar=1.0, in1=gv[:, t4, :], op0=ALU.mult, op1=ALU.mult,
                        accum_out=ss[:, 0, t12:t12 + 1]),
                        reads=((gv_n, t4),), writes=(sqk, (ss_n, t12)))
                sl4 = slice(blk * 4, blk * 4 + 4)
                k4 = tuple((ss_n, blk * 4 + t4) for t4 in range(4))
                S.op("act", lambda h, sl4=sl4: h.activation(out=ss[:, 0, sl4], in_=ss[:, 0, sl4], func=AF.Sqrt,
                                                            bias=small[:, 0:1], scale=1.0 / 256),
                     reads=k4 + (("small", 0),), writes=k4)
                S.op("dve", lambda h, sl4=sl4: h.reciprocal(out=ss[:, 0, sl4], in_=ss[:, 0, sl4]), reads=k4, writes=k4)
                for t4 in range(4):
                    t12 = blk * 4 + t4
                    S.op("dve", lambda h, t12=t12, t4=t4: h.scalar_tensor_tensor(
                        out=vn[:, t12, :], in0=gv[:, t4, :], scalar=ss[:, 0, t12:t12 + 1], in1=sgun[:, 0, :],
                        op0=ALU.mult, op1=ALU.mult),
                        reads=((gv_n, t4), (ss_n, t12), ("sgun", 0)), writes=((vn_n, t12),))
            for blk in (0, 1):
                for b2 in range(2):
                    for t2 in range(2):
                        tl = slice(blk * 512 + b2 * 256 + t2 * 128, blk * 512 + b2 * 256 + (t2 + 1) * 128)
                        ba, bak = bank()
                        for fh in range(2):
                            mm(ba[:, fh * 128:(fh + 1) * 128], fT[:, fh, tl], fcc[:, 0, :], True, True,
                               ((f_n, fh, blk), ("fcc", 0)), bak)
                            mm(ba[:, 256 + fh * 128:256 + (fh + 1) * 128], fT[:, fh, tl], fcc[:, 1, :], True, True,
                               ((f_n, fh, blk), ("fcc", 0)), bak)
                        S.op("act", lambda h, ba=ba, t2=t2: h.activation(out=AB[:, t2, :], in_=ba[:], func=AF.Copy),
                             reads=(bak,), writes=((ab_n, t2),))
                    bo, bok = bank()
                    for fh in range(2):
                        n_ = 0
                        for t2 in range(2):
                            for cs_ in range(2):
                                mm(bo[:, fh * 256:(fh + 1) * 256], AB[:, t2, cs_ * 256 + fh * 128:cs_ * 256 + (fh + 1) * 128],
                                   ct256[:, cs_, t2, :], n_ == 0, n_ == 3, ((ab_n, t2), ("ct256", 0)), bok)
                                n_ += 1
                    for fh in range(2):
                        S.op("act", lambda h, bo=bo, fh=fh, blk=blk, b2=b2: h.activation(
                            out=fmT[:, fh, blk * 512 + b2 * 256:blk * 512 + (b2 + 1) * 256],
                            in_=bo[:, fh * 256:(fh + 1) * 256], func=AF.Copy),
                            reads=(bok,), writes=((fm_n, fh, blk, b2),))
            for blk in (2, 0, 1):
                for g in range(4):
                    pr, gh = g // 2, g % 2
                    rows = slice(gh * 64, (gh + 1) * 64)
                    bm_, bmk = bank()
                    for t4 in range(4):
                        mm(bm_[:, t4 * 128:(t4 + 1) * 128], vn[:, blk * 4 + t4, pr * 128:(pr + 1) * 128], sguw[:, 0, g, :], True, True,
                           ((vn_n, blk * 4 + t4), ("sguw", 0)), bmk)
                    sgt, sgk = rotp("sg", sgp)
                    for t4 in range(4):
                        S.op("dve", lambda h, bm_=bm_, sgt=sgt, t4=t4, g=g, rows=rows: h.tensor_tensor(
                            out=sgt[rows, t4 * 128:(t4 + 1) * 128], in0=bm_[rows, t4 * 128:(t4 + 1) * 128],
                            in1=sgub[rows, g, :], op=ALU.add),
                            reads=(bmk, ("sgub", 0)), writes=(sgk,))
                    S.op("dve", lambda h, sgt=sgt, rows=rows, pr=pr, blk=blk: h.tensor_tensor(
                        out=obT[rows, pr, tok(blk)], in0=sgt[rows, :], in1=uT[rows, pr, tok(blk)], op=ALU.mult),
                        reads=(sgk, (u_n, pr, blk)), writes=((ob_n, pr, blk, gh),))
            R.free(u_n), R.free(f_n), R.free(vn_n), R.free(gv_n), R.free(ss_n), R.free(ab_n)

            chk("sgu")
            qs_n, qTs = R.alloc("qTs", 2048 * 2, BF16, [1, 2048])
            ks_n, kTs = R.alloc("kTs", 2048 * 2, BF16, [1, 2048])
            vs_n, Vs = R.alloc("Vs", 16 * 128 * 2, BF16, [16, 128])
            kc_n, kcT = R.alloc("kcT", 256 * 2, BF16, [1, 256])
            vc_n, Vc = R.alloc("Vc", 2 * 128 * 2, BF16, [2, 128])
            bi_n, bias = R.alloc("bias", NSLOT * 128 * 4, F32, [NSLOT, 128])
            om_n, oTm = R.alloc("oTm", 2048 * 2, BF16, [1, 2048])
            sb_n, sbuf_s = R.alloc("sbs", 2 * 640 * 4, F32, [2, 640])
            ago3 = agoutA.ap().rearrange("(r x) t -> r x t", r=4)
            agoB3 = agoutB.ap().rearrange("(r x) t -> r x t", r=4)
            S.dma("sp", lambda h: h.dma_start(out=qTs[:, 0, :].rearrange("p (r t) -> p r t", r=4),
                                              in_=ago3[:, 0:512, :][:, bass.ds(dyn["off_hp"], 128), :].rearrange("r p t -> p r t")),
                  reads=(("agoutA", 0),), writes=((qs_n, 0),))
            S.dma("sp", lambda h: h.dma_start(out=kTs[:, 0, :].rearrange("p (r t) -> p r t", r=4),
                                              in_=ago3[:, 512:1024, :][:, bass.ds(dyn["off_hp"], 128), :].rearrange("r p t -> p r t")),
                  reads=(("agoutA", 0),), writes=((ks_n, 0),))
            for r in range(4):
                S.dma("sp", lambda h, r=r: h.dma_start(
                    out=Vs[:, r * 4:(r + 1) * 4, :],
                    in_=agoB3[r, 0:512, :].rearrange("(c p) f -> p c f", p=128)[:, :, bass.ds(dyn["off_hp"], 128)]),
                    reads=(("agoutB", 0),), writes=((vs_n, r),))
            S.dma("pool", lambda h: h.dma_start(out=kcT[:, 0, :], in_=dr["kcT"][l, :, :]), writes=((kc_n, 0),))
            S.dma("pool", lambda h: h.dma_start(out=Vc[:, :, :], in_=dr["vc"][l].rearrange("(c p) f -> p c f", p=128)),
                  writes=((vc_n, 0),))
            def s_part(hh, n):
                nonlocal pcount
                rows = slice(hh * 64, (hh + 1) * 64)
                if n <= 1:
                    ms, slot0 = [0, 1, 2, 3], 5 + 4 * n
                elif n >= 14:
                    ms, slot0 = [12, 13, 14, 15], 13 + 4 * (n - 14)
                else:
                    ms, slot0 = list(range(n - 2, n + 3)), 0
                nw = len(ms)
                qsl = slice(n * 128, (n + 1) * 128)
                bA, bAk = bank()
                for i_, m in enumerate(ms[:4]):
                    mm(bA[:, i_ * 128:(i_ + 1) * 128], kTs[rows, 0, m * 128:(m + 1) * 128], qTs[rows, 0, qsl], True, True,
                       ((ks_n, 0), (qs_n, 0)), bAk)
                bB, bBk = bank()
                nb = 0
                if nw == 5:
                    m = ms[4]
                    mm(bB[:, 0:128], kTs[rows, 0, m * 128:(m + 1) * 128], qTs[rows, 0, qsl], True, True,
                       ((ks_n, 0), (qs_n, 0)), bBk)
                    nb = 1
                for c2 in range(2):
                    mm(bB[:, (nb + c2) * 128:(nb + c2 + 1) * 128], kcT[rows, 0, c2 * 128:(c2 + 1) * 128], qTs[rows, 0, qsl],
                       True, True, ((kc_n, 0), (qs_n, 0)), bBk)
                pi = pcount % 4
                ri = pcount % 2
                pcount += 1
                S.op("dve", lambda h: h.tensor_tensor(
                    out=sbuf_s[:, ri, 0:512], in0=bA[:, 0:512], in1=bias[:, slot0:slot0 + 4, :].rearrange("p s q -> p (s q)"),
                    op=ALU.add), reads=(bAk, (bi_n, 0)), writes=((sb_n, ri, 0),))
                if nw == 5:
                    S.op("dve", lambda h: h.tensor_tensor(
                        out=sbuf_s[:, ri, 512:640], in0=bB[:, 0:128], in1=bias[:, slot0 + 4, :], op=ALU.add),
                        reads=(bBk, (bi_n, 0)), writes=((sb_n, ri, 1),))
                nwc = nw * 128
                S.op("act", lambda h: h.activation(out=pT[:, pi, 0:nwc], in_=sbuf_s[:, ri, 0:nwc], func=AF.Exp),
                     reads=((sb_n, ri, 0), (sb_n, ri, 1)), writes=((pT_n, pi),))
                S.op("act", lambda h: h.activation(
                    out=pT[:, pi, nwc:nwc + 256], in_=bB[:, nb * 128:(nb + 2) * 128], func=AF.Exp),
                    reads=(bBk,), writes=((pT_n, pi, "c"),))
                return dict(hh=hh, n=n, rows=rows, ms=ms, nw=nw, qsl=qsl, pi=pi, ri=ri)

            def pv_part(st_):
                hh, n, rows, ms, nw, qsl, pi, ri = (st_[k_] for k_ in ("hh", "n", "rows", "ms", "nw", "qsl", "pi", "ri"))
                bo, bok = bank()
                ntile = nw + 2
                for i_ in range(ntile):
                    lhs = Vs[:, ms[i_], :] if i_ < nw else Vc[:, i_ - nw, :]
                    mm(bo[:, 0:128], lhs, pT[:, pi, i_ * 128:(i_ + 1) * 128], i_ == 0, i_ == ntile - 1,
                       ((vs_n, 0), (vs_n, 1), (vs_n, 2), (vs_n, 3), (vc_n, 0), (pT_n, pi), (pT_n, pi, "c")), bok)
                for i_ in range(ntile):
                    mm(bo[:, 128:256], onesb[:], pT[:, pi, i_ * 128:(i_ + 1) * 128], i_ == 0, i_ == ntile - 1,
                       (("onesb", 0), (pT_n, pi), (pT_n, pi, "c")), bok)
                S.op("dve", lambda h: h.reciprocal(out=rden[rows, ri, 0:128], in_=bo[rows, 128:256]),
                     reads=(bok,), writes=((rd_n, ri),))
                S.op("dve", lambda h: h.tensor_tensor(
                    out=oTm[rows, 0, qsl], in0=bo[rows, 0:128], in1=rden[rows, ri, 0:128], op=ALU.mult),
                    reads=(bok, (rd_n, ri)), writes=((om_n, n, hh),))

            prev_st = None
            for hh in range(2):
                S.dma("sp", lambda h, hh=hh: h.dma_start(out=bias[:].rearrange("p s q -> p (s q)"), in_=dr["bsmp"][l, hh, :, :]),
                      writes=((bi_n, 0),))
                for n in range(16):
                    st_ = s_part(hh, n)
                    if prev_st is not None:
                        pv_part(prev_st)
                    prev_st = st_
            pv_part(prev_st)
            S.dma("sp", lambda h: h.dma_start(out=ag2in.ap()[:, :], in_=oTm[:, 0, :]),
                  reads=tuple((om_n, n, hh) for n in range(16) for hh in range(2)), writes=(("ag2in", 0),))
            S.collective(lambda h: h.collective_compute("AllGather", ALU.bypass, replica_groups=groups,
                                                        ins=[ag2in.ap().opt()], outs=[ag2out.ap().opt()]),
                         reads=(("ag2in", 0),), writes=(("ag2out", 0),))
            R.free(qs_n), R.free(ks_n), R.free(vs_n), R.free(kc_n), R.free(vc_n), R.free(bi_n), R.free(om_n), R.free(sb_n)
            R.free(pT_n), R.free(rd_n)

            chk("sattn")
            fa_n, fTall = R.alloc("fTall", 2 * 2048 * 2, BF16, [2, 2048])
            for fh in range(2):
                S.dma("sp", lambda h, fh=fh: h.dma_start(
                    out=fTall[:, fh, :].rearrange("p (r t) -> p r t", r=4),
                    in_=agoB3[:, 512 + fh * 128:512 + (fh + 1) * 128, :].rearrange("r p t -> p r t")),
                    reads=(("agoutB", 0),), writes=((fa_n, fh),))
            abs_n, ABs = R.alloc("ABs", 16 * 512 * 2, BF16, [16, 512])
            for t16 in range(16):
                tl = slice(t16 * 128, (t16 + 1) * 128)
                ba, bak = bank()
                for fh in range(2):
                    mm(ba[:, fh * 128:(fh + 1) * 128], fTall[:, fh, tl], fcc[:, 2, :], True, True, ((fa_n, fh), ("fcc", 0)), bak)
                    mm(ba[:, 256 + fh * 128:256 + (fh + 1) * 128], fTall[:, fh, tl], fcc[:, 3, :], True, True,
                       ((fa_n, fh), ("fcc", 0)), bak)
                if t16 % 2 == 0:
                    S.op("act", lambda h, ba=ba, t16=t16: h.activation(out=ABs[:, t16, :], in_=ba[:], func=AF.Copy),
                         reads=(bak,), writes=((abs_n, t16),))
                else:
                    S.op("dve", lambda h, ba=ba, t16=t16: h.tensor_copy(out=ABs[:, t16, :], in_=ba[:]),
                         reads=(bak,), writes=((abs_n, t16),))
            bo0, bo0k = bank()
            bo1, bo1k = bank()
            bos = ((bo0, bo0k), (bo1, bo1k))
            ctmv = dr["ctm"].rearrange("s (c p) t -> s p c t", p=128)
            for piece in range(4):
                stc, sck = stage()
                sts, ssk = stage()
                vc_ = stc[:, 0:2048].rearrange("p (c t) -> p c t", c=4)
                vs_ = sts[:, 0:2048].rearrange("p (c t) -> p c t", c=4)
                S.dma("sp", lambda h, piece=piece, vc_=vc_: h.dma_start(out=vc_, in_=ctmv[0, :, piece * 4:(piece + 1) * 4, :]),
                      writes=(sck,))
                S.dma("sp", lambda h, piece=piece, vs_=vs_: h.dma_start(out=vs_, in_=ctmv[1, :, piece * 4:(piece + 1) * 4, :]),
                      writes=(ssk,))
                for t4 in range(4):
                    t16 = piece * 4 + t4
                    for fh in range(2):
                        bo, bok = bos[fh]
                        mm(bo[:], ABs[:, t16, fh * 128:(fh + 1) * 128], vc_[:, t4, :], t16 == 0, False, ((abs_n, t16), sck), bok)
                        mm(bo[:], ABs[:, t16, 256 + fh * 128:256 + (fh + 1) * 128], vs_[:, t4, :], False, t16 == 15,
                           ((abs_n, t16), ssk), bok)
            for fh in range(2):
                bo, bok = bos[fh]
                S.op("act", lambda h, bo=bo, fh=fh: h.activation(out=fmT[:, fh, tok(2)], in_=bo[:], func=AF.Copy),
                     reads=(bok,), writes=((fm_n, fh, 2, 0),))
            S.dma("sp", lambda h: h.dma_start(
                out=oaT[:, :, tok(2)],
                in_=ag2out.ap().rearrange("(c p) t -> p c t", p=128)[:, :, bass.ds(dyn["off_j"], 512)]),
                reads=(("ag2out", 0),), writes=tuple((oa_n, hp, "s") for hp in range(4)))
            R.free(abs_n), R.free(fa_n)

            chk("sfnet")
            mx_n, mixedT = R.alloc("mixedT", 8 * NTOK * 2, BF16, [8, NTOK])
            ac_n, acc = R.alloc("acc", 512 * 4, F32, [1, 512])
            oa_keys = lambda blk: tuple((oa_n, hp, b4, hh) for hp in range(4) for b4 in (2 * blk, 2 * blk + 1) for hh in range(2)) \
                if blk < 2 else tuple((oa_n, hp, "s") for hp in range(4))
            ob_keys = lambda blk: tuple((ob_n, pr, blk, gh) for pr in range(2) for gh in range(2))
            fm_keys = lambda blk: tuple((fm_n, fh, blk, b2) for fh in range(2) for b2 in range(2)) if blk < 2 \
                else tuple((fm_n, fh, 2, 0) for fh in range(2))
            def merge_loads(dcp):
                gst = []
                for br in range(3):
                    s_, k_ = stage()
                    v_ = s_[:, 0:2048].rearrange("p (c n) -> p c n", c=8)
                    c0 = 2304 + br * 1024 + dcp * 256
                    load_w(dr["w_in"][l, :, c0:c0 + 256].rearrange("(c p) n -> p c n", p=128), v_, k_)
                    gst.append((v_, k_))
                sp_, pk = stage()
                vp = sp_[:, 0:2048].rearrange("p (c n) -> p c n", c=8)
                load_w(dr["p_all"][l, :, dcp * 256:(dcp + 1) * 256].rearrange("(c p) n -> p c n", p=128), vp, pk)
                return gst, vp, pk

            nxt = merge_loads(0)
            for step in range(8):
                dcp = step % 4
                gst, vp, pk = nxt
                if step + 1 < 8:
                    nxt = merge_loads((step + 1) % 4)
                for blk in ((0, 1) if step < 4 else (2,)):
                    for cc in range(2):
                        dc = dcp * 2 + cc
                        for br in range(3):
                            vg_, gk_ = gst[br]
                            bg, bgk = bank()
                            for kc in range(8):
                                mm(bg[:], vg_[:, kc, cc * 128:(cc + 1) * 128], zT[:, kc, tok(blk)], kc == 0, kc == 7,
                                   (gk_, ("z", kc, blk)), bgk)
                            by, byk = bank()
                            if br == 0:
                                for c4 in range(4):
                                    mm(by[:], vp[:, c4, cc * 128:(cc + 1) * 128], oaT[:, c4, tok(blk)], c4 == 0, c4 == 3,
                                       (pk,) + oa_keys(blk), byk)
                            elif br == 1:
                                for c2 in range(2):
                                    mm(by[:], vp[:, 4 + c2, cc * 128:(cc + 1) * 128], obT[:, c2, tok(blk)], c2 == 0, c2 == 1,
                                       (pk,) + ob_keys(blk), byk)
                            else:
                                for c2 in range(2):
                                    mm(by[:], vp[:, 6 + c2, cc * 128:(cc + 1) * 128], fmT[:, c2, tok(blk)], c2 == 0, c2 == 1,
                                       (pk,) + fm_keys(blk), byk)
                            sgt, sgk = rotp("sg", sgp)
                            S.op("act", lambda h, sgt=sgt, bg=bg, br=br, dc=dc: h.activation(
                                out=sgt[:], in_=bg[:], func=AF.Sigmoid, bias=bgate(l, br, dc), scale=1.0),
                                reads=(bgk, ("vecs", 0)), writes=(sgk,))
                            if br == 0:
                                S.op("dve", lambda h, sgt=sgt, by=by: h.tensor_tensor(out=acc[:, 0, :], in0=sgt[:], in1=by[:],
                                                                                      op=ALU.mult),
                                     reads=(sgk, byk), writes=((ac_n, 0),))
                            elif br == 1:
                                S.op("dve", lambda h, sgt=sgt, by=by: h.tensor_tensor(out=sgt[:], in0=sgt[:], in1=by[:],
                                                                                      op=ALU.mult),
                                     reads=(sgk, byk), writes=(sgk,))
                                S.op("dve", lambda h, sgt=sgt: h.tensor_tensor(out=acc[:, 0, :], in0=acc[:, 0, :], in1=sgt[:],
                                                                               op=ALU.add),
                                     reads=(sgk, (ac_n, 0)), writes=((ac_n, 0),))
                            else:
                                S.op("dve", lambda h, sgt=sgt, by=by: h.tensor_tensor(out=sgt[:], in0=sgt[:], in1=by[:],
                                                                                      op=ALU.mult),
                                     reads=(sgk, byk), writes=(sgk,))
                                S.op("dve", lambda h, sgt=sgt, dc=dc, blk=blk: h.tensor_tensor(
                                    out=mixedT[:, dc, tok(blk)], in0=acc[:, 0, :], in1=sgt[:], op=ALU.add),
                                    reads=(sgk, (ac_n, 0)), writes=((mx_n, dc, blk),))
            vos = []
            for dcp in range(4):
                so_, ok_ = stage()
                vo = so_[:, 0:2048].rearrange("p (c n) -> p c n", c=8)
                load_w(dr["w_out"][l, :, dcp * 256:(dcp + 1) * 256].rearrange("(c p) n -> p c n", p=128), vo, ok_)
                vos.append((vo, ok_))
            for blk in range(NBLK):
                cond = 0 if blk < 2 else 1
                for dc in range(8):
                    vo, ok_ = vos[dc // 2]
                    cc = dc % 2
                    bd, bdk = bank()
                    for kc in range(8):
                        mm(bd[:], vo[:, kc, cc * 128:(cc + 1) * 128], mixedT[:, kc, tok(blk)], kc == 0, kc == 7,
                           (ok_, (mx_n, kc, blk)), bdk)
                    S.op("dve", lambda h, bd=bd, dc=dc, blk=blk, cond=cond: h.scalar_tensor_tensor(
                        out=hT[:, dc, tok(blk)], in0=bd[:], scalar=GV[:, cond, 1, dc:dc + 1],
                        in1=hT[:, dc, tok(blk)], op0=ALU.mult, op1=ALU.add),
                        reads=(bdk, ("GV", par, cond, 1), ("h", dc, blk)), writes=(("h", dc, blk),))
                    if post_blk is not None:
                        post_blk.dc(blk, dc)
                if post_blk is not None:
                    post_blk(blk)
            if post_blk is not None:
                post_blk(NBLK)
            R.free(mx_n), R.free(ac_n), R.free(oa_n), R.free(ob_n), R.free(fm_n)

        S.op("dve", lambda h: h.memset(small[:], RMS_EPS), writes=(("small", 0),))
        try:
            g0 = mod_gen(0, early=True)
            for mark in g0:
                if mark == "i0":
                    break
            for l in range(nlayers):
                chk("mod")
                ffn(l, 0, bgen=(g0 if l == 0 else None), post_blk=make_post(lambda b_, l=l: normC(l, 1, b_), order=(2, 0, 1)), skip_norm=(l > 0))
                chk("ffn1")
                mixers(l, post_blk=make_post(lambda b_, l=l: normC(l, 2, b_)))
                chk("mix")
                if l + 1 < nlayers:
                    ffn(l, 1, bgen=mod_gen(l + 1), post_blk=make_post(lambda b_, l=l: normC(l + 1, 0, b_)), skip_norm=True)
                else:
                    ffn(l, 1, post_blk=make_post(finalC), skip_norm=True)
        except StopBuild:
            pass
        if stop is not None:
            for blk in range(NBLK):
                final_norm(blk)

        with nc.Block() as block:
            @block.tensor
            def _(h):
                for f in S.q["pe"]:
                    f(h)

            @block.scalar
            def _(h):
                for f in S.q["act"]:
                    f(h)

            @block.vector
            def _(h):
                for f in S.q["dve"]:
                    f(h)

            @block.gpsimd
            def _(h):
                for f in S.q["pool"]:
                    f(h)

            @block.sync
            def _(h):
                with ExitStack() as rs:
                    dyn["stack"] = rs
                    for f in S.q["sp"]:
                        f(h)
                    done = {}
                    for s, v, e in S.out_tokens:
                        if id(s) not in done or done[id(s)][1] < v:
                            done[id(s)] = (s, v)
                    for s, v in done.values():
                        h.wait_ge(s, v)
    return nc


def _bias_tables(rpb_l, heads):
    cols = np.arange(GRID_W)
    col_start = np.clip(cols - 8, 0, GRID_W - 16)
    colok = (cols[None, :] >= col_start[:, None]) & (cols[None, :] < col_start[:, None] + 16)
    dcol = np.clip(cols[None, :] - cols[:, None], -15, 15) + 15
    slots = [(5, 3 + d) for d in range(5)]
    slots += [(0, m) for m in range(4)] + [(1, m) for m in range(4)]
    slots += [(14, m) for m in range(12, 16)] + [(15, m) for m in range(12, 16)]
    out = np.full((len(heads), 128, NSLOT, 128), -1e30, np.float32)
    for hi, hd in enumerate(heads):
        for si, (n, m) in enumerate(slots):
            for qr2 in range(2):
                r = 2 * n + qr2
                r0 = min(max(r - 4, 0), ROWS - 8)
                for kr2 in range(2):
                    rk = 2 * m + kr2
                    if not (r0 <= rk < r0 + 8):
                        continue
                    drow = rk - r + 7
                    blk = np.where(colok, rpb_l[hd, drow][dcol], np.float32(-1e30))
                    out[hi, kr2 * 64:(kr2 + 1) * 64, si, qr2 * 64:(qr2 + 1) * 64] = blk.T
    return out.reshape(len(heads), 128, NSLOT * 128)


def _consts():
    bf = ml_dtypes.bfloat16
    c = np.arange(64)
    ang = 2 * np.pi * np.outer(c, c) / 64.0
    fcc = np.zeros((128, 4, 128), np.float32)
    for si, T in enumerate((256, 2048)):
        sc = 1.0 / math.sqrt(T * 64.0)
        for g in range(2):
            fcc[g * 64:(g + 1) * 64, si * 2 + 0, g * 64:(g + 1) * 64] = np.cos(ang) * sc
            fcc[g * 64:(g + 1) * 64, si * 2 + 1, g * 64:(g + 1) * 64] = np.sin(ang) * sc
    t = np.arange(256)
    a = 2 * np.pi * (np.outer(t, t) % 256) / 256.0
    ct = np.stack([np.cos(a), -np.sin(a)], 0).astype(np.float32)
    ct256 = ct.reshape(2, 2, 128, 256).transpose(2, 0, 1, 3).reshape(128, 1024)
    t2 = np.arange(2048, dtype=np.int64)
    a2 = 2 * np.pi * (np.outer(t2, t2) % 2048) / 2048.0
    ctm = np.stack([np.cos(a2), -np.sin(a2)], 0).astype(bf)
    return fcc.reshape(128, 512), np.ascontiguousarray(ct256), ctm


_NC_CACHE = {}


def kernel(x_prompt, x_sample, cache_k, cache_v, c, c_ctx, w_mod, b_mod, g_norm,
           ffn_w_gate, ffn_w_up, ffn_w_down, w_in, b_gate, rpb, sgu_norm, sgu_w, sgu_b,
           p_attn, p_sgu, p_fnet, w_out, g_final):
    f32 = np.float32
    A = lambda a: np.ascontiguousarray(np.asarray(a, dtype=f32))
    x_prompt, x_sample, cache_k, cache_v = A(x_prompt), A(x_sample), A(cache_k), A(cache_v)
    c, c_ctx, b_mod, g_norm, b_gate, rpb = A(c), A(c_ctx), A(b_mod), A(g_norm), A(b_gate), A(rpb)
    sgu_norm, sgu_w, sgu_b, g_final = A(sgu_norm), A(sgu_w), A(sgu_b), A(g_final)
    w_mod, wg, wu, wd, w_in, w_out = A(w_mod), A(ffn_w_gate), A(ffn_w_up), A(ffn_w_down), A(w_in), A(w_out)
    p_all = np.ascontiguousarray(np.concatenate([A(p_attn), A(p_sgu), A(p_fnet)], axis=1))

    def pv(v):
        lead = v.shape[:-1]
        return np.moveaxis(v.reshape(lead + (8, 128)), -1, 0)

    vecs = np.zeros((128, 512), f32)
    vecs[:, 0:96] = pv(g_norm).reshape(128, 96)
    vecs[:, 96:96 + 288] = np.moveaxis(b_mod.reshape(NL, 72, 128), -1, 0).reshape(128, 288)
    vecs[:, 384:384 + 96] = pv(b_gate).reshape(128, 96)
    vecs[:, 480:488] = pv(g_final).reshape(128, 8)
    sgu_wT = np.ascontiguousarray(sgu_w.transpose(0, 3, 1, 2).reshape(NL, 128, 512))
    sgu_nb = np.ascontiguousarray(np.broadcast_to(sgu_norm.reshape(1, NL * 256), (128, NL * 256)))
    sgu_bb = np.ascontiguousarray(np.broadcast_to(sgu_b.reshape(1, NL * 512), (128, NL * 512)))
    fcc, ct256, ctm_full = _consts()

    in_maps = []
    for core in range(8):
        sq, j = core // 4, core % 4
        xs = np.concatenate([x_prompt[core * 4:(core + 1) * 4].reshape(1024, D),
                             x_sample[sq, j * 512:(j + 1) * 512]], axis=0)
        condT = np.stack([pv(c_ctx), pv(c[sq])], axis=-1).reshape(128, 16)
        hs = slice(2 * j, 2 * j + 2)
        kcT = np.ascontiguousarray(cache_k[sq, :, :, hs, :].reshape(NL, 256, 128).transpose(0, 2, 1))
        vcm = np.ascontiguousarray(cache_v[sq, :, :, hs, :].reshape(NL, 256, 128))
        bsmp = np.stack([_bias_tables(rpb[l], [2 * j, 2 * j + 1]) for l in range(NL)], 0)
        in_maps.append({
            "xT": np.ascontiguousarray(xs.T), "vecs": vecs, "condT": np.ascontiguousarray(condT),
            "w_mod": w_mod, "wg": wg, "wu": wu, "wd": wd, "w_in": w_in, "p_all": p_all, "w_out": w_out,
            "sgu_wT": sgu_wT, "sgu_nb": sgu_nb, "sgu_bb": sgu_bb, "fcc": fcc, "ct256": ct256,
            "kcT": kcT, "vc": vcm, "bsmp": np.ascontiguousarray(bsmp),
            "ctm": np.ascontiguousarray(ctm_full[:, :, j * 512:(j + 1) * 512]),
            "info": np.array([[j * 128, j * 512, 0, 0]], dtype=np.int32),
        })
    if "nc" not in _NC_CACHE:
        _NC_CACHE["nc"] = build_program()
    nld = _NC_CACHE.get("nld", NL)
    if nld != NL:
        for m in in_maps:
            for k_ in ("w_mod", "wg", "wu", "wd", "w_in", "p_all", "w_out", "sgu_wT", "kcT", "vc", "bsmp"):
                m[k_] = np.ascontiguousarray(m[k_][:nld])
    res = run_bass_kernel_spmd(_NC_CACHE["nc"], in_maps, core_ids=list(range(8)))
    y_prompt = np.zeros((32, 256, D), f32)
    y_sample = np.zeros((2, 2048, D), f32)
    nk = np.zeros((32, NL, 256, 8, 64), f32)
    nv = np.zeros((32, NL, 256, 8, 64), f32)
    for core in range(8):
        r = res.results[core]
        sq, j = core // 4, core % 4
        y = np.asarray(r["yT"]).T
        y_prompt[core * 4:(core + 1) * 4] = y[0:1024].reshape(4, 256, D)
        y_sample[sq, j * 512:(j + 1) * 512] = y[1024:1536]
        nk[core * 4:(core + 1) * 4] = np.asarray(r["nk"]).reshape(NL, 4, 256, 8, 64).transpose(1, 0, 2, 3, 4)
        nv[core * 4:(core + 1) * 4] = np.asarray(r["nv"]).reshape(NL, 4, 256, 8, 64).transpose(1, 0, 2, 3, 4)
    return (y_prompt, y_sample, nk, nv)
```
